# Optimizing a Trainium2 kernel written in Bass

```python
import jax, jax.numpy as jnp
from jax import lax
import numpy as np

D_MODEL = 2048
BATCH = 2
SEQ = 8192
DEPTH = 2

CHUNK = 64
N_MIXERS = 2
N_A_LAYERS = (DEPTH + 1) // 2
N_B_LAYERS = DEPTH // 2
EPS = 1e-6

D_FF = 5632

SGU_BLOCK = 128
SGU_WIDTH = 2 * D_MODEL
SGU_GROUPS = 8
SGU_GROUP_DIM = SGU_WIDTH // SGU_GROUPS

MLA_HEADS = 16
Q_LORA = 512
KV_LORA = 512
QK_NOPE = 128
QK_ROPE = 64
V_DIM = 128
QK_DIM = QK_NOPE + QK_ROPE
ROPE_THETA = 10000.0
Q_BLOCK = 128

kernel_name = "hybrid_sgu_mla_macaron_encoder"


def rmsnorm(x, g):
    xf = x.astype(jnp.float32)
    y = xf * lax.rsqrt(jnp.mean(xf * xf, axis=-1, keepdims=True) + EPS)
    return (y * g.astype(jnp.float32)).astype(x.dtype)


def layernorm(x, g, b):
    xf = x.astype(jnp.float32)
    mu = jnp.mean(xf, axis=-1, keepdims=True)
    var = jnp.mean(jnp.square(xf - mu), axis=-1, keepdims=True)
    y = (xf - mu) * lax.rsqrt(var + EPS)
    return (y * g.astype(jnp.float32) + b.astype(jnp.float32)).astype(x.dtype)


def swiglu(h, w_in, w_out):
    gate, up = jnp.split(h @ w_in, 2, axis=-1)
    return (jax.nn.silu(gate) * up) @ w_out


def rope(x, positions):
    half = x.shape[-1] // 2
    inv_freq = 1.0 / (ROPE_THETA ** (jnp.arange(half, dtype=jnp.float32) / half))
    ang = positions.astype(jnp.float32)[..., None] * inv_freq
    cos = jnp.cos(ang)[:, :, None, :]
    sin = jnp.sin(ang)[:, :, None, :]
    xf = x.astype(jnp.float32)
    x1, x2 = xf[..., :half], xf[..., half:]
    out = jnp.concatenate([x1 * cos - x2 * sin, x1 * sin + x2 * cos], axis=-1)
    return out.astype(x.dtype)


def sgu_mixer(h, w_in, v_gain, v_bias, w_spatial, b_spatial, w_out):
    B, S, _ = h.shape
    uv = jax.nn.gelu(h @ w_in)
    u, v = jnp.split(uv, 2, axis=-1)
    v = layernorm(v, v_gain, v_bias)
    nb = S // SGU_BLOCK
    v = v.reshape(B, nb, SGU_BLOCK, SGU_GROUPS, SGU_GROUP_DIM)
    pos_chunk = jnp.arange(SGU_BLOCK) // CHUNK
    mask = pos_chunk[:, None] >= pos_chunk[None, :]
    ws = jnp.where(mask[None], w_spatial, jnp.zeros_like(w_spatial))
    mixed = jnp.einsum('gij,bnjgc->bnigc', ws, v)
    mixed = mixed + b_spatial.T[None, None, :, :, None]
    gated = u * mixed.reshape(B, S, SGU_WIDTH)
    return gated @ w_out


def mla_mixer(h, positions, w_in, q_norm_g, w_q_up, kv_norm_g, w_kv_up, w_out):
    B, S, _ = h.shape
    proj = h @ w_in
    q_lat, kv_lat, k_rope = jnp.split(proj, [Q_LORA, Q_LORA + KV_LORA], axis=-1)
    q = (rmsnorm(q_lat, q_norm_g) @ w_q_up).reshape(B, S, MLA_HEADS, QK_DIM)
    q = jnp.concatenate([q[..., :QK_NOPE], rope(q[..., QK_NOPE:], positions)], axis=-1)
    q = q * (QK_DIM ** -0.5)
    k_rope = rope(k_rope[:, :, None, :], positions)
    kv = (rmsnorm(kv_lat, kv_norm_g) @ w_kv_up).reshape(B, S, MLA_HEADS, QK_NOPE + V_DIM)
    k_nope, v = kv[..., :QK_NOPE], kv[..., QK_NOPE:]
    k = jnp.concatenate(
        [k_nope, jnp.broadcast_to(k_rope, (B, S, MLA_HEADS, QK_ROPE))], axis=-1)

    nq = S // Q_BLOCK
    q_blocks = q.reshape(B, nq, Q_BLOCK, MLA_HEADS, QK_DIM).transpose(1, 0, 2, 3, 4)
    key_chunk = jnp.arange(S) // CHUNK

    def attend(args):
        qb, idx = args
        q_chunk = (idx * Q_BLOCK + jnp.arange(Q_BLOCK)) // CHUNK
        mask = key_chunk[None, :] <= q_chunk[:, None]
        s = jnp.einsum('bqhd,bkhd->bhqk', qb, k).astype(jnp.float32)
        s = jnp.where(mask[None, None], s, -jnp.inf)
        p = jax.nn.softmax(s, axis=-1).astype(v.dtype)
        return jnp.einsum('bhqk,bkhd->bqhd', p, v)

    o = lax.map(attend, (q_blocks, jnp.arange(nq)))
    o = o.transpose(1, 0, 2, 3, 4).reshape(B, S, MLA_HEADS * V_DIM)
    return o @ w_out


def setup_inputs(seed: int = 0) -> dict:
    key = jax.random.key(seed)
    ks = jax.random.split(key, 24)

    def nrm(k, shape, scale):
        return jax.random.normal(k, shape, jnp.float32) * scale

    def gain(k, shape):
        return 1.0 + 0.02 * jax.random.normal(k, shape, jnp.float32)

    x = jax.random.normal(ks[0], (BATCH, SEQ, D_MODEL), jnp.float32)
    offset = jax.random.randint(ks[1], (BATCH, 1), 0, 4096, dtype=jnp.int32)
    positions = offset + jnp.arange(SEQ, dtype=jnp.int32)[None, :]
    return {
        "x": x,
        "positions": positions,
        "ln_ffn1": gain(ks[2], (DEPTH, D_MODEL)),
        "ffn1_w_in": nrm(ks[3], (DEPTH, D_MODEL, 2 * D_FF), D_MODEL ** -0.5),
        "ffn1_w_out": nrm(ks[4], (DEPTH, D_FF, D_MODEL), D_FF ** -0.5),
        "ln_mix": gain(ks[5], (DEPTH, D_MODEL)),
        "ln_ffn2": gain(ks[6], (DEPTH, D_MODEL)),
        "ffn2_w_in": nrm(ks[7], (DEPTH, D_MODEL, 2 * D_FF), D_MODEL ** -0.5),
        "ffn2_w_out": nrm(ks[8], (DEPTH, D_FF, D_MODEL), D_FF ** -0.5),
        "sgu_w_in": nrm(ks[9], (N_A_LAYERS, D_MODEL, 2 * SGU_WIDTH), D_MODEL ** -0.5),
        "sgu_v_gain": gain(ks[10], (N_A_LAYERS, SGU_WIDTH)),
        "sgu_v_bias": nrm(ks[11], (N_A_LAYERS, SGU_WIDTH), 0.02),
        "sgu_w_spatial": nrm(ks[12], (N_A_LAYERS, SGU_GROUPS, SGU_BLOCK, SGU_BLOCK), SGU_BLOCK ** -0.5),
        "sgu_b_spatial": gain(ks[13], (N_A_LAYERS, SGU_GROUPS, SGU_BLOCK)),
        "sgu_w_out": nrm(ks[14], (N_A_LAYERS, SGU_WIDTH, D_MODEL), SGU_WIDTH ** -0.5),
        "mla_w_in": nrm(ks[15], (N_B_LAYERS, D_MODEL, Q_LORA + KV_LORA + QK_ROPE), D_MODEL ** -0.5),
        "mla_q_norm": gain(ks[16], (N_B_LAYERS, Q_LORA)),
        "mla_w_q_up": nrm(ks[17], (N_B_LAYERS, Q_LORA, MLA_HEADS * QK_DIM), Q_LORA ** -0.5),
        "mla_kv_norm": gain(ks[18], (N_B_LAYERS, KV_LORA)),
        "mla_w_kv_up": nrm(ks[19], (N_B_LAYERS, KV_LORA, MLA_HEADS * (QK_NOPE + V_DIM)), KV_LORA ** -0.5),
        "mla_w_out": nrm(ks[20], (N_B_LAYERS, MLA_HEADS * V_DIM, D_MODEL), (MLA_HEADS * V_DIM) ** -0.5),
        "ln_final": gain(ks[21], (D_MODEL,)),
    }


def reference(x, positions, ln_ffn1, ffn1_w_in, ffn1_w_out, ln_mix, ln_ffn2, ffn2_w_in, ffn2_w_out,
              sgu_w_in, sgu_v_gain, sgu_v_bias, sgu_w_spatial, sgu_b_spatial, sgu_w_out,
              mla_w_in, mla_q_norm, mla_w_q_up, mla_kv_norm, mla_w_kv_up, mla_w_out, ln_final):
    for i in range(DEPTH):
        x = x + 0.5 * swiglu(rmsnorm(x, ln_ffn1[i]), ffn1_w_in[i], ffn1_w_out[i])
        h = rmsnorm(x, ln_mix[i])
        j = i // N_MIXERS
        if i % N_MIXERS == 0:
            x = x + sgu_mixer(h, sgu_w_in[j], sgu_v_gain[j], sgu_v_bias[j],
                              sgu_w_spatial[j], sgu_b_spatial[j], sgu_w_out[j])
        else:
            x = x + mla_mixer(h, positions, mla_w_in[j], mla_q_norm[j], mla_w_q_up[j],
                              mla_kv_norm[j], mla_w_kv_up[j], mla_w_out[j])
        x = x + 0.5 * swiglu(rmsnorm(x, ln_ffn2[i]), ffn2_w_in[i], ffn2_w_out[i])
    return rmsnorm(x, ln_final)
```

```python
import numpy as np
import concourse.bass as bass
import concourse.mybir as mybir

F32 = mybir.dt.float32
BF16 = mybir.dt.bfloat16
AF = mybir.ActivationFunctionType
ALU = mybir.AluOpType

D = 2048
DFF = 5632
NTOK = 2048
EPS = 1e-6
ENGS = ("sync", "scalar", "vector", "gpsimd", "tensor")


class Chan:
    def __init__(self, sem, inc):
        self.sem = sem
        self.inc = inc
        self.n = 0


class Prog:
    def __init__(self, nc):
        self.nc = nc
        self.q = {e: [] for e in ENGS}
        self.waited = {}
        self.nchan = 0
        self.pe = self.chan("pe")
        self.act = self.chan("act")
        self.dve = self.chan("dve")
        self.pool = self.chan("pool")

    def chan(self, name, dma=False):
        self.nchan += 1
        sem = self.nc.alloc_semaphore(name=f"{name}_{self.nchan}")
        return Chan(sem, 16 if dma else 1)

    def op(self, eng, fn, waits=(), post=None):
        ws = []
        for w in waits:
            if w is None:
                continue
            ch, val = w
            k = (eng, id(ch))
            if self.waited.get(k, 0) >= val:
                continue
            self.waited[k] = val
            ws.append((ch.sem, val))
        tok = None
        if post is not None:
            post.n += post.inc
            tok = (post, post.n)
        self.q[eng].append((fn, ws, post))
        return tok

    def PE(self, fn, waits=(), post=False):
        return self.op("tensor", fn, waits, self.pe if post else None)

    def ACT(self, fn, waits=()):
        return self.op("scalar", fn, waits, self.act)

    def DVE(self, fn, waits=()):
        return self.op("vector", fn, waits, self.dve)

    def emit(self):
        with self.nc.Block() as block:
            for eng in ENGS:
                ops = self.q[eng]

                def body(e, ops=ops):
                    for fn, ws, post in ops:
                        for sem, val in ws:
                            e.wait_ge(sem, val)
                        ins = fn(e)
                        if post is not None:
                            ins.then_inc(post.sem, post.inc)

                getattr(block, eng)(body)


class Mem:
    def __init__(self, P):
        nc = P.nc
        self.P = P
        A = nc.alloc_sbuf_tensor
        self.NW = 3
        self.wraw = [A(f"w{i}", [128, 8192], BF16) for i in range(self.NW)]
        self.wchan = [P.chan(f"wl{i}", dma=True) for i in range(self.NW)]
        self.wfree = [None] * self.NW
        self.wk = 0
        self.hT = A("hT", [128, 16 * 1024], BF16)
        self.big = A("big", [128, 44 * 1024], BF16)
        self.NX = 4
        self.xt = [A(f"xt{i}", [128, 512], F32) for i in range(self.NX)]
        self.xl = [P.chan(f"xl{i}", dma=True) for i in range(self.NX)]
        self.xs = [P.chan(f"xs{i}", dma=True) for i in range(self.NX)]
        self.xk = 0
        self.NSG = 4
        self.sg = [A(f"sg{i}", [128, 512], F32) for i in range(self.NSG)]
        self.sgfree = [None] * self.NSG
        self.sgk = 0
        self.sq = [A(f"sq{i}", [128, 512], BF16) for i in range(2)]
        self.sqfree = [None, None]
        self.rstd = A("rstd", [128, 1024], F32)
        self.ones = A("ones", [128, 128], BF16)
        self.accl = [P.chan(f"accl{i}", dma=True) for i in range(4)]
        self.NB = 5
        self.bank = [nc.alloc_psum_tensor(f"pb{i}", [128, 512], F32) for i in range(5)]
        self.bankT = nc.alloc_psum_tensor("pbT", [128, 1024], BF16)
        self.bankT_free = None
        self.bank += [None, nc.alloc_psum_tensor("pb6", [128, 512], F32), nc.alloc_psum_tensor("pb7", [128, 512], F32)]
        self.bfree = [None] * 8
        self.bk = 0
        self.big_free = None
        self.hT_free = None
        self.store_toks = []

    def bank_acquire(self):
        b = self.bk % self.NB
        self.bk += 1
        return b, self.bfree[b]

    def wslot_acquire(self):
        s = self.wk % self.NW
        self.wk += 1
        return s, self.wfree[s]


def setup_consts(P, M):
    P.op("gpsimd", lambda e: e.memset(M.ones[:], 1.0), post=P.pool)
    M.ones_tok = (P.pool, P.pool.n)


def prologue(P, M, xin, t0, T, gcol, gtok):
    acc = M.big.bitcast(F32).reshape([128, 22, 1024])
    xv = xin.rearrange("(c p) t -> p c t", p=128)
    hT = M.hT.reshape([128, 16, 1024])
    ltoks = []
    for k in range(4):
        waits = [M.big_free] + M.store_toks + getattr(M, "big_free_extra", [])
        tok = P.op("sync", lambda e, k=k: e.dma_start(out=acc[:, 4 * k:4 * k + 4, 0:T],
                                                      in_=xv[:, 4 * k:4 * k + 4, t0:t0 + T]),
                   waits=waits, post=M.accl[k])
        ltoks.append(tok)
    M.store_toks = []
    ntt = T // 512
    rtoks = []
    for tt in range(ntt):
        b, btok = M.bank_acquire()
        cs = slice(tt * 512, (tt + 1) * 512)
        for c in range(16):
            s = c % 2
            atok = P.ACT(lambda e, c=c, s=s, cs=cs: e.activation(out=M.sq[s][:], in_=acc[:, c, cs], func=AF.Square),
                         waits=[ltoks[c // 4], M.sqfree[s]])
            ptok = P.PE(lambda e, c=c, s=s, b=b: e.matmul(M.bank[b][:], M.ones[:], M.sq[s][:], start=(c == 0), stop=(c == 15)),
                        waits=[atok, btok, M.ones_tok], post=True)
            M.sqfree[s] = ptok
        d1 = P.DVE(lambda e, b=b, cs=cs: e.tensor_scalar(M.rstd[:, cs], M.bank[b][:], 1.0 / D, EPS, ALU.mult, ALU.add),
                   waits=[ptok, M.hT_free])
        M.bfree[b] = d1
        a2 = P.ACT(lambda e, cs=cs: e.activation(out=M.rstd[:, cs], in_=M.rstd[:, cs], func=AF.Sqrt), waits=[d1])
        d2 = P.DVE(lambda e, cs=cs: e.reciprocal(M.rstd[:, cs], M.rstd[:, cs]), waits=[a2])
        rtoks.append(d2)
    last = None
    for tt in range(ntt):
        cs = slice(tt * 512, (tt + 1) * 512)
        for c in range(16):
            last = P.DVE(lambda e, c=c, cs=cs: e.scalar_tensor_tensor(out=hT[:, c, cs], in0=acc[:, c, cs], scalar=gcol[:, c:c + 1],
                                                                      in1=M.rstd[:, cs], op0=ALU.mult, op1=ALU.mult),
                         waits=[rtoks[tt], gtok, M.hT_free])
    return acc, hT, last


def stream_linear(P, M, groups, KC, in_T, T, evac, in_tok, pre=None, kp=128):
    ntt = T // 512
    lastpe = None
    for loads_fn, view_fn, chunks in groups:
        s, sfree = M.wslot_acquire()
        raw = M.wraw[s]
        for dst, src in loads_fn(raw):
            ld = P.op("gpsimd", lambda e, dst=dst, src=src: e.dma_start(out=dst, in_=src), waits=[sfree], post=M.wchan[s])
        wv = view_fn(raw)
        for c0, width, tag in chunks:
            for tt in range(ntt):
                if pre is not None:
                    pre(tag, tt)
                b, btok = M.bank_acquire()
                for kc in range(KC):
                    lastk = kc == KC - 1
                    tok = P.PE(lambda e, b=b, kc=kc, c0=c0, width=width, tt=tt, wv=wv: e.matmul(
                        M.bank[b][0:width, :], wv[0:kp, kc, c0:c0 + width], in_T[0:kp, kc, tt * 512:(tt + 1) * 512],
                        start=(kc == 0), stop=(kc == KC - 1)),
                        waits=[ld, btok, in_tok], post=lastk)
                M.bfree[b] = evac(tag, tt, M.bank[b], tok)
                lastpe = tok
        M.wfree[s] = lastpe
    return lastpe


def ffn_phase(P, M, xin, xout, gcol, gtok, w_in, w_out):
    w_in_v = w_in.rearrange("(c p) f -> p c f", p=128)
    w_out_v = w_out.rearrange("(k p) d -> p k d", p=128)
    xin_v = xin.rearrange("(c p) t -> p c t", p=128)
    xout_v = xout.rearrange("(c p) t -> p c t", p=128)
    T = 1024
    actT = M.big.reshape([128, 44, 1024])
    for half in range(2):
        t0 = half * T
        acc, hT, htok = prologue(P, M, xin, t0, T, gcol, gtok)
        groups = []
        for F in range(22):
            f0 = F * 256

            def loads(raw, f0=f0):
                v = raw.reshape([128, 16, 512])
                return [(v[:, :, 0:256], w_in_v[:, :, f0:f0 + 256]),
                        (v[:, :, 256:512], w_in_v[:, :, DFF + f0:DFF + f0 + 256])]
            chunks = [(0, 128, ("g", 2 * F)), (256, 128, ("u", 2 * F)),
                      (128, 128, ("g", 2 * F + 1)), (384, 128, ("u", 2 * F + 1))]
            groups.append((loads, lambda raw: raw.reshape([128, 16, 512]), chunks))
        pend = {}
        st = {"last": None}

        def evac1(tag, tt, bank, tok):
            kind, j = tag
            if kind == "g":
                s = M.sgk % M.NSG
                M.sgk += 1
                a = P.ACT(lambda e, s=s, bank=bank: e.activation(out=M.sg[s][:], in_=bank[:], func=AF.Silu),
                          waits=[tok, M.sgfree[s]])
                pend[(j, tt)] = (s, a)
                return a
            s, a = pend.pop((j, tt))
            d = P.DVE(lambda e, s=s, bank=bank, j=j, tt=tt: e.tensor_tensor(out=actT[:, j, tt * 512:(tt + 1) * 512], in0=M.sg[s][:],
                                                                            in1=bank[:], op=ALU.mult),
                      waits=[tok, a, htok])
            M.sgfree[s] = d
            st["last"] = d
            return d
        pe1 = stream_linear(P, M, groups, 16, hT, T, evac1, htok)
        M.hT_free = pe1
        act_tok = st["last"]
        groups2 = []
        for dc in range(16):
            def loads2(raw, dc=dc):
                v = raw.reshape([128, 64, 128])
                return [(v[:, 0:44, :], w_out_v[:, :, dc * 128:(dc + 1) * 128])]
            groups2.append((loads2, lambda raw: raw.reshape([128, 64, 128]), [(0, 128, dc)]))
        xsl = {}

        def pre2(dc, tt):
            i = M.xk % M.NX
            M.xk += 1
            prev_store = (M.xs[i], M.xs[i].n) if M.xs[i].n else None
            l = P.op("sync", lambda e, i=i, dc=dc, tt=tt, t0=t0: e.dma_start(out=M.xt[i][:], in_=xin_v[:, dc, t0 + tt * 512:t0 + (tt + 1) * 512]),
                     waits=[prev_store], post=M.xl[i])
            xsl[(dc, tt)] = (i, l)

        def evac2(dc, tt, bank, tok):
            i, l = xsl.pop((dc, tt))
            d = P.DVE(lambda e, i=i, bank=bank: e.scalar_tensor_tensor(out=M.xt[i][:], in0=bank[:], scalar=0.5, in1=M.xt[i][:],
                                                                       op0=ALU.mult, op1=ALU.add),
                      waits=[tok, l])
            stok = P.op("sync", lambda e, i=i, dc=dc, tt=tt, t0=t0: e.dma_start(out=xout_v[:, dc, t0 + tt * 512:t0 + (tt + 1) * 512], in_=M.xt[i][:]),
                        waits=[d], post=M.xs[i])
            M.store_toks = [(M.xs[k], M.xs[k].n) for k in range(M.NX) if M.xs[k].n]
            return d
        pe2 = stream_linear(P, M, groups2, 44, actT, T, evac2, act_tok, pre=pre2)
        M.big_free = pe2


def final_norm_phase(P, M, xin, out, gcol, gtok):
    T = 1024
    out_v = out.rearrange("(c p) t -> p c t", p=128)
    ost = P.chan("ost", dma=True)
    for half in range(2):
        t0 = half * T
        acc = M.big.bitcast(F32).reshape([128, 22, 1024])
        xv = xin.rearrange("(c p) t -> p c t", p=128)
        ltoks = []
        for k in range(4):
            waits = [M.big_free] + M.store_toks + ([(ost, ost.n)] if ost.n else [])
            ltoks.append(P.op("sync", lambda e, k=k, t0=t0: e.dma_start(out=acc[:, 4 * k:4 * k + 4, 0:T], in_=xv[:, 4 * k:4 * k + 4, t0:t0 + T]),
                              waits=waits, post=M.accl[k]))
        M.store_toks = []
        rtoks = []
        for tt in range(2):
            b, btok = M.bank_acquire()
            cs = slice(tt * 512, (tt + 1) * 512)
            for c in range(16):
                s = c % 2
                atok = P.ACT(lambda e, c=c, s=s, cs=cs: e.activation(out=M.sq[s][:], in_=acc[:, c, cs], func=AF.Square),
                             waits=[ltoks[c // 4], M.sqfree[s]])
                ptok = P.PE(lambda e, c=c, s=s, b=b: e.matmul(M.bank[b][:], M.ones[:], M.sq[s][:], start=(c == 0), stop=(c == 15)),
                            waits=[atok, btok, M.ones_tok], post=True)
                M.sqfree[s] = ptok
            d1 = P.DVE(lambda e, b=b, cs=cs: e.tensor_scalar(M.rstd[:, cs], M.bank[b][:], 1.0 / D, EPS, ALU.mult, ALU.add), waits=[ptok])
            M.bfree[b] = d1
            a2 = P.ACT(lambda e, cs=cs: e.activation(out=M.rstd[:, cs], in_=M.rstd[:, cs], func=AF.Sqrt), waits=[d1])
            rtoks.append(P.DVE(lambda e, cs=cs: e.reciprocal(M.rstd[:, cs], M.rstd[:, cs]), waits=[a2]))
        for k in range(4):
            for c in range(4 * k, 4 * k + 4):
                for tt in range(2):
                    cs = slice(tt * 512, (tt + 1) * 512)
                    last = P.DVE(lambda e, c=c, cs=cs: e.scalar_tensor_tensor(out=acc[:, c, cs], in0=acc[:, c, cs], scalar=gcol[:, c:c + 1],
                                                                              in1=M.rstd[:, cs], op0=ALU.mult, op1=ALU.mult),
                                 waits=[rtoks[tt], gtok, P.act and (P.act, P.act.n)])
            P.op("sync", lambda e, k=k, t0=t0: e.dma_start(out=out_v[:, 4 * k:4 * k + 4, t0:t0 + T], in_=acc[:, 4 * k:4 * k + 4, 0:T]),
                 waits=[last], post=ost)
        M.big_free = None
    P.op("sync", lambda e: e.wait_ge(ost.sem, ost.n), waits=[])


def resid_evac(P, M, xin_v, xout_v, t0, scale, first_wait=None):
    xsl = {}

    def pre(dc, tt):
        i = M.xk % M.NX
        M.xk += 1
        prev_store = (M.xs[i], M.xs[i].n) if M.xs[i].n else None
        l = P.op("sync", lambda e, i=i, dc=dc, tt=tt: e.dma_start(out=M.xt[i][:], in_=xin_v[:, dc, t0 + tt * 512:t0 + (tt + 1) * 512]),
                 waits=[prev_store, first_wait], post=M.xl[i])
        xsl[(dc, tt)] = (i, l)

    def evac(dc, tt, bank, tok):
        i, l = xsl.pop((dc, tt))
        d = P.DVE(lambda e, i=i, bank=bank: e.scalar_tensor_tensor(out=M.xt[i][:], in0=bank[:], scalar=scale, in1=M.xt[i][:],
                                                                   op0=ALU.mult, op1=ALU.add), waits=[tok, l])
        P.op("sync", lambda e, i=i, dc=dc, tt=tt: e.dma_start(out=xout_v[:, dc, t0 + tt * 512:t0 + (tt + 1) * 512], in_=M.xt[i][:]),
             waits=[d], post=M.xs[i])
        M.store_toks = [(M.xs[k], M.xs[k].n) for k in range(M.NX) if M.xs[k].n]
        return d
    return pre, evac


def sgu_phase(P, M, xin, xout, gcol, gtok, w_in, w_out, C):
    w_in_v = w_in.rearrange("(c p) f -> p c f", p=128)
    w_out_v = w_out.rearrange("(k p) d -> p k d", p=128)
    xin_v = xin.rearrange("(c p) t -> p c t", p=128)
    xout_v = xout.rearrange("(c p) t -> p c t", p=128)
    T = 512
    bigF = M.big.bitcast(F32).reshape([128, 44, 512])
    vT = bigF
    gT = M.big.reshape([128, 44, 1024])
    bsp = bigF[:, 32:40, :]
    mu, rv, msq = bigF[:, 40, :], bigF[:, 41, :], bigF[:, 42, :]
    bch = P.chan("bsp", dma=True)
    btok = P.op("sync", lambda e: e.dma_start(out=bsp, in_=C["bspB"].rearrange("p (g f) -> p g f", g=8)), waits=[M.big_free], post=bch)
    GA, GB = 0.044715, 1.5957691216057308
    SGU_DEPTH = 2

    def gelu_ops(bank, tok, out_ap, extra):
        s = M.sgk % M.NSG
        M.sgk += 1
        sg = M.sg[s]
        a1 = P.ACT(lambda e: e.activation(out=sg[:], in_=bank[:], func=AF.Square), waits=[tok, M.sgfree[s]])
        d1 = P.DVE(lambda e: e.tensor_scalar(sg[:], sg[:], GA, 1.0, ALU.mult, ALU.add), waits=[a1])
        d2 = P.DVE(lambda e: e.tensor_tensor(out=sg[:], in0=sg[:], in1=bank[:], op=ALU.mult), waits=[d1])
        a2 = P.ACT(lambda e: e.activation(out=sg[:], in_=sg[:], func=AF.Sigmoid, scale=GB), waits=[d2])
        d3 = P.DVE(lambda e: e.tensor_tensor(out=out_ap, in0=sg[:], in1=bank[:], op=ALU.mult), waits=[a2] + extra)
        M.sgfree[s] = d3
        return d3

    for q in range(4):
        t0 = q * T
        acc, hT, htok = prologue(P, M, xin, t0, T, gcol, gtok)
        groups = []
        for g in range(8):
            def loads(raw, g=g):
                v = raw.reshape([128, 16, 512])
                return [(v[:, :, :], w_in_v[:, :, 4096 + g * 512:4096 + (g + 1) * 512])]
            groups.append((loads, lambda raw: raw.reshape([128, 16, 512]), [(k * 128, 128, 4 * g + k) for k in range(4)]))
        st = {}

        dq = []

        def evac_v(c, tt, bank, tok):
            d3 = gelu_ops(bank, tok, vT[:, c, :], [htok, btok])

            def tail(c=c, d3=d3):
                for which, fn in ((6, AF.Copy), (7, AF.Square)):
                    k = (2 * c + which) % 2
                    a = P.ACT(lambda e, k=k, fn=fn, c=c: e.activation(out=M.sq[k][:], in_=vT[:, c, :], func=fn), waits=[d3, M.sqfree[k]])
                    p = P.PE(lambda e, k=k, which=which, c=c: e.matmul(M.bank[which][:], M.ones[:], M.sq[k][:], start=(c == 0), stop=(c == 31)),
                             waits=[a, M.bfree[which], M.ones_tok], post=True)
                    M.sqfree[k] = p
                st["p"] = p
            dq.append(tail)
            while len(dq) > SGU_DEPTH:
                dq.pop(0)()
            return d3
        pe1 = stream_linear(P, M, groups, 16, hT, T, evac_v, htok)
        while dq:
            dq.pop(0)()
        d = P.DVE(lambda e: e.tensor_scalar(mu, M.bank[6][:], 1.0 / 4096, 0.0, ALU.mult, ALU.add), waits=[st["p"]])
        d = P.DVE(lambda e: e.tensor_tensor(out=msq, in0=mu, in1=mu, op=ALU.mult), waits=[d])
        d = P.DVE(lambda e: e.scalar_tensor_tensor(out=rv, in0=M.bank[7][:], scalar=1.0 / 4096, in1=msq, op0=ALU.mult, op1=ALU.subtract), waits=[d])
        M.bfree[6] = d
        M.bfree[7] = d
        d = P.DVE(lambda e: e.tensor_scalar(rv, rv, 1.0, EPS, ALU.mult, ALU.add), waits=[d])
        a = P.ACT(lambda e: e.activation(out=rv, in_=rv, func=AF.Sqrt), waits=[d])
        stat_tok = P.DVE(lambda e: e.reciprocal(rv, rv), waits=[a])
        groups = []
        for g in range(8):
            def loads(raw, g=g):
                v = raw.reshape([128, 16, 512])
                return [(v[:, :, :], w_in_v[:, :, g * 512:(g + 1) * 512])]
            groups.append((loads, lambda raw: raw.reshape([128, 16, 512]), [(k * 128, 128, 4 * g + k) for k in range(4)]))
        st3 = {}
        dq3 = []

        def evac_u(c, tt, bank, tok):
            g = c // 4
            i = M.xk % M.NX
            M.xk += 1
            prev_store = (M.xs[i], M.xs[i].n) if M.xs[i].n else None
            ut = M.xt[i]
            du = gelu_ops(bank, tok, ut[:], [prev_store, st3.get(("utfree", i))])
            def tail(c=c, g=g, i=i, ut=ut, du=du):
                d = P.DVE(lambda e: e.tensor_tensor(out=vT[:, c, :], in0=vT[:, c, :], in1=mu, op=ALU.subtract), waits=[stat_tok])
                d = P.DVE(lambda e: e.tensor_tensor(out=vT[:, c, :], in0=vT[:, c, :], in1=rv, op=ALU.mult), waits=[d])
                k = c % 2
                d = P.DVE(lambda e: e.tensor_scalar(M.sq[k][:], vT[:, c, :], C["gam"][:, c:c + 1], C["bet"][:, c:c + 1], ALU.mult, ALU.add),
                          waits=[d, M.sqfree[k], *C["ctoks"]])
                for blk in range(4):
                    p = P.PE(lambda e, blk=blk, k=k: e.transpose(M.bankT[:, blk * 128:(blk + 1) * 128], M.sq[k][:, blk * 128:(blk + 1) * 128], C["ident"][:]),
                             waits=[d, M.bankT_free, *C["ctoks"]], post=(blk == 3))
                M.sqfree[k] = p
                vtok = C["vtok"][c % 2]
                dcp = P.DVE(lambda e: e.tensor_copy(vtok[:], M.bankT[:, 0:512]), waits=[p, st3.get(("vtokfree", c % 2))])
                M.bankT_free = dcp
                b, btk = M.bank_acquire()
                for blk in range(4):
                    p = P.PE(lambda e, blk=blk, b=b, g=g: e.matmul(M.bank[b][:, blk * 128:(blk + 1) * 128], vtok[:, blk * 128:(blk + 1) * 128],
                                                                   C["wsT"][:, g, :], start=True, stop=True),
                             waits=[dcp, btk, *C["ctoks"]], post=(blk == 3))
                st3[("vtokfree", c % 2)] = p
                s = M.sgk % M.NSG
                M.sgk += 1
                sg = M.sg[s]
                d = P.DVE(lambda e: e.tensor_tensor(out=sg[:], in0=M.bank[b][:], in1=bsp[:, g, :], op=ALU.add), waits=[p, M.sgfree[s], btok])
                M.bfree[b] = d
                d = P.DVE(lambda e: e.tensor_tensor(out=gT[:, c, 0:512], in0=sg[:], in1=ut[:], op=ALU.mult), waits=[d, du])
                M.sgfree[s] = d
                st3[("utfree", i)] = d
                st3["last"] = d
            dq3.append(tail)
            while len(dq3) > SGU_DEPTH:
                dq3.pop(0)()
            return du
        pe3 = stream_linear(P, M, groups, 16, hT, T, evac_u, htok)
        while dq3:
            dq3.pop(0)()
        pe3 = (P.pe, P.pe.n)
        M.hT_free = pe3
        groups = []
        for dc in range(16):
            def loads4(raw, dc=dc):
                v = raw.reshape([128, 64, 128])
                return [(v[:, 0:32, :], w_out_v[:, :, dc * 128:(dc + 1) * 128])]
            groups.append((loads4, lambda raw: raw.reshape([128, 64, 128]), [(0, 128, dc)]))
        pre, evac = resid_evac(P, M, xin_v, xout_v, t0, 1.0, first_wait=st3["last"])
        pe4 = stream_linear(P, M, groups, 32, gT, T, evac, st3["last"], pre=pre)
        M.big_free = pe4


I32 = mybir.dt.int32
NH = 16
QS = 192.0 ** -0.5
PI = float(np.pi)
NEG = -30000.0


def stg_setup(P, M):
    nc = P.nc
    M.stg = [nc.alloc_sbuf_tensor(f"stg{i}", [128, 512], BF16) for i in range(4)]
    M.sts = [P.chan(f"sts{i}", dma=True) for i in range(4)]
    M.stk = 0


def stg_acquire(M):
    i = M.stk % 4
    M.stk += 1
    prev = (M.sts[i], M.sts[i].n) if M.sts[i].n else None
    return i, prev


def mla_a_phase(P, M, xin, gcol, gtok, w_in, w_q_up, w_kv_up, C, out):
    w_in_v = w_in.rearrange("(c p) f -> p c f", p=128)
    wq_v = w_q_up.rearrange("(c p) f -> p c f", p=128)
    wkv_v = w_kv_up.rearrange("(c p) f -> p c f", p=128)
    bigF = M.big.bitcast(F32).reshape([128, 22, 1024])
    bigI = M.big.bitcast(I32).reshape([128, 22, 1024])
    bigB = M.big.reshape([128, 44, 1024])
    latT = bigF
    qlnT = bigB[:, 20:24, :]
    kvlnT = bigB[:, 24:28, :]
    cos2 = bigF[0:64, 16:18, :]
    sinS = bigF[0:64, 18:20, :]
    rq, rkv = bigF[:, 20, :], bigF[:, 21, :]
    pch = P.chan("pos", dma=True)
    posi = bigI[0:64, 0:2, :]
    t = P.op("sync", lambda e: e.dma_start(out=posi, in_=C["pos"].rearrange("p (a t) -> p a t", a=2)), waits=[M.big_free] + M.store_toks, post=pch)
    posf, ang, tmpk, tmpf = bigF[0:64, 2:4, :], bigF[0:64, 4:6, :], bigI[0:64, 6:8, :], bigF[0:64, 8:10, :]
    d = P.DVE(lambda e: e.tensor_copy(posf, posi), waits=[t])
    d = P.DVE(lambda e: e.tensor_scalar(ang, posf, C["invf"][:, 0:1], 0.0, ALU.mult, ALU.add), waits=[d] + C["ctoks"])

    def reduce_sin(dst, shift, d):
        d = P.DVE(lambda e: e.tensor_scalar(dst, ang, 1.0, shift, ALU.mult, ALU.add), waits=[d])
        d = P.DVE(lambda e: e.tensor_scalar(tmpk, dst, 1.0 / (2 * PI), 0.0, ALU.mult, ALU.add), waits=[d])
        d = P.DVE(lambda e: e.tensor_copy(tmpf, tmpk), waits=[d])
        d = P.DVE(lambda e: e.scalar_tensor_tensor(out=dst, in0=tmpf, scalar=-2 * PI, in1=dst, op0=ALU.mult, op1=ALU.add), waits=[d])
        d = P.DVE(lambda e: e.tensor_scalar(tmpf, dst, PI, -2 * PI, ALU.is_gt, ALU.mult), waits=[d])
        d = P.DVE(lambda e: e.tensor_tensor(out=dst, in0=dst, in1=tmpf, op=ALU.add), waits=[d])
        d = P.DVE(lambda e: e.tensor_scalar(tmpf, dst, -PI, 2 * PI, ALU.is_lt, ALU.mult), waits=[d])
        d = P.DVE(lambda e: e.tensor_tensor(out=dst, in0=dst, in1=tmpf, op=ALU.add), waits=[d])
        d = P.DVE(lambda e: e.tensor_scalar(dst, dst, -3.14159, 3.14159, ALU.max, ALU.min), waits=[d])
        a = P.ACT(lambda e: e.activation(out=dst, in_=dst, func=AF.Sin), waits=[d])
        return a
    a = reduce_sin(cos2, PI / 2, d)
    a2 = reduce_sin(sinS, 0.0, a)
    tabtok = P.DVE(lambda e: e.tensor_scalar(sinS, sinS, C["sign"][:, 0:1], 0.0, ALU.mult, ALU.add), waits=[a2])
    M.big_free_extra = [tabtok]
    T = 1024
    for half in range(2):
        t0 = half * T
        acc, hT, htok = prologue(P, M, xin, t0, T, gcol, gtok)
        cosh, sinh = cos2[:, half, :], sinS[:, half, :]
        def l0(raw):
            return [(raw.reshape([128, 16, 512])[:, :, :], w_in_v[:, :, 0:512])]

        def l1(raw):
            return [(raw.reshape([128, 16, 512])[:, :, :], w_in_v[:, :, 512:1024])]

        def l2(raw):
            v = raw.reshape([128, 16, 512])
            return [(v[:, :, 0:64], w_in_v[:, :, 1024:1088]), (v[:, :, 64:96], w_in_v[:, :, 1056:1088]), (v[:, :, 96:128], w_in_v[:, :, 1024:1056])]
        vw = lambda raw: raw.reshape([128, 16, 512])
        groups = [(l0, vw, [(k * 128, 128, k) for k in range(4)]), (l1, vw, [(k * 128, 128, 4 + k) for k in range(4)]),
                  (l2, vw, [(0, 64, 8), (64, 64, 9)])]
        st = {}

        def evac_lat(k, tt, bank, tok):
            w = 128 if k < 8 else 64
            a = P.ACT(lambda e: e.activation(out=latT[0:w, k, tt * 512:(tt + 1) * 512], in_=bank[0:w, :], func=AF.Copy), waits=[tok, htok])
            st[(k, tt)] = a
            return a
        pe = stream_linear(P, M, groups, 16, hT, T, evac_lat, htok)
        M.hT_free = pe
        lntok = {}
        for name, base, gl, dstT, rs in (("q", 0, C["gq"], qlnT, rq), ("kv", 4, C["gkv"], kvlnT, rkv)):
            for tt in range(2):
                cs = slice(tt * 512, (tt + 1) * 512)
                b, btok = M.bank_acquire()
                for c in range(4):
                    s = c % 2
                    at = P.ACT(lambda e, c=c, s=s, cs=cs, base=base: e.activation(out=M.sq[s][:], in_=latT[:, base + c, cs], func=AF.Square),
                               waits=[st[(base + c, tt)], M.sqfree[s]])
                    pt = P.PE(lambda e, c=c, s=s, b=b: e.matmul(M.bank[b][:], M.ones[:], M.sq[s][:], start=(c == 0), stop=(c == 3)),
                              waits=[at, btok, M.ones_tok], post=True)
                    M.sqfree[s] = pt
                d1 = P.DVE(lambda e, b=b, cs=cs, rs=rs: e.tensor_scalar(rs[:, cs], M.bank[b][:], 1.0 / 512, EPS, ALU.mult, ALU.add), waits=[pt])
                M.bfree[b] = d1
                a2 = P.ACT(lambda e, cs=cs, rs=rs: e.activation(out=rs[:, cs], in_=rs[:, cs], func=AF.Sqrt), waits=[d1])
                d2 = P.DVE(lambda e, cs=cs, rs=rs: e.reciprocal(rs[:, cs], rs[:, cs]), waits=[a2])
                for c in range(4):
                    dl = P.DVE(lambda e, c=c, cs=cs, rs=rs, gl=gl, dstT=dstT, base=base: e.scalar_tensor_tensor(
                        out=dstT[:, c, cs], in0=latT[:, base + c, cs], scalar=gl[:, c:c + 1], in1=rs[:, cs], op0=ALU.mult, op1=ALU.mult),
                        waits=[d2] + C["ctoks"])
            lntok[name] = dl
        for tt in range(2):
            cs = slice(tt * 512, (tt + 1) * 512)
            d = P.DVE(lambda e, cs=cs, cosh=cosh: e.tensor_tensor(out=latT[0:64, 8, cs], in0=latT[0:64, 8, cs], in1=cosh[:, cs], op=ALU.mult),
                      waits=[st[(8, tt)], tabtok])
            d = P.DVE(lambda e, cs=cs, sinh=sinh: e.tensor_tensor(out=latT[0:64, 9, cs], in0=latT[0:64, 9, cs], in1=sinh[:, cs], op=ALU.mult),
                      waits=[st[(9, tt)], d])
            i, prev = stg_acquire(M)
            d = P.DVE(lambda e, cs=cs, i=i: e.tensor_tensor(out=M.stg[i][0:64, :], in0=latT[0:64, 8, cs], in1=latT[0:64, 9, cs], op=ALU.add), waits=[d, prev])
            P.op("sync", lambda e, t0=t0, tt=tt, i=i: e.dma_start(out=out["KrT"][:, t0 + tt * 512:t0 + (tt + 1) * 512], in_=M.stg[i][0:64, :]),
                 waits=[d], post=M.sts[i])
        groups = []
        for hg in range(4):
            h0 = hg * 4

            def lq(raw, h0=h0):
                v = raw.reshape([128, 4, 2048])
                L = [(v[:, :, 0:768], wq_v[:, :, h0 * 192:h0 * 192 + 768])]
                for k in range(4):
                    b0 = (h0 + k) * 192
                    L.append((v[:, :, 768 + k * 64:768 + k * 64 + 32], wq_v[:, :, b0 + 160:b0 + 192]))
                    L.append((v[:, :, 768 + k * 64 + 32:768 + k * 64 + 64], wq_v[:, :, b0 + 128:b0 + 160]))
                return L
            ch = []
            for k in range(4):
                ch += [(k * 192, 128, ("qn", h0 + k)), (k * 192 + 128, 64, ("qr", h0 + k)), (768 + k * 64, 64, ("qs", h0 + k))]
            groups.append((lq, lambda raw: raw.reshape([128, 4, 2048]), ch))
        pend = {}

        def evac_q(tag, tt, bank, tok):
            kind, h = tag
            cs = slice(tt * 512, (tt + 1) * 512)
            gs = slice(t0 + tt * 512, t0 + (tt + 1) * 512)
            if kind == "qn":
                i, prev = stg_acquire(M)
                a = P.ACT(lambda e: e.activation(out=M.stg[i][:], in_=bank[:], func=AF.Copy, scale=QS), waits=[tok, prev])
                P.op("sync", lambda e: e.dma_start(out=out["Qn"][h, :, gs], in_=M.stg[i][:]), waits=[a], post=M.sts[i])
                return a
            if kind == "qr":
                s = M.sgk % M.NSG
                M.sgk += 1
                a = P.ACT(lambda e: e.activation(out=M.sg[s][0:64, :], in_=bank[0:64, :], func=AF.Copy), waits=[tok, M.sgfree[s]])
                pend[(h, tt)] = (s, a)
                return a
            s, a = pend.pop((h, tt))
            csl, ssl = cosh[:, cs], sinh[:, cs]
            s2 = M.sgk % M.NSG
            M.sgk += 1
            d2 = P.DVE(lambda e: e.scalar_tensor_tensor(out=M.sg[s2][0:64, :], in0=bank[0:64, :], scalar=QS, in1=ssl, op0=ALU.mult, op1=ALU.mult),
                       waits=[tok, M.sgfree[s2], tabtok])
            d1 = P.DVE(lambda e: e.scalar_tensor_tensor(out=M.sg[s][0:64, :], in0=M.sg[s][0:64, :], scalar=QS, in1=csl, op0=ALU.mult, op1=ALU.mult),
                       waits=[a, d2])
            i, prev = stg_acquire(M)
            d3 = P.DVE(lambda e: e.tensor_tensor(out=M.stg[i][0:64, :], in0=M.sg[s][0:64, :], in1=M.sg[s2][0:64, :], op=ALU.add), waits=[d1, prev])
            M.sgfree[s] = d3
            M.sgfree[s2] = d3
            P.op("sync", lambda e: e.dma_start(out=out["Qr"][h, :, gs], in_=M.stg[i][0:64, :]), waits=[d3], post=M.sts[i])
            return d2
        stream_linear(P, M, groups, 4, qlnT, T, evac_q, lntok["q"])
        groups = []
        for hg in range(2):
            def lk(raw, hg=hg):
                return [(raw.reshape([128, 4, 2048])[:, :, :], wkv_v[:, :, hg * 2048:(hg + 1) * 2048])]
            groups.append((lk, lambda raw: raw.reshape([128, 4, 2048]), [(k * 256, 128, hg * 8 + k) for k in range(8)]))

        def evac_kn(h, tt, bank, tok):
            gs = slice(t0 + tt * 512, t0 + (tt + 1) * 512)
            i, prev = stg_acquire(M)
            a = P.ACT(lambda e: e.activation(out=M.stg[i][:], in_=bank[:], func=AF.Copy), waits=[tok, prev])
            P.op("sync", lambda e: e.dma_start(out=out["KnT_f"](h)[:, gs], in_=M.stg[i][:]), waits=[a], post=M.sts[i])
            return a
        stream_linear(P, M, groups, 4, kvlnT, T, evac_kn, lntok["kv"])
        lastpe = None
        for hg in range(4):
            h0 = hg * 4
            s, sfree = M.wslot_acquire()
            wv = M.wraw[s].reshape([128, 4, 2048])
            for k in range(4):
                ld = P.op("gpsimd", lambda e, k=k, wv=wv, h0=h0: e.dma_start(out=wv[:, :, k * 128:(k + 1) * 128],
                                                                           in_=wkv_v[:, :, (h0 + k) * 256 + 128:(h0 + k) * 256 + 256]),
                          waits=[sfree], post=M.wchan[s])
            for blk in range(T // 128):
                b, btok = M.bank_acquire()
                for kc in range(4):
                    tok = P.PE(lambda e, b=b, kc=kc, blk=blk, wv=wv: e.matmul(M.bank[b][:], kvlnT[:, kc, blk * 128:(blk + 1) * 128], wv[:, kc, 0:512],
                                                                              start=(kc == 0), stop=(kc == 3)),
                               waits=[ld, btok, lntok["kv"]], post=(kc == 3))
                i, prev = stg_acquire(M)
                a = P.ACT(lambda e, b=b, i=i: e.activation(out=M.stg[i][:], in_=M.bank[b][:], func=AF.Copy), waits=[tok, prev])
                M.bfree[b] = a
                bg = half * 8 + blk
                for jp in range(2):
                    P.op("sync", lambda e, i=i, h0=h0, bg=bg, jp=jp: e.dma_start(out=out["V_f"](h0 + 2 * jp)[:, :, bg, :].rearrange("h p d -> p h d"),
                                                                                 in_=M.stg[i].reshape([128, 4, 128])[:, 2 * jp:2 * jp + 2, :]),
                         waits=[a], post=M.sts[i])
                lastpe = tok
            M.wfree[s] = lastpe
        M.big_free = lastpe
        M.big_free_extra = [(P.dve, P.dve.n), (P.act, P.act.n)]
    M.store_toks = M.store_toks + [(M.sts[i], M.sts[i].n) for i in range(4) if M.sts[i].n] + [(pch, pch.n)]


def attn_setup(P, M):
    nc = P.nc
    if hasattr(M, "stg"):
        M.pT = M.stg[0:4]
    else:
        M.pT = [nc.alloc_sbuf_tensor(f"pT{i}", [128, 512], BF16) for i in range(4)]
    M.pTfree = [None] * 4
    M.pTk = 0


def mla_b_phase(P, M, xin, xout, w_out, Qn, Qr, Kf, KrT_all, Vf, qmask_dram, kmask_dram, keytile):
    w_out_v = w_out.rearrange("(c p) f -> p c f", p=128)
    xin_v = xin.rearrange("(c p) t -> p c t", p=128)
    xout_v = xout.rearrange("(c p) t -> p c t", p=128)
    bigB = M.big.reshape([128, 44, 1024])
    hB = M.hT.reshape([128, 32, 512])
    Qn_g, Qr_g = hB[:, 0:16, :], hB[:, 16:32, :]
    OT = bigB[:, 32:40, :].rearrange("p c (a t) -> p (c a) t", a=2)
    kset = [bigB[:, 16 * s:16 * s + 8, :] for s in range(2)]
    vset = [bigB[:, 16 * s + 8:16 * s + 16, :] for s in range(2)]
    kch = [P.chan(f"kv{s}", dma=True) for s in range(2)]
    kvfree = [None, None]
    qch = P.chan("qld", dma=True)
    cch = P.chan("acst", dma=True)
    M.NW = 2
    M.wk = 0
    kr3 = M.wraw[2].reshape([128, 4, 2048])
    qm = kr3[64:66, 0, :].rearrange("p (m t) -> p m t", m=4)
    kmask = kr3[64:66, 1, 0:128]
    w0 = [M.big_free] + M.store_toks + getattr(M, "big_free_extra", [])
    P.op("sync", lambda e: e.dma_start(out=kr3[0:64, :, :], in_=KrT_all.rearrange("r p t -> p r t")), waits=w0 + [M.wfree[2]], post=cch)
    cch2 = P.chan("acst2", dma=True)
    P.op("gpsimd", lambda e: e.dma_start(out=kr3[64:66, 0, :], in_=qmask_dram[:, :]), waits=[M.wfree[2]], post=cch2)
    P.op("gpsimd", lambda e: e.dma_start(out=kmask, in_=kmask_dram[:, :]), post=cch2)
    ctoks = [(cch, cch.n), (cch2, cch2.n)]
    kvk = 0
    qfree = None
    ot_free = None
    for g in range(4):
        P.op("sync", lambda e, g=g: e.dma_start(out=Qn_g, in_=Qn[:, :, g * 512:(g + 1) * 512].rearrange("h p t -> p h t")), waits=[qfree, M.hT_free] + w0, post=qch)
        P.op("sync", lambda e, g=g: e.dma_start(out=Qr_g[0:64], in_=Qr[:, :, g * 512:(g + 1) * 512].rearrange("h p t -> p h t")), waits=[qfree], post=qch)
        qtok = (qch, qch.n)
        nkb = 16 * g + 16
        for h in range(NH):
            s = kvk % 2
            kvk += 1
            P.op("sync", lambda e, s=s, h=h: e.dma_start(out=kset[s].rearrange("p (r a) t -> p r a t", a=2), in_=Kf(h).rearrange("r p (a t) -> p r a t", a=2)), waits=[kvfree[s]] + w0, post=kch[s])
            P.op("sync", lambda e, s=s, h=h: e.dma_start(out=vset[s].rearrange("p (r a) (b d) -> p r a b d", a=2, d=128), in_=Vf(h).rearrange("r p (a b) d -> p r a b d", a=2)), waits=[kvfree[s]], post=kch[s])
            kvtok = (kch[s], kch[s].n)
            K4 = kset[s].rearrange("p (r a) t -> p r a t", a=2)
            V5 = vset[s].rearrange("p (r a) (b d) -> p r a b d", a=2, d=128)
            pq = []
            lastpv = None

            def emit_pv(item, last, V5=V5, kvtok=kvtok):
                kb, c0, pslot, atok = item
                r, i = keytile(kb)
                t1 = P.PE(lambda e: e.matmul(M.bank[6][:, c0:512], V5[:, r, i // 8, i % 8, :], M.pT[pslot][:, c0:512], start=(kb == 0), stop=last),
                          waits=[atok, M.bfree[6], kvtok])
                t2 = P.PE(lambda e: e.matmul(M.bank[7][:, c0:512], M.ones[:], M.pT[pslot][:, c0:512], start=(kb == 0), stop=last),
                          waits=[M.bfree[7], M.ones_tok], post=True)
                M.pTfree[pslot] = t2
                return t2
            for kb in range(nkb):
                r, i = keytile(kb)
                rel = kb - 16 * g
                j0 = 0 if rel <= 3 else (rel - 3 + 3) // 4
                c0 = j0 * 128
                b, btok = M.bank_acquire()
                kcols = slice((i % 8) * 128, (i % 8) * 128 + 128)
                krcols = slice(i * 128, i * 128 + 128)
                maybe = rel >= 0
                P.PE(lambda e, b=b, c0=c0, r=r, i=i, kcols=kcols, h=h, K4=K4: e.matmul(M.bank[b][:, c0:512], K4[:, r, i // 8, kcols], Qn_g[:, h, c0:512], start=True, stop=False),
                     waits=[btok, kvtok, qtok] + ctoks)
                tok = P.PE(lambda e, b=b, c0=c0, r=r, krcols=krcols, h=h, maybe=maybe: e.matmul(M.bank[b][:, c0:512], kr3[0:64, r, krcols], Qr_g[0:64, h, c0:512],
                                                                                              start=False, stop=(not maybe)), post=True)
                if maybe:
                    jm, m = rel // 4, rel % 4
                    tok = P.PE(lambda e, b=b, jm=jm, m=m: e.matmul(M.bank[b][:, jm * 128:(jm + 1) * 128], kmask, qm[:, m, jm * 128:(jm + 1) * 128],
                                                                  start=False, stop=True), post=True)
                pslot = M.pTk % 4
                M.pTk += 1
                atok = P.ACT(lambda e, b=b, c0=c0, pslot=pslot: e.activation(out=M.pT[pslot][:, c0:512], in_=M.bank[b][:, c0:512], func=AF.Exp),
                             waits=[tok, M.pTfree[pslot]])
                M.bfree[b] = atok
                pq.append((kb, c0, pslot, atok))
                if len(pq) > 2:
                    emit_pv(pq.pop(0), False)
            while pq:
                lastpv = emit_pv(pq.pop(0), len(pq) == 0)
            kvfree[s] = lastpv
            sgi = M.sgk % M.NSG
            M.sgk += 1
            d = P.DVE(lambda e, sgi=sgi: e.reciprocal(M.sg[sgi][:], M.bank[7][:]), waits=[lastpv, M.sgfree[sgi]])
            M.bfree[7] = d
            d = P.DVE(lambda e, sgi=sgi, h=h: e.tensor_tensor(out=OT[:, h, :], in0=M.bank[6][:], in1=M.sg[sgi][:], op=ALU.mult), waits=[d, ot_free])
            M.bfree[6] = d
            M.sgfree[sgi] = d
        qfree = lastpv
        groups = []
        for dg in range(4):
            def lo(raw, dg=dg):
                return [(raw.reshape([128, 16, 512])[:, :, :], w_out_v[:, :, dg * 512:(dg + 1) * 512])]
            groups.append((lo, lambda raw: raw.reshape([128, 16, 512]), [(k * 128, 128, dg * 4 + k) for k in range(4)]))
        pre, evac = resid_evac(P, M, xin_v, xout_v, g * 512, 1.0)
        pe = stream_linear(P, M, groups, 16, OT, 512, evac, d, pre=pre)
        M.big_free = pe
        ot_free = pe
    M.hT_free = qfree
    M.big_free_extra = []
    M.NW = 3
    M.wk = 0
    M.wfree[2] = qfree


def _load_small(P, nc, specs):
    ch = P.chan("small", dma=True)
    ch2 = P.chan("small2", dma=True)
    T = {}
    for name, src, shape, dt, cast in specs:
        t = nc.alloc_sbuf_tensor(name + "_sb", shape, dt)
        T[name] = t
        if cast:
            P.op("gpsimd", lambda e, t=t, src=src: e.dma_start(out=t[:], in_=src), post=ch2)
        else:
            P.op("sync", lambda e, t=t, src=src: e.dma_start(out=t[:], in_=src), post=ch)
    toks = [(c, c.n) for c in (ch, ch2) if c.n]
    return T, toks


def build_program(mode, phases):
    nc = bass.Bass("TRN2", target_bir_lowering=False)
    I = lambda n, s, dt=F32: nc.dram_tensor(n, s, dt, kind="ExternalInput").ap()
    O = lambda n, s, dt=F32: nc.dram_tensor(n, s, dt, kind="ExternalOutput").ap()
    P = Prog(nc)
    M = Mem(P)
    setup_consts(P, M)
    xT = I("xT", [D, NTOK])
    gcd = I("gc", [128, 112])
    small = [("gc", gcd[:, :], [128, 112], F32, False)]
    cur = xT
    if mode == "A":
        xs = O("xT_mid", [D, NTOK])
        if "sgu" in phases:
            sd = {k: I(k, s) for k, s in (("gam", [128, 32]), ("bet", [128, 32]), ("wsT", [128, 1024]), ("ident", [128, 128]), ("bspB", [128, 4096]), ("zeros", [64, 512]))}
            small += [("gam", sd["gam"][:, :], [128, 32], F32, False), ("bet", sd["bet"][:, :], [128, 32], F32, False),
                      ("wsT", sd["wsT"][:, :], [128, 1024], BF16, True), ("ident", sd["ident"][:, :], [128, 128], BF16, True)]
        if "mlaA" in phases:
            md = {k: I(k, s) for k, s in (("gq", [128, 4]), ("gkv", [128, 4]), ("invf", [64, 1]), ("sign", [64, 1]))}
            pos = I("pos", [64, NTOK], I32)
            small += [(k, md[k][:, :], list(md[k].shape), F32, False) for k in md]
        S, stoks = _load_small(P, nc, small)
        gc = S["gc"]
        gtok = stoks[0]
        W = {}
        for name, shape in (("f1_in0", [D, 2 * DFF]), ("f1_out0", [DFF, D]), ("f2_in0", [D, 2 * DFF]), ("f2_out0", [DFF, D]),
                            ("f1_in1", [D, 2 * DFF]), ("f1_out1", [DFF, D]), ("sgu_w_in", [D, 8192]), ("sgu_w_out", [4096, D]),
                            ("mla_w_in", [D, 1088]), ("w_q_up", [512, 3072]), ("w_kv_up", [512, 4096])):
            need = {"f1_in0": "f10", "f1_out0": "f10", "f2_in0": "f20", "f2_out0": "f20", "f1_in1": "f11", "f1_out1": "f11",
                    "sgu_w_in": "sgu", "sgu_w_out": "sgu", "mla_w_in": "mlaA", "w_q_up": "mlaA", "w_kv_up": "mlaA"}[name]
            if need in phases:
                W[name] = I(name, shape)
        if "f10" in phases:
            ffn_phase(P, M, cur, xs, gc[:, 0:16], gtok, W["f1_in0"], W["f1_out0"])
            cur = xs
        if "sgu" in phases:
            w3 = S["wsT"].reshape([128, 8, 128])
            zch = P.chan("zmask", dma=True)
            t = P.op("gpsimd", lambda e: e.dma_start(out=w3[64:128, :, 0:64], in_=sd["zeros"].rearrange("p (g f) -> p g f", g=8)), waits=stoks, post=zch)
            C = {"gam": S["gam"], "bet": S["bet"], "wsT": w3, "ident": S["ident"], "bspB": sd["bspB"], "ctoks": stoks + [t],
                 "vtok": [nc.alloc_sbuf_tensor(f"vtok{i}", [128, 512], BF16) for i in range(2)]}
            sgu_phase(P, M, cur, xs, gc[:, 32:48], gtok, W["sgu_w_in"], W["sgu_w_out"], C)
            cur = xs
        if "f20" in phases:
            ffn_phase(P, M, cur, xs, gc[:, 64:80], gtok, W["f2_in0"], W["f2_out0"])
            cur = xs
        if "f11" in phases:
            ffn_phase(P, M, cur, xs, gc[:, 16:32], gtok, W["f1_in1"], W["f1_out1"])
            cur = xs
        if "mlaA" in phases:
            stg_setup(P, M)
            out = {"Qn": O("Qn", [NH, 128, NTOK], BF16), "Qr": O("Qr", [NH, 64, NTOK], BF16), "KnT": O("KnT", [NH, 128, NTOK], BF16),
                   "KrT": O("KrT", [64, NTOK], BF16), "V": O("V", [NH, 128, 16, 128], BF16)}
            out["KnT_f"] = lambda h: out["KnT"][h]
            out["V_f"] = lambda h: out["V"][h:h + 2]
            C = {"gq": S["gq"], "gkv": S["gkv"], "invf": S["invf"], "sign": S["sign"], "ctoks": stoks, "pos": pos}
            mla_a_phase(P, M, cur, gc[:, 48:64], gtok, W["mla_w_in"], W["w_q_up"], W["w_kv_up"], C, out)
        if cur is xT:
            cp = P.chan("cp", dma=True)
            P.op("sync", lambda e: e.dma_start(out=xs[:, :], in_=xT[:, :]), post=cp)
            M.store_toks = M.store_toks + [(cp, cp.n)]
        P.op("sync", lambda e: e.nop(), waits=M.store_toks)
    else:
        oT = O("oT", [D, NTOK])
        xs = nc.dram_tensor("xs_scratch", [D, NTOK], F32).ap()
        S, stoks = _load_small(P, nc, small)
        gc = S["gc"]
        gtok = stoks[0]
        if "mlaB" in phases:
            attn_setup(P, M)
            Qn, Qr = I("Qn", [NH, 128, NTOK], BF16), I("Qr", [NH, 64, NTOK], BF16)
            KnT_all, KrT_all = I("KnT_all", [4, NH, 128, NTOK], BF16), I("KrT_all", [4, 64, NTOK], BF16)
            V_all = I("V_all", [4, NH, 128, 16, 128], BF16)
            qmask, kmask = I("qmask", [2, 2048]), I("kmask", [2, 128])
            w_out = I("mla_w_out", [D, D])
            mla_b_phase(P, M, cur, xs, w_out, Qn, Qr, lambda h: KnT_all[:, h], KrT_all, lambda h: V_all[:, h], qmask, kmask, lambda kb: (kb // 16, kb % 16))
            cur = xs
        if "f21" in phases:
            w1, w2 = I("f2_in1", [D, 2 * DFF]), I("f2_out1", [DFF, D])
            ffn_phase(P, M, cur, xs, gc[:, 80:96], gtok, w1, w2)
            cur = xs
        final_norm_phase(P, M, cur, oT, gc[:, 96:112], gtok)
    P.emit()
    return nc


def _keytile_rankmajor(kb):
    c8, j2 = kb % 8, kb // 8
    return (c8, 2 * j2) if c8 < 4 else (7 - c8, 2 * j2 + 1)


def build_fused():
    nc = bass.Bass("TRN2", target_bir_lowering=False)
    I = lambda n, s, dt=F32: nc.dram_tensor(n, s, dt, kind="ExternalInput").ap()
    P = Prog(nc)
    M = Mem(P)
    setup_consts(P, M)
    xT = I("xT", [D, NTOK])
    oT = nc.dram_tensor("oT", [D, NTOK], F32, kind="ExternalOutput").ap()
    xs = nc.dram_tensor("xs_scratch", [D, NTOK], F32).ap()
    gcd = I("gc", [128, 112])
    sd = {k: I(k, s) for k, s in (("gam", [128, 32]), ("bet", [128, 32]), ("wsT", [128, 1024]), ("ident", [128, 128]), ("bspB", [128, 4096]), ("zeros", [64, 512]))}
    md = {k: I(k, s) for k, s in (("gq", [128, 4]), ("gkv", [128, 4]), ("invf", [64, 1]), ("sign", [64, 1]))}
    pos = I("pos", [64, NTOK], I32)
    qmask, kmask = I("qmask", [2, 2048]), I("kmask", [2, 128])
    small = [("gc", gcd[:, :], [128, 112], F32, False),
             ("gam", sd["gam"][:, :], [128, 32], F32, False), ("bet", sd["bet"][:, :], [128, 32], F32, False),
             ("wsT", sd["wsT"][:, :], [128, 1024], BF16, True), ("ident", sd["ident"][:, :], [128, 128], BF16, True)]
    small += [(k, md[k][:, :], list(md[k].shape), F32, False) for k in md]
    S, stoks = _load_small(P, nc, small)
    gc = S["gc"]
    gtok = stoks[0]
    W = {name: I(name, shape) for name, shape in (
        ("f1_in0", [D, 2 * DFF]), ("f1_out0", [DFF, D]), ("f2_in0", [D, 2 * DFF]), ("f2_out0", [DFF, D]),
        ("f1_in1", [D, 2 * DFF]), ("f1_out1", [DFF, D]), ("f2_in1", [D, 2 * DFF]), ("f2_out1", [DFF, D]),
        ("sgu_w_in", [D, 8192]), ("sgu_w_out", [4096, D]),
        ("mla_w_in", [D, 1088]), ("w_q_up", [512, 3072]), ("w_kv_up", [512, 4096]), ("mla_w_out", [D, D]))}
    ffn_phase(P, M, xT, xs, gc[:, 0:16], gtok, W["f1_in0"], W["f1_out0"])
    w3 = S["wsT"].reshape([128, 8, 128])
    zch = P.chan("zmask", dma=True)
    t = P.op("gpsimd", lambda e: e.dma_start(out=w3[64:128, :, 0:64], in_=sd["zeros"].rearrange("p (g f) -> p g f", g=8)), waits=stoks, post=zch)
    C = {"gam": S["gam"], "bet": S["bet"], "wsT": w3, "ident": S["ident"], "bspB": sd["bspB"], "ctoks": stoks + [t],
         "vtok": [nc.alloc_sbuf_tensor(f"vtok{i}", [128, 512], BF16) for i in range(2)]}
    sgu_phase(P, M, xs, xs, gc[:, 32:48], gtok, W["sgu_w_in"], W["sgu_w_out"], C)
    ffn_phase(P, M, xs, xs, gc[:, 64:80], gtok, W["f2_in0"], W["f2_out0"])
    ffn_phase(P, M, xs, xs, gc[:, 16:32], gtok, W["f1_in1"], W["f1_out1"])
    stg_setup(P, M)
    Qn = nc.dram_tensor("Qn_s", [NH, 128, NTOK], BF16).ap()
    Qr = nc.dram_tensor("Qr_s", [NH, 64, NTOK], BF16).ap()
    kn_own = [nc.dram_tensor(f"kn_own{k}", [256, NTOK], BF16).ap() for k in range(8)]
    v_own = [nc.dram_tensor(f"v_own{k}", [256, NTOK], BF16).ap() for k in range(8)]
    kr_own = nc.dram_tensor("kr_own", [64, NTOK], BF16).ap()
    kn_all = [nc.dram_tensor(f"kn_all{k}", [4 * 256, NTOK], BF16).ap() for k in range(8)]
    v_all = [nc.dram_tensor(f"v_all{k}", [4 * 256, NTOK], BF16).ap() for k in range(8)]
    kr_all = nc.dram_tensor("kr_all", [4 * 64, NTOK], BF16).ap()
    out = {"Qn": Qn, "Qr": Qr, "KrT": kr_own,
           "KnT_f": lambda h: kn_own[h // 2][(h % 2) * 128:(h % 2) * 128 + 128, :],
           "V_f": lambda h: v_own[h // 2].rearrange("(h p) (b d) -> h p b d", p=128, d=128)}
    CA = {"gq": S["gq"], "gkv": S["gkv"], "invf": S["invf"], "sign": S["sign"], "ctoks": stoks, "pos": pos}
    mla_a_phase(P, M, xs, gc[:, 48:64], gtok, W["mla_w_in"], W["w_q_up"], W["w_kv_up"], CA, out)
    cc = P.chan("cc")
    RG = [[0, 1, 2, 3], [4, 5, 6, 7]]
    for src, dst in [(kr_own, kr_all)] + list(zip(kn_own, kn_all)) + list(zip(v_own, v_all)):
        cctok = P.op("gpsimd", lambda e, src=src, dst=dst: e.collective_compute("AllGather", op=ALU.bypass, replica_groups=RG,
                                                                               ins=[src.opt()], outs=[dst.opt()]),
                     waits=M.store_toks, post=cc)
    M.store_toks = [cctok]
    Kf = lambda h: kn_all[h // 2].rearrange("(r q) t -> r q t", r=4)[:, (h % 2) * 128:(h % 2) * 128 + 128, :]
    Vf = lambda h: v_all[h // 2].rearrange("(r q) (b d) -> r q b d", r=4, d=128)[:, (h % 2) * 128:(h % 2) * 128 + 128]
    KrT_all = kr_all.rearrange("(r p) t -> r p t", r=4)
    attn_setup(P, M)
    mla_b_phase(P, M, xs, xs, W["mla_w_out"], Qn, Qr, Kf, KrT_all, Vf, qmask, kmask, _keytile_rankmajor)
    ffn_phase(P, M, xs, xs, gc[:, 80:96], gtok, W["f2_in1"], W["f2_out1"])
    final_norm_phase(P, M, xs, oT, gc[:, 96:112], gtok)
    P.emit()
    return nc


from concourse.bass_utils import run_bass_kernel_spmd
import ml_dtypes

_BF = ml_dtypes.bfloat16
SEQ = 8192
_FUSED = False


def _gblock(r, i):
    return 8 * (i // 2) + (r if i % 2 == 0 else 7 - r)


def _tok_index(r):
    return np.concatenate([np.arange(_gblock(r, i) * 128, _gblock(r, i) * 128 + 128) for i in range(16)])


def _col(v, n):
    return np.ascontiguousarray(np.asarray(v, np.float32).reshape(n, 128).T)


def _qmask(r):
    qm = np.zeros((2, 4, 4, 128), np.float32)
    for j in range(4):
        delta = r if j % 2 == 0 else 3 - r
        for m in range(4):
            if m > delta:
                qm[0, m, j, :] = NEG
            elif m == delta:
                qm[1, m, j, :64] = NEG
    return qm.reshape(2, 2048)


def kernel(x, positions, ln_ffn1, ffn1_w_in, ffn1_w_out, ln_mix, ln_ffn2, ffn2_w_in, ffn2_w_out,
           sgu_w_in, sgu_v_gain, sgu_v_bias, sgu_w_spatial, sgu_b_spatial, sgu_w_out,
           mla_w_in, mla_q_norm, mla_w_q_up, mla_kv_norm, mla_w_kv_up, mla_w_out, ln_final):
    f32 = lambda a: np.ascontiguousarray(np.asarray(a, np.float32))
    x = np.asarray(x, np.float32)
    positions = np.asarray(positions, np.int32)
    gc = np.concatenate([_col(ln_ffn1[0], 16), _col(ln_ffn1[1], 16), _col(ln_mix[0], 16), _col(ln_mix[1], 16),
                         _col(ln_ffn2[0], 16), _col(ln_ffn2[1], 16), _col(ln_final, 16)], axis=1)
    invf = (1.0 / (10000.0 ** (np.arange(32, dtype=np.float32) / 32))).astype(np.float32)
    common = {
        "gc": gc, "gam": _col(sgu_v_gain[0], 32), "bet": _col(sgu_v_bias[0], 32),
        "wsT": np.ascontiguousarray(np.asarray(sgu_w_spatial[0], np.float32).transpose(2, 0, 1).reshape(128, 1024)),
        "ident": np.eye(128, dtype=np.float32),
        "bspB": np.ascontiguousarray(np.broadcast_to(np.tile(np.asarray(sgu_b_spatial[0], np.float32), (1, 4)).reshape(1, 4096), (128, 4096))),
        "zeros": np.zeros((64, 512), np.float32),
        "gq": _col(mla_q_norm[0], 4), "gkv": _col(mla_kv_norm[0], 4),
        "invf": np.concatenate([invf, invf])[:, None].copy(),
        "sign": np.concatenate([-np.ones(32), np.ones(32)]).astype(np.float32)[:, None].copy(),
        "f1_in0": f32(ffn1_w_in[0]), "f1_out0": f32(ffn1_w_out[0]), "f2_in0": f32(ffn2_w_in[0]), "f2_out0": f32(ffn2_w_out[0]),
        "f1_in1": f32(ffn1_w_in[1]), "f1_out1": f32(ffn1_w_out[1]),
        "sgu_w_in": f32(sgu_w_in[0]), "sgu_w_out": f32(sgu_w_out[0]),
        "mla_w_in": f32(mla_w_in[0]), "w_q_up": f32(mla_w_q_up[0]), "w_kv_up": f32(mla_w_kv_up[0]),
    }
    idxs = [_tok_index(r) for r in range(4)]
    kmask = np.stack([np.ones(128), np.concatenate([np.zeros(64), np.ones(64)])]).astype(np.float32)
    common.update({"kmask": kmask, "mla_w_out": f32(mla_w_out[0]), "f2_in1": f32(ffn2_w_in[1]), "f2_out1": f32(ffn2_w_out[1])})
    maps = []
    for c in range(8):
        b, r = c // 4, c % 4
        m = dict(common)
        m["xT"] = np.ascontiguousarray(x[b, idxs[r]].T)
        m["pos"] = np.ascontiguousarray(np.broadcast_to(positions[b, idxs[r]][None], (64, 2048))).astype(np.int32)
        m["qmask"] = _qmask(r)
        maps.append(m)
    nc = build_fused()
    R = run_bass_kernel_spmd(nc, maps, core_ids=list(range(8))).results
    out = np.empty((2, SEQ, D), np.float32)
    for c in range(8):
        b, r = c // 4, c % 4
        out[b, idxs[r]] = R[c]["oT"].T
    return out
```

```python
import numpy as np
import concourse.bass as bass
import concourse.mybir as mybir

F32 = mybir.dt.float32
BF16 = mybir.dt.bfloat16
AF = mybir.ActivationFunctionType
ALU = mybir.AluOpType

D = 2048
DFF = 5632
NTOK = 2048
EPS = 1e-6
ENGS = ("sync", "scalar", "vector", "gpsimd", "tensor")


class Chan:
    def __init__(self, sem, inc):
        self.sem = sem
        self.inc = inc
        self.n = 0


class Prog:
    def __init__(self, nc):
        self.nc = nc
        self.q = {e: [] for e in ENGS}
        self.waited = {}
        self.nchan = 0
        self.pe = self.chan("pe")
        self.act = self.chan("act")
        self.dve = self.chan("dve")
        self.pool = self.chan("pool")

    def chan(self, name, dma=False):
        self.nchan += 1
        sem = self.nc.alloc_semaphore(name=f"{name}_{self.nchan}")
        return Chan(sem, 16 if dma else 1)

    def op(self, eng, fn, waits=(), post=None):
        ws = []
        for w in waits:
            if w is None:
                continue
            ch, val = w
            k = (eng, id(ch))
            if self.waited.get(k, 0) >= val:
                continue
            self.waited[k] = val
            ws.append((ch.sem, val))
        tok = None
        if post is not None:
            post.n += post.inc
            tok = (post, post.n)
        self.q[eng].append((fn, ws, post))
        return tok

    def PE(self, fn, waits=(), post=False):
        return self.op("tensor", fn, waits, self.pe if post else None)

    def ACT(self, fn, waits=()):
        return self.op("scalar", fn, waits, self.act)

    def DVE(self, fn, waits=()):
        return self.op("vector", fn, waits, self.dve)

    def emit(self):
        with self.nc.Block() as block:
            for eng in ENGS:
                ops = self.q[eng]

                def body(e, ops=ops):
                    for fn, ws, post in ops:
                        for sem, val in ws:
                            e.wait_ge(sem, val)
                        ins = fn(e)
                        if post is not None:
                            ins.then_inc(post.sem, post.inc)

                getattr(block, eng)(body)


class Mem:
    def __init__(self, P):
        nc = P.nc
        self.P = P
        A = nc.alloc_sbuf_tensor
        self.NW = 3
        self.wraw = [A(f"w{i}", [128, 8192], BF16) for i in range(self.NW)]
        self.wchan = [P.chan(f"wl{i}", dma=True) for i in range(self.NW)]
        self.wfree = [None] * self.NW
        self.wsave = [None] * self.NW
        self.wk = 0
        self.hT = A("hT", [128, 16 * 1024], BF16)
        self.big = A("big", [128, 44 * 1024], BF16)
        self.NX = 4
        self.xt = [A(f"xt{i}", [128, 512], F32) for i in range(self.NX)]
        self.xl = [P.chan(f"xl{i}", dma=True) for i in range(self.NX)]
        self.xs = [P.chan(f"xs{i}", dma=True) for i in range(self.NX)]
        self.xk = 0
        self.NSG = 4
        self.sg = [A(f"sg{i}", [128, 512], F32) for i in range(self.NSG)]
        self.sgfree = [None] * self.NSG
        self.sgk = 0
        self.sq = [A(f"sq{i}", [128, 512], BF16) for i in range(2)]
        self.sqfree = [None, None]
        self.rstd = A("rstd", [128, 1024], F32)
        self.ones = A("ones", [128, 128], BF16)
        self.accl = [P.chan(f"accl{i}", dma=True) for i in range(4)]
        self.NB = 5
        self.bank = [nc.alloc_psum_tensor(f"pb{i}", [128, 512], F32) for i in range(5)]
        self.bankT = nc.alloc_psum_tensor("pbT", [128, 1024], BF16)
        self.bankT_free = None
        self.bank += [None, nc.alloc_psum_tensor("pb6", [128, 512], F32), nc.alloc_psum_tensor("pb7", [128, 512], F32)]
        self.bfree = [None] * 8
        self.bk = 0
        self.big_free = None
        self.hT_free = None
        self.store_toks = []

    def bank_acquire(self):
        b = self.bk % self.NB
        self.bk += 1
        return b, self.bfree[b]

    def wslot_acquire(self):
        s = self.wk % self.NW
        self.wk += 1
        return s, self.wfree[s]


def setup_consts(P, M):
    P.op("gpsimd", lambda e: e.memset(M.ones[:], 1.0), post=P.pool)
    M.ones_tok = (P.pool, P.pool.n)


def prologue(P, M, xin, t0, T, gcol, gtok):
    acc = M.big.bitcast(F32).reshape([128, 22, 1024])
    xv = xin.rearrange("(c p) t -> p c t", p=128)
    hT = M.hT.reshape([128, 16, 1024])
    ltoks = []
    for k in range(4):
        waits = [M.big_free] + M.store_toks + getattr(M, "big_free_extra", [])
        tok = P.op("sync", lambda e, k=k: e.dma_start(out=acc[:, 4 * k:4 * k + 4, 0:T],
                                                      in_=xv[:, 4 * k:4 * k + 4, t0:t0 + T]),
                   waits=waits, post=M.accl[k])
        ltoks.append(tok)
    M.store_toks = []
    ntt = T // 512
    rtoks = []
    for tt in range(ntt):
        b, btok = M.bank_acquire()
        cs = slice(tt * 512, (tt + 1) * 512)
        for c in range(16):
            s = c % 2
            atok = P.ACT(lambda e, c=c, s=s, cs=cs: e.activation(out=M.sq[s][:], in_=acc[:, c, cs], func=AF.Square),
                         waits=[ltoks[c // 4], M.sqfree[s]])
            ptok = P.PE(lambda e, c=c, s=s, b=b: e.matmul(M.bank[b][:], M.ones[:], M.sq[s][:], start=(c == 0), stop=(c == 15)),
                        waits=[atok, btok, M.ones_tok], post=True)
            M.sqfree[s] = ptok
        d1 = P.DVE(lambda e, b=b, cs=cs: e.tensor_scalar(M.rstd[:, cs], M.bank[b][:], 1.0 / D, EPS, ALU.mult, ALU.add),
                   waits=[ptok, M.hT_free])
        M.bfree[b] = d1
        a2 = P.ACT(lambda e, cs=cs: e.activation(out=M.rstd[:, cs], in_=M.rstd[:, cs], func=AF.Sqrt), waits=[d1])
        d2 = P.DVE(lambda e, cs=cs: e.reciprocal(M.rstd[:, cs], M.rstd[:, cs]), waits=[a2])
        rtoks.append(d2)
    last = None
    for tt in range(ntt):
        cs = slice(tt * 512, (tt + 1) * 512)
        for c in range(16):
            last = P.DVE(lambda e, c=c, cs=cs: e.scalar_tensor_tensor(out=hT[:, c, cs], in0=acc[:, c, cs], scalar=gcol[:, c:c + 1],
                                                                      in1=M.rstd[:, cs], op0=ALU.mult, op1=ALU.mult),
                         waits=[rtoks[tt], gtok, M.hT_free])
    return acc, hT, last


def stream_linear(P, M, groups, KC, in_T, T, evac, in_tok, pre=None, kp=128):
    ntt = T // 512
    lastpe = None
    for grp in groups:
        loads_fn, view_fn, chunks = grp[0], grp[1], grp[2]
        ex = grp[3] if len(grp) > 3 else {}
        s, sfree = M.wslot_acquire()
        raw = M.wraw[s]
        for dst, src in loads_fn(raw):
            ld = P.op("gpsimd", lambda e, dst=dst, src=src: e.dma_start(out=dst, in_=src),
                      waits=[sfree, M.wsave[s]] + ex.get("waits", []), post=M.wchan[s])
        if "save" in ex:
            sdst, ssrc_fn, cb = ex["save"]
            ssrc = ssrc_fn(raw)
            M.wsave[s] = P.op("gpsimd", lambda e, sdst=sdst, ssrc=ssrc: e.dma_start(out=sdst, in_=ssrc), waits=[ld], post=M.wchan[s])
            cb(M.wsave[s])
        wv = view_fn(raw)
        for c0, width, tag in chunks:
            for tt in range(ntt):
                if pre is not None:
                    pre(tag, tt)
                b, btok = M.bank_acquire()
                for kc in range(KC):
                    lastk = kc == KC - 1
                    tok = P.PE(lambda e, b=b, kc=kc, c0=c0, width=width, tt=tt, wv=wv: e.matmul(
                        M.bank[b][0:width, :], wv[0:kp, kc, c0:c0 + width], in_T[0:kp, kc, tt * 512:(tt + 1) * 512],
                        start=(kc == 0), stop=(kc == KC - 1)),
                        waits=[ld, btok, in_tok], post=lastk)
                M.bfree[b] = evac(tag, tt, M.bank[b], tok)
                lastpe = tok
        M.wfree[s] = lastpe
    return lastpe


def ffn_phase(P, M, xin, xout, gcol, gtok, w_in, w_out):
    w_in_v = w_in.rearrange("(c p) f -> p c f", p=128)
    w_out_v = w_out.rearrange("(k p) d -> p k d", p=128)
    xin_v = xin.rearrange("(c p) t -> p c t", p=128)
    xout_v = xout.rearrange("(c p) t -> p c t", p=128)
    T = 1024
    actT = M.big.reshape([128, 44, 1024])
    for half in range(2):
        t0 = half * T
        acc, hT, htok = prologue(P, M, xin, t0, T, gcol, gtok)
        groups = []
        for F in range(22):
            f0 = F * 256

            def loads(raw, f0=f0):
                v = raw.reshape([128, 16, 512])
                return [(v[:, :, 0:256], w_in_v[:, :, f0:f0 + 256]),
                        (v[:, :, 256:512], w_in_v[:, :, DFF + f0:DFF + f0 + 256])]
            chunks = [(0, 128, ("g", 2 * F)), (256, 128, ("u", 2 * F)),
                      (128, 128, ("g", 2 * F + 1)), (384, 128, ("u", 2 * F + 1))]
            groups.append((loads, lambda raw: raw.reshape([128, 16, 512]), chunks))
        pend = {}
        st = {"last": None}

        def evac1(tag, tt, bank, tok):
            kind, j = tag
            if kind == "g":
                s = M.sgk % M.NSG
                M.sgk += 1
                a = P.ACT(lambda e, s=s, bank=bank: e.activation(out=M.sg[s][:], in_=bank[:], func=AF.Silu),
                          waits=[tok, M.sgfree[s]])
                pend[(j, tt)] = (s, a)
                return a
            s, a = pend.pop((j, tt))
            d = P.DVE(lambda e, s=s, bank=bank, j=j, tt=tt: e.tensor_tensor(out=actT[:, j, tt * 512:(tt + 1) * 512], in0=M.sg[s][:],
                                                                            in1=bank[:], op=ALU.mult),
                      waits=[tok, a, htok])
            M.sgfree[s] = d
            st["last"] = d
            return d
        pe1 = stream_linear(P, M, groups, 16, hT, T, evac1, htok)
        M.hT_free = pe1
        act_tok = st["last"]
        groups2 = []
        for dc in range(16):
            def loads2(raw, dc=dc):
                v = raw.reshape([128, 64, 128])
                return [(v[:, 0:44, :], w_out_v[:, :, dc * 128:(dc + 1) * 128])]
            groups2.append((loads2, lambda raw: raw.reshape([128, 64, 128]), [(0, 128, dc)]))
        xsl = {}

        def pre2(dc, tt):
            i = M.xk % M.NX
            M.xk += 1
            prev_store = (M.xs[i], M.xs[i].n) if M.xs[i].n else None
            l = P.op("sync", lambda e, i=i, dc=dc, tt=tt, t0=t0: e.dma_start(out=M.xt[i][:], in_=xin_v[:, dc, t0 + tt * 512:t0 + (tt + 1) * 512]),
                     waits=[prev_store], post=M.xl[i])
            xsl[(dc, tt)] = (i, l)

        def evac2(dc, tt, bank, tok):
            i, l = xsl.pop((dc, tt))
            d = P.DVE(lambda e, i=i, bank=bank: e.scalar_tensor_tensor(out=M.xt[i][:], in0=bank[:], scalar=0.5, in1=M.xt[i][:],
                                                                       op0=ALU.mult, op1=ALU.add),
                      waits=[tok, l])
            stok = P.op("sync", lambda e, i=i, dc=dc, tt=tt, t0=t0: e.dma_start(out=xout_v[:, dc, t0 + tt * 512:t0 + (tt + 1) * 512], in_=M.xt[i][:]),
                        waits=[d], post=M.xs[i])
            M.store_toks = [(M.xs[k], M.xs[k].n) for k in range(M.NX) if M.xs[k].n]
            return d
        pe2 = stream_linear(P, M, groups2, 44, actT, T, evac2, act_tok, pre=pre2)
        M.big_free = pe2


def final_norm_phase(P, M, xin, out, gcol, gtok):
    T = 1024
    out_v = out.rearrange("(c p) t -> p c t", p=128)
    ost = P.chan("ost", dma=True)
    for half in range(2):
        t0 = half * T
        acc = M.big.bitcast(F32).reshape([128, 22, 1024])
        xv = xin.rearrange("(c p) t -> p c t", p=128)
        ltoks = []
        for k in range(4):
            waits = [M.big_free] + M.store_toks + ([(ost, ost.n)] if ost.n else [])
            ltoks.append(P.op("sync", lambda e, k=k, t0=t0: e.dma_start(out=acc[:, 4 * k:4 * k + 4, 0:T], in_=xv[:, 4 * k:4 * k + 4, t0:t0 + T]),
                              waits=waits, post=M.accl[k]))
        M.store_toks = []
        rtoks = []
        for tt in range(2):
            b, btok = M.bank_acquire()
            cs = slice(tt * 512, (tt + 1) * 512)
            for c in range(16):
                s = c % 2
                atok = P.ACT(lambda e, c=c, s=s, cs=cs: e.activation(out=M.sq[s][:], in_=acc[:, c, cs], func=AF.Square),
                             waits=[ltoks[c // 4], M.sqfree[s]])
                ptok = P.PE(lambda e, c=c, s=s, b=b: e.matmul(M.bank[b][:], M.ones[:], M.sq[s][:], start=(c == 0), stop=(c == 15)),
                            waits=[atok, btok, M.ones_tok], post=True)
                M.sqfree[s] = ptok
            d1 = P.DVE(lambda e, b=b, cs=cs: e.tensor_scalar(M.rstd[:, cs], M.bank[b][:], 1.0 / D, EPS, ALU.mult, ALU.add), waits=[ptok])
            M.bfree[b] = d1
            a2 = P.ACT(lambda e, cs=cs: e.activation(out=M.rstd[:, cs], in_=M.rstd[:, cs], func=AF.Sqrt), waits=[d1])
            rtoks.append(P.DVE(lambda e, cs=cs: e.reciprocal(M.rstd[:, cs], M.rstd[:, cs]), waits=[a2]))
        for k in range(4):
            for c in range(4 * k, 4 * k + 4):
                for tt in range(2):
                    cs = slice(tt * 512, (tt + 1) * 512)
                    last = P.DVE(lambda e, c=c, cs=cs: e.scalar_tensor_tensor(out=acc[:, c, cs], in0=acc[:, c, cs], scalar=gcol[:, c:c + 1],
                                                                              in1=M.rstd[:, cs], op0=ALU.mult, op1=ALU.mult),
                                 waits=[rtoks[tt], gtok, P.act and (P.act, P.act.n)])
            P.op("sync", lambda e, k=k, t0=t0: e.dma_start(out=out_v[:, 4 * k:4 * k + 4, t0:t0 + T], in_=acc[:, 4 * k:4 * k + 4, 0:T]),
                 waits=[last], post=ost)
        M.big_free = None
    P.op("sync", lambda e: e.wait_ge(ost.sem, ost.n), waits=[])


def resid_evac(P, M, xin_v, xout_v, t0, scale, first_wait=None):
    xsl = {}

    def pre(dc, tt):
        i = M.xk % M.NX
        M.xk += 1
        prev_store = (M.xs[i], M.xs[i].n) if M.xs[i].n else None
        l = P.op("sync", lambda e, i=i, dc=dc, tt=tt: e.dma_start(out=M.xt[i][:], in_=xin_v[:, dc, t0 + tt * 512:t0 + (tt + 1) * 512]),
                 waits=[prev_store, first_wait], post=M.xl[i])
        xsl[(dc, tt)] = (i, l)

    def evac(dc, tt, bank, tok):
        i, l = xsl.pop((dc, tt))
        d = P.DVE(lambda e, i=i, bank=bank: e.scalar_tensor_tensor(out=M.xt[i][:], in0=bank[:], scalar=scale, in1=M.xt[i][:],
                                                                   op0=ALU.mult, op1=ALU.add), waits=[tok, l])
        P.op("sync", lambda e, i=i, dc=dc, tt=tt: e.dma_start(out=xout_v[:, dc, t0 + tt * 512:t0 + (tt + 1) * 512], in_=M.xt[i][:]),
             waits=[d], post=M.xs[i])
        M.store_toks = [(M.xs[k], M.xs[k].n) for k in range(M.NX) if M.xs[k].n]
        return d
    return pre, evac


def sgu_phase(P, M, xin, xout, gcol, gtok, w_in, w_out, C):
    nc = P.nc
    w_in_v = w_in.rearrange("(c p) f -> p c f", p=128)
    w_out_v = w_out.rearrange("(k p) d -> p k d", p=128)
    xin_v = xin.rearrange("(c p) t -> p c t", p=128)
    xout_v = xout.rearrange("(c p) t -> p c t", p=128)
    T = 512
    bigF = M.big.bitcast(F32).reshape([128, 44, 512])
    vT = bigF
    gT = M.big.reshape([128, 44, 1024])
    bsp = bigF[:, 32:40, :]
    mu, rv, msq = bigF[:, 40, :], bigF[:, 41, :], bigF[:, 42, :]
    bch = P.chan("bsp", dma=True)
    btok = P.op("sync", lambda e: e.dma_start(out=bsp, in_=C["bspB"].rearrange("p (g f) -> p g f", g=8)), waits=[M.big_free], post=bch)
    GA, GB = 0.044715, 1.5957691216057308
    SGU_DEPTH = 2

    def gelu_ops(bank, tok, out_ap, extra):
        s = M.sgk % M.NSG
        M.sgk += 1
        sg = M.sg[s]
        a1 = P.ACT(lambda e: e.activation(out=sg[:], in_=bank[:], func=AF.Square), waits=[tok, M.sgfree[s]])
        d1 = P.DVE(lambda e: e.tensor_scalar(sg[:], sg[:], GA, 1.0, ALU.mult, ALU.add), waits=[a1])
        d2 = P.DVE(lambda e: e.tensor_tensor(out=sg[:], in0=sg[:], in1=bank[:], op=ALU.mult), waits=[d1])
        a2 = P.ACT(lambda e: e.activation(out=sg[:], in_=sg[:], func=AF.Sigmoid, scale=GB), waits=[d2])
        d3 = P.DVE(lambda e: e.tensor_tensor(out=out_ap, in0=sg[:], in1=bank[:], op=ALU.mult), waits=[a2] + extra)
        M.sgfree[s] = d3
        return d3

    scr = {}
    scr_tok = {}

    def _sgu_group(q, key, src_view, vshape, kc, chunks):
        vw = lambda raw: raw.reshape(vshape)
        if q == 0:
            scr[key] = nc.dram_tensor("sguscr_%s_%d" % key, [128, kc, vshape[2]], BF16).ap()

            def loads(raw):
                return [(raw.reshape(vshape)[:, 0:kc, :], src_view)]

            def cb(tok, key=key):
                scr_tok[key] = tok
            return (loads, vw, chunks, {"save": (scr[key], lambda raw: raw.reshape(vshape)[:, 0:kc, :], cb)})

        def loads2(raw):
            return [(raw.reshape(vshape)[:, 0:kc, :], scr[key])]
        return (loads2, vw, chunks, {"waits": [scr_tok[key]]})

    for q in range(4):
        t0 = q * T
        acc, hT, htok = prologue(P, M, xin, t0, T, gcol, gtok)
        groups = []
        for g in range(8):
            groups.append(_sgu_group(q, ("v", g), w_in_v[:, :, 4096 + g * 512:4096 + (g + 1) * 512], [128, 16, 512], 16,
                                     [(k * 128, 128, 4 * g + k) for k in range(4)]))
        st = {}

        dq = []

        def evac_v(c, tt, bank, tok):
            d3 = gelu_ops(bank, tok, vT[:, c, :], [htok, btok])

            def tail(c=c, d3=d3):
                for which, fn in ((6, AF.Copy), (7, AF.Square)):
                    k = (2 * c + which) % 2
                    a = P.ACT(lambda e, k=k, fn=fn, c=c: e.activation(out=M.sq[k][:], in_=vT[:, c, :], func=fn), waits=[d3, M.sqfree[k]])
                    p = P.PE(lambda e, k=k, which=which, c=c: e.matmul(M.bank[which][:], M.ones[:], M.sq[k][:], start=(c == 0), stop=(c == 31)),
                             waits=[a, M.bfree[which], M.ones_tok], post=True)
                    M.sqfree[k] = p
                st["p"] = p
            dq.append(tail)
            while len(dq) > SGU_DEPTH:
                dq.pop(0)()
            return d3
        pe1 = stream_linear(P, M, groups, 16, hT, T, evac_v, htok)
        while dq:
            dq.pop(0)()
        d = P.DVE(lambda e: e.tensor_scalar(mu, M.bank[6][:], 1.0 / 4096, 0.0, ALU.mult, ALU.add), waits=[st["p"]])
        d = P.DVE(lambda e: e.tensor_tensor(out=msq, in0=mu, in1=mu, op=ALU.mult), waits=[d])
        d = P.DVE(lambda e: e.scalar_tensor_tensor(out=rv, in0=M.bank[7][:], scalar=1.0 / 4096, in1=msq, op0=ALU.mult, op1=ALU.subtract), waits=[d])
        M.bfree[6] = d
        M.bfree[7] = d
        d = P.DVE(lambda e: e.tensor_scalar(rv, rv, 1.0, EPS, ALU.mult, ALU.add), waits=[d])
        a = P.ACT(lambda e: e.activation(out=rv, in_=rv, func=AF.Sqrt), waits=[d])
        stat_tok = P.DVE(lambda e: e.reciprocal(rv, rv), waits=[a])
        groups = []
        for g in range(8):
            groups.append(_sgu_group(q, ("u", g), w_in_v[:, :, g * 512:(g + 1) * 512], [128, 16, 512], 16,
                                     [(k * 128, 128, 4 * g + k) for k in range(4)]))
        st3 = {}
        dq3 = []

        def evac_u(c, tt, bank, tok):
            g = c // 4
            i = M.xk % M.NX
            M.xk += 1
            prev_store = (M.xs[i], M.xs[i].n) if M.xs[i].n else None
            ut = M.xt[i]
            du = gelu_ops(bank, tok, ut[:], [prev_store, st3.get(("utfree", i))])
            def tail(c=c, g=g, i=i, ut=ut, du=du):
                d = P.DVE(lambda e: e.tensor_tensor(out=vT[:, c, :], in0=vT[:, c, :], in1=mu, op=ALU.subtract), waits=[stat_tok])
                d = P.DVE(lambda e: e.tensor_tensor(out=vT[:, c, :], in0=vT[:, c, :], in1=rv, op=ALU.mult), waits=[d])
                k = c % 2
                d = P.DVE(lambda e: e.tensor_scalar(M.sq[k][:], vT[:, c, :], C["gam"][:, c:c + 1], C["bet"][:, c:c + 1], ALU.mult, ALU.add),
                          waits=[d, M.sqfree[k], *C["ctoks"]])
                for blk in range(4):
                    p = P.PE(lambda e, blk=blk, k=k: e.transpose(M.bankT[:, blk * 128:(blk + 1) * 128], M.sq[k][:, blk * 128:(blk + 1) * 128], C["ident"][:]),
                             waits=[d, M.bankT_free, *C["ctoks"]], post=(blk == 3))
                M.sqfree[k] = p
                vtok = C["vtok"][c % 2]
                dcp = P.DVE(lambda e: e.tensor_copy(vtok[:], M.bankT[:, 0:512]), waits=[p, st3.get(("vtokfree", c % 2))])
                M.bankT_free = dcp
                b, btk = M.bank_acquire()
                for blk in range(4):
                    p = P.PE(lambda e, blk=blk, b=b, g=g: e.matmul(M.bank[b][:, blk * 128:(blk + 1) * 128], vtok[:, blk * 128:(blk + 1) * 128],
                                                                   C["wsT"][:, g, :], start=True, stop=True),
                             waits=[dcp, btk, *C["ctoks"]], post=(blk == 3))
                st3[("vtokfree", c % 2)] = p
                s = M.sgk % M.NSG
                M.sgk += 1
                sg = M.sg[s]
                d = P.DVE(lambda e: e.tensor_tensor(out=sg[:], in0=M.bank[b][:], in1=bsp[:, g, :], op=ALU.add), waits=[p, M.sgfree[s], btok])
                M.bfree[b] = d
                d = P.DVE(lambda e: e.tensor_tensor(out=gT[:, c, 0:512], in0=sg[:], in1=ut[:], op=ALU.mult), waits=[d, du])
                M.sgfree[s] = d
                st3[("utfree", i)] = d
                st3["last"] = d
            dq3.append(tail)
            while len(dq3) > SGU_DEPTH:
                dq3.pop(0)()
            return du
        pe3 = stream_linear(P, M, groups, 16, hT, T, evac_u, htok)
        while dq3:
            dq3.pop(0)()
        pe3 = (P.pe, P.pe.n)
        M.hT_free = pe3
        groups = []
        for dc in range(16):
            groups.append(_sgu_group(q, ("o", dc), w_out_v[:, :, dc * 128:(dc + 1) * 128], [128, 64, 128], 32, [(0, 128, dc)]))
        pre, evac = resid_evac(P, M, xin_v, xout_v, t0, 1.0, first_wait=st3["last"])
        pe4 = stream_linear(P, M, groups, 32, gT, T, evac, st3["last"], pre=pre)
        M.big_free = pe4


I32 = mybir.dt.int32
NH = 16
QS = 192.0 ** -0.5
PI = float(np.pi)
NEG = -30000.0


def stg_setup(P, M):
    nc = P.nc
    M.stg = [nc.alloc_sbuf_tensor(f"stg{i}", [128, 512], BF16) for i in range(4)]
    M.sts = [P.chan(f"sts{i}", dma=True) for i in range(4)]
    M.stk = 0


def stg_acquire(M):
    i = M.stk % 4
    M.stk += 1
    prev = (M.sts[i], M.sts[i].n) if M.sts[i].n else None
    return i, prev


def mla_a_phase(P, M, xin, gcol, gtok, w_in, w_q_up, w_kv_up, C, out):
    w_in_v = w_in.rearrange("(c p) f -> p c f", p=128)
    wq_v = w_q_up.rearrange("(c p) f -> p c f", p=128)
    wkv_v = w_kv_up.rearrange("(c p) f -> p c f", p=128)
    bigF = M.big.bitcast(F32).reshape([128, 22, 1024])
    bigI = M.big.bitcast(I32).reshape([128, 22, 1024])
    bigB = M.big.reshape([128, 44, 1024])
    latT = bigF
    qlnT = bigB[:, 20:24, :]
    kvlnT = bigB[:, 24:28, :]
    cos2 = bigF[0:64, 16:18, :]
    sinS = bigF[0:64, 18:20, :]
    rq, rkv = bigF[:, 20, :], bigF[:, 21, :]
    pch = P.chan("pos", dma=True)
    posi = bigI[0:64, 0:2, :]
    t = P.op("sync", lambda e: e.dma_start(out=posi, in_=C["pos"].rearrange("p (a t) -> p a t", a=2)), waits=[M.big_free] + M.store_toks, post=pch)
    posf, ang, tmpk, tmpf = bigF[0:64, 2:4, :], bigF[0:64, 4:6, :], bigI[0:64, 6:8, :], bigF[0:64, 8:10, :]
    d = P.DVE(lambda e: e.tensor_copy(posf, posi), waits=[t])
    d = P.DVE(lambda e: e.tensor_scalar(ang, posf, C["invf"][:, 0:1], 0.0, ALU.mult, ALU.add), waits=[d] + C["ctoks"])

    def reduce_sin(dst, shift, d):
        d = P.DVE(lambda e: e.tensor_scalar(dst, ang, 1.0, shift, ALU.mult, ALU.add), waits=[d])
        d = P.DVE(lambda e: e.tensor_scalar(tmpk, dst, 1.0 / (2 * PI), 0.0, ALU.mult, ALU.add), waits=[d])
        d = P.DVE(lambda e: e.tensor_copy(tmpf, tmpk), waits=[d])
        d = P.DVE(lambda e: e.scalar_tensor_tensor(out=dst, in0=tmpf, scalar=-2 * PI, in1=dst, op0=ALU.mult, op1=ALU.add), waits=[d])
        d = P.DVE(lambda e: e.tensor_scalar(tmpf, dst, PI, -2 * PI, ALU.is_gt, ALU.mult), waits=[d])
        d = P.DVE(lambda e: e.tensor_tensor(out=dst, in0=dst, in1=tmpf, op=ALU.add), waits=[d])
        d = P.DVE(lambda e: e.tensor_scalar(tmpf, dst, -PI, 2 * PI, ALU.is_lt, ALU.mult), waits=[d])
        d = P.DVE(lambda e: e.tensor_tensor(out=dst, in0=dst, in1=tmpf, op=ALU.add), waits=[d])
        d = P.DVE(lambda e: e.tensor_scalar(dst, dst, -3.14159, 3.14159, ALU.max, ALU.min), waits=[d])
        a = P.ACT(lambda e: e.activation(out=dst, in_=dst, func=AF.Sin), waits=[d])
        return a
    a = reduce_sin(cos2, PI / 2, d)
    a2 = reduce_sin(sinS, 0.0, a)
    tabtok = P.DVE(lambda e: e.tensor_scalar(sinS, sinS, C["sign"][:, 0:1], 0.0, ALU.mult, ALU.add), waits=[a2])
    M.big_free_extra = [tabtok]
    T = 1024
    for half in range(2):
        t0 = half * T
        acc, hT, htok = prologue(P, M, xin, t0, T, gcol, gtok)
        cosh, sinh = cos2[:, half, :], sinS[:, half, :]
        def l0(raw):
            return [(raw.reshape([128, 16, 512])[:, :, :], w_in_v[:, :, 0:512])]

        def l1(raw):
            return [(raw.reshape([128, 16, 512])[:, :, :], w_in_v[:, :, 512:1024])]

        def l2(raw):
            v = raw.reshape([128, 16, 512])
            return [(v[:, :, 0:64], w_in_v[:, :, 1024:1088]), (v[:, :, 64:96], w_in_v[:, :, 1056:1088]), (v[:, :, 96:128], w_in_v[:, :, 1024:1056])]
        vw = lambda raw: raw.reshape([128, 16, 512])
        groups = [(l0, vw, [(k * 128, 128, k) for k in range(4)]), (l1, vw, [(k * 128, 128, 4 + k) for k in range(4)]),
                  (l2, vw, [(0, 64, 8), (64, 64, 9)])]
        st = {}

        def evac_lat(k, tt, bank, tok):
            w = 128 if k < 8 else 64
            a = P.ACT(lambda e: e.activation(out=latT[0:w, k, tt * 512:(tt + 1) * 512], in_=bank[0:w, :], func=AF.Copy), waits=[tok, htok])
            st[(k, tt)] = a
            return a
        pe = stream_linear(P, M, groups, 16, hT, T, evac_lat, htok)
        M.hT_free = pe
        lntok = {}
        for name, base, gl, dstT, rs in (("q", 0, C["gq"], qlnT, rq), ("kv", 4, C["gkv"], kvlnT, rkv)):
            for tt in range(2):
                cs = slice(tt * 512, (tt + 1) * 512)
                b, btok = M.bank_acquire()
                for c in range(4):
                    s = c % 2
                    at = P.ACT(lambda e, c=c, s=s, cs=cs, base=base: e.activation(out=M.sq[s][:], in_=latT[:, base + c, cs], func=AF.Square),
                               waits=[st[(base + c, tt)], M.sqfree[s]])
                    pt = P.PE(lambda e, c=c, s=s, b=b: e.matmul(M.bank[b][:], M.ones[:], M.sq[s][:], start=(c == 0), stop=(c == 3)),
                              waits=[at, btok, M.ones_tok], post=True)
                    M.sqfree[s] = pt
                d1 = P.DVE(lambda e, b=b, cs=cs, rs=rs: e.tensor_scalar(rs[:, cs], M.bank[b][:], 1.0 / 512, EPS, ALU.mult, ALU.add), waits=[pt])
                M.bfree[b] = d1
                a2 = P.ACT(lambda e, cs=cs, rs=rs: e.activation(out=rs[:, cs], in_=rs[:, cs], func=AF.Sqrt), waits=[d1])
                d2 = P.DVE(lambda e, cs=cs, rs=rs: e.reciprocal(rs[:, cs], rs[:, cs]), waits=[a2])
                for c in range(4):
                    dl = P.DVE(lambda e, c=c, cs=cs, rs=rs, gl=gl, dstT=dstT, base=base: e.scalar_tensor_tensor(
                        out=dstT[:, c, cs], in0=latT[:, base + c, cs], scalar=gl[:, c:c + 1], in1=rs[:, cs], op0=ALU.mult, op1=ALU.mult),
                        waits=[d2] + C["ctoks"])
            lntok[name] = dl
        for tt in range(2):
            cs = slice(tt * 512, (tt + 1) * 512)
            d = P.DVE(lambda e, cs=cs, cosh=cosh: e.tensor_tensor(out=latT[0:64, 8, cs], in0=latT[0:64, 8, cs], in1=cosh[:, cs], op=ALU.mult),
                      waits=[st[(8, tt)], tabtok])
            d = P.DVE(lambda e, cs=cs, sinh=sinh: e.tensor_tensor(out=latT[0:64, 9, cs], in0=latT[0:64, 9, cs], in1=sinh[:, cs], op=ALU.mult),
                      waits=[st[(9, tt)], d])
            i, prev = stg_acquire(M)
            d = P.DVE(lambda e, cs=cs, i=i: e.tensor_tensor(out=M.stg[i][0:64, :], in0=latT[0:64, 8, cs], in1=latT[0:64, 9, cs], op=ALU.add), waits=[d, prev])
            P.op("sync", lambda e, t0=t0, tt=tt, i=i: e.dma_start(out=out["KrT"][:, t0 + tt * 512:t0 + (tt + 1) * 512], in_=M.stg[i][0:64, :]),
                 waits=[d], post=M.sts[i])
        groups = []
        for hg in range(4):
            h0 = hg * 4

            def lq(raw, h0=h0):
                v = raw.reshape([128, 4, 2048])
                L = [(v[:, :, 0:768], wq_v[:, :, h0 * 192:h0 * 192 + 768])]
                for k in range(4):
                    b0 = (h0 + k) * 192
                    L.append((v[:, :, 768 + k * 64:768 + k * 64 + 32], wq_v[:, :, b0 + 160:b0 + 192]))
                    L.append((v[:, :, 768 + k * 64 + 32:768 + k * 64 + 64], wq_v[:, :, b0 + 128:b0 + 160]))
                return L
            ch = []
            for k in range(4):
                ch += [(k * 192, 128, ("qn", h0 + k)), (k * 192 + 128, 64, ("qr", h0 + k)), (768 + k * 64, 64, ("qs", h0 + k))]
            groups.append((lq, lambda raw: raw.reshape([128, 4, 2048]), ch))
        pend = {}

        def evac_q(tag, tt, bank, tok):
            kind, h = tag
            cs = slice(tt * 512, (tt + 1) * 512)
            gs = slice(t0 + tt * 512, t0 + (tt + 1) * 512)
            if kind == "qn":
                i, prev = stg_acquire(M)
                a = P.ACT(lambda e: e.activation(out=M.stg[i][:], in_=bank[:], func=AF.Copy, scale=QS), waits=[tok, prev])
                P.op("sync", lambda e: e.dma_start(out=out["Qn"][h, :, gs], in_=M.stg[i][:]), waits=[a], post=M.sts[i])
                return a
            if kind == "qr":
                s = M.sgk % M.NSG
                M.sgk += 1
                a = P.ACT(lambda e: e.activation(out=M.sg[s][0:64, :], in_=bank[0:64, :], func=AF.Copy), waits=[tok, M.sgfree[s]])
                pend[(h, tt)] = (s, a)
                return a
            s, a = pend.pop((h, tt))
            csl, ssl = cosh[:, cs], sinh[:, cs]
            s2 = M.sgk % M.NSG
            M.sgk += 1
            d2 = P.DVE(lambda e: e.scalar_tensor_tensor(out=M.sg[s2][0:64, :], in0=bank[0:64, :], scalar=QS, in1=ssl, op0=ALU.mult, op1=ALU.mult),
                       waits=[tok, M.sgfree[s2], tabtok])
            d1 = P.DVE(lambda e: e.scalar_tensor_tensor(out=M.sg[s][0:64, :], in0=M.sg[s][0:64, :], scalar=QS, in1=csl, op0=ALU.mult, op1=ALU.mult),
                       waits=[a, d2])
            i, prev = stg_acquire(M)
            d3 = P.DVE(lambda e: e.tensor_tensor(out=M.stg[i][0:64, :], in0=M.sg[s][0:64, :], in1=M.sg[s2][0:64, :], op=ALU.add), waits=[d1, prev])
            M.sgfree[s] = d3
            M.sgfree[s2] = d3
            P.op("sync", lambda e: e.dma_start(out=out["Qr"][h, :, gs], in_=M.stg[i][0:64, :]), waits=[d3], post=M.sts[i])
            return d2
        stream_linear(P, M, groups, 4, qlnT, T, evac_q, lntok["q"])
        groups = []
        for hg in range(2):
            def lk(raw, hg=hg):
                return [(raw.reshape([128, 4, 2048])[:, :, :], wkv_v[:, :, hg * 2048:(hg + 1) * 2048])]
            groups.append((lk, lambda raw: raw.reshape([128, 4, 2048]), [(k * 256, 128, hg * 8 + k) for k in range(8)]))

        def evac_kn(h, tt, bank, tok):
            gs = slice(t0 + tt * 512, t0 + (tt + 1) * 512)
            i, prev = stg_acquire(M)
            a = P.ACT(lambda e: e.activation(out=M.stg[i][:], in_=bank[:], func=AF.Copy), waits=[tok, prev])
            P.op("sync", lambda e: e.dma_start(out=out["KnT_f"](h)[:, gs], in_=M.stg[i][:]), waits=[a], post=M.sts[i])
            return a
        stream_linear(P, M, groups, 4, kvlnT, T, evac_kn, lntok["kv"])
        lastpe = None
        for hg in range(4):
            h0 = hg * 4
            s, sfree = M.wslot_acquire()
            wv = M.wraw[s].reshape([128, 4, 2048])
            for k in range(4):
                ld = P.op("gpsimd", lambda e, k=k, wv=wv, h0=h0: e.dma_start(out=wv[:, :, k * 128:(k + 1) * 128],
                                                                           in_=wkv_v[:, :, (h0 + k) * 256 + 128:(h0 + k) * 256 + 256]),
                          waits=[sfree], post=M.wchan[s])
            for blk in range(T // 128):
                b, btok = M.bank_acquire()
                for kc in range(4):
                    tok = P.PE(lambda e, b=b, kc=kc, blk=blk, wv=wv: e.matmul(M.bank[b][:], kvlnT[:, kc, blk * 128:(blk + 1) * 128], wv[:, kc, 0:512],
                                                                              start=(kc == 0), stop=(kc == 3)),
                               waits=[ld, btok, lntok["kv"]], post=(kc == 3))
                i, prev = stg_acquire(M)
                a = P.ACT(lambda e, b=b, i=i: e.activation(out=M.stg[i][:], in_=M.bank[b][:], func=AF.Copy), waits=[tok, prev])
                M.bfree[b] = a
                bg = half * 8 + blk
                for jp in range(2):
                    P.op("sync", lambda e, i=i, h0=h0, bg=bg, jp=jp: e.dma_start(out=out["V_f"](h0 + 2 * jp)[:, :, bg, :].rearrange("h p d -> p h d"),
                                                                                 in_=M.stg[i].reshape([128, 4, 128])[:, 2 * jp:2 * jp + 2, :]),
                         waits=[a], post=M.sts[i])
                lastpe = tok
            M.wfree[s] = lastpe
        M.big_free = lastpe
        M.big_free_extra = [(P.dve, P.dve.n), (P.act, P.act.n)]
    M.store_toks = M.store_toks + [(M.sts[i], M.sts[i].n) for i in range(4) if M.sts[i].n] + [(pch, pch.n)]


def attn_setup(P, M):
    nc = P.nc
    if hasattr(M, "stg"):
        M.pT = M.stg[0:4]
    else:
        M.pT = [nc.alloc_sbuf_tensor(f"pT{i}", [128, 512], BF16) for i in range(4)]
    M.pTfree = [None] * 4
    M.pTk = 0


def mla_b_phase(P, M, xin, xout, w_out, Qn, Qr, Kf, KrT_all, Vf, qmask_dram, kmask_dram, keytile):
    w_out_v = w_out.rearrange("(c p) f -> p c f", p=128)
    xin_v = xin.rearrange("(c p) t -> p c t", p=128)
    xout_v = xout.rearrange("(c p) t -> p c t", p=128)
    bigB = M.big.reshape([128, 44, 1024])
    hB = M.hT.reshape([128, 32, 512])
    Qn_g, Qr_g = hB[:, 0:16, :], hB[:, 16:32, :]
    OT = bigB[:, 32:40, :].rearrange("p c (a t) -> p (c a) t", a=2)
    kset = [bigB[:, 16 * s:16 * s + 8, :] for s in range(2)]
    vset = [bigB[:, 16 * s + 8:16 * s + 16, :] for s in range(2)]
    kch = [P.chan(f"kv{s}", dma=True) for s in range(2)]
    kvfree = [None, None]
    qch = P.chan("qld", dma=True)
    cch = P.chan("acst", dma=True)
    M.NW = 2
    M.wk = 0
    kr3 = M.wraw[2].reshape([128, 4, 2048])
    qm = kr3[64:66, 0, :].rearrange("p (m t) -> p m t", m=4)
    kmask = kr3[64:66, 1, 0:128]
    w0 = [M.big_free] + M.store_toks + getattr(M, "big_free_extra", [])
    P.op("sync", lambda e: e.dma_start(out=kr3[0:64, :, :], in_=KrT_all.rearrange("r p t -> p r t")), waits=w0 + [M.wfree[2]], post=cch)
    cch2 = P.chan("acst2", dma=True)
    P.op("gpsimd", lambda e: e.dma_start(out=kr3[64:66, 0, :], in_=qmask_dram[:, :]), waits=[M.wfree[2]], post=cch2)
    P.op("gpsimd", lambda e: e.dma_start(out=kmask, in_=kmask_dram[:, :]), post=cch2)
    ctoks = [(cch, cch.n), (cch2, cch2.n)]
    kvk = 0
    qfree = None
    ot_free = None
    for g in range(4):
        P.op("sync", lambda e, g=g: e.dma_start(out=Qn_g, in_=Qn[:, :, g * 512:(g + 1) * 512].rearrange("h p t -> p h t")), waits=[qfree, M.hT_free] + w0, post=qch)
        P.op("sync", lambda e, g=g: e.dma_start(out=Qr_g[0:64], in_=Qr[:, :, g * 512:(g + 1) * 512].rearrange("h p t -> p h t")), waits=[qfree], post=qch)
        qtok = (qch, qch.n)
        nkb = 16 * g + 16
        for h in range(NH):
            s = kvk % 2
            kvk += 1
            P.op("sync", lambda e, s=s, h=h: e.dma_start(out=kset[s].rearrange("p (r a) t -> p r a t", a=2), in_=Kf(h).rearrange("r p (a t) -> p r a t", a=2)), waits=[kvfree[s]] + w0, post=kch[s])
            P.op("sync", lambda e, s=s, h=h: e.dma_start(out=vset[s].rearrange("p (r a) (b d) -> p r a b d", a=2, d=128), in_=Vf(h).rearrange("r p (a b) d -> p r a b d", a=2)), waits=[kvfree[s]], post=kch[s])
            kvtok = (kch[s], kch[s].n)
            K4 = kset[s].rearrange("p (r a) t -> p r a t", a=2)
            V5 = vset[s].rearrange("p (r a) (b d) -> p r a b d", a=2, d=128)
            pq = []
            lastpv = None

            def emit_pv(item, last, V5=V5, kvtok=kvtok):
                kb, c0, pslot, atok = item
                r, i = keytile(kb)
                t1 = P.PE(lambda e: e.matmul(M.bank[6][:, c0:512], V5[:, r, i // 8, i % 8, :], M.pT[pslot][:, c0:512], start=(kb == 0), stop=last),
                          waits=[atok, M.bfree[6], kvtok])
                t2 = P.PE(lambda e: e.matmul(M.bank[7][:, c0:512], M.ones[:], M.pT[pslot][:, c0:512], start=(kb == 0), stop=last),
                          waits=[M.bfree[7], M.ones_tok], post=True)
                M.pTfree[pslot] = t2
                return t2
            for kb in range(nkb):
                r, i = keytile(kb)
                rel = kb - 16 * g
                j0 = 0 if rel <= 3 else (rel - 3 + 3) // 4
                c0 = j0 * 128
                b, btok = M.bank_acquire()
                kcols = slice((i % 8) * 128, (i % 8) * 128 + 128)
                krcols = slice(i * 128, i * 128 + 128)
                maybe = rel >= 0
                P.PE(lambda e, b=b, c0=c0, r=r, i=i, kcols=kcols, h=h, K4=K4: e.matmul(M.bank[b][:, c0:512], K4[:, r, i // 8, kcols], Qn_g[:, h, c0:512], start=True, stop=False),
                     waits=[btok, kvtok, qtok] + ctoks)
                tok = P.PE(lambda e, b=b, c0=c0, r=r, krcols=krcols, h=h, maybe=maybe: e.matmul(M.bank[b][:, c0:512], kr3[0:64, r, krcols], Qr_g[0:64, h, c0:512],
                                                                                              start=False, stop=(not maybe)), post=True)
                if maybe:
                    jm, m = rel // 4, rel % 4
                    tok = P.PE(lambda e, b=b, jm=jm, m=m: e.matmul(M.bank[b][:, jm * 128:(jm + 1) * 128], kmask, qm[:, m, jm * 128:(jm + 1) * 128],
                                                                  start=False, stop=True), post=True)
                pslot = M.pTk % 4
                M.pTk += 1
                atok = P.ACT(lambda e, b=b, c0=c0, pslot=pslot: e.activation(out=M.pT[pslot][:, c0:512], in_=M.bank[b][:, c0:512], func=AF.Exp),
                             waits=[tok, M.pTfree[pslot]])
                M.bfree[b] = atok
                pq.append((kb, c0, pslot, atok))
                if len(pq) > 2:
                    emit_pv(pq.pop(0), False)
            while pq:
                lastpv = emit_pv(pq.pop(0), len(pq) == 0)
            kvfree[s] = lastpv
            sgi = M.sgk % M.NSG
            M.sgk += 1
            d = P.DVE(lambda e, sgi=sgi: e.reciprocal(M.sg[sgi][:], M.bank[7][:]), waits=[lastpv, M.sgfree[sgi]])
            M.bfree[7] = d
            d = P.DVE(lambda e, sgi=sgi, h=h: e.tensor_tensor(out=OT[:, h, :], in0=M.bank[6][:], in1=M.sg[sgi][:], op=ALU.mult), waits=[d, ot_free])
            M.bfree[6] = d
            M.sgfree[sgi] = d
        qfree = lastpv
        groups = []
        for dg in range(4):
            def lo(raw, dg=dg):
                return [(raw.reshape([128, 16, 512])[:, :, :], w_out_v[:, :, dg * 512:(dg + 1) * 512])]
            groups.append((lo, lambda raw: raw.reshape([128, 16, 512]), [(k * 128, 128, dg * 4 + k) for k in range(4)]))
        pre, evac = resid_evac(P, M, xin_v, xout_v, g * 512, 1.0)
        pe = stream_linear(P, M, groups, 16, OT, 512, evac, d, pre=pre)
        M.big_free = pe
        ot_free = pe
    M.hT_free = qfree
    M.big_free_extra = []
    M.NW = 3
    M.wk = 0
    M.wfree[2] = qfree


def _load_small(P, nc, specs):
    ch = P.chan("small", dma=True)
    ch2 = P.chan("small2", dma=True)
    T = {}
    for name, src, shape, dt, cast in specs:
        t = nc.alloc_sbuf_tensor(name + "_sb", shape, dt)
        T[name] = t
        if cast:
            P.op("gpsimd", lambda e, t=t, src=src: e.dma_start(out=t[:], in_=src), post=ch2)
        else:
            P.op("sync", lambda e, t=t, src=src: e.dma_start(out=t[:], in_=src), post=ch)
    toks = [(c, c.n) for c in (ch, ch2) if c.n]
    return T, toks


def build_program(mode, phases):
    nc = bass.Bass("TRN2", target_bir_lowering=False)
    I = lambda n, s, dt=F32: nc.dram_tensor(n, s, dt, kind="ExternalInput").ap()
    O = lambda n, s, dt=F32: nc.dram_tensor(n, s, dt, kind="ExternalOutput").ap()
    P = Prog(nc)
    M = Mem(P)
    setup_consts(P, M)
    xT = I("xT", [D, NTOK])
    gcd = I("gc", [128, 112])
    small = [("gc", gcd[:, :], [128, 112], F32, False)]
    cur = xT
    if mode == "A":
        xs = O("xT_mid", [D, NTOK])
        if "sgu" in phases:
            sd = {k: I(k, s) for k, s in (("gam", [128, 32]), ("bet", [128, 32]), ("wsT", [128, 1024]), ("ident", [128, 128]), ("bspB", [128, 4096]), ("zeros", [64, 512]))}
            small += [("gam", sd["gam"][:, :], [128, 32], F32, False), ("bet", sd["bet"][:, :], [128, 32], F32, False),
                      ("wsT", sd["wsT"][:, :], [128, 1024], BF16, True), ("ident", sd["ident"][:, :], [128, 128], BF16, True)]
        if "mlaA" in phases:
            md = {k: I(k, s) for k, s in (("gq", [128, 4]), ("gkv", [128, 4]), ("invf", [64, 1]), ("sign", [64, 1]))}
            pos = I("pos", [64, NTOK], I32)
            small += [(k, md[k][:, :], list(md[k].shape), F32, False) for k in md]
        S, stoks = _load_small(P, nc, small)
        gc = S["gc"]
        gtok = stoks[0]
        W = {}
        for name, shape in (("f1_in0", [D, 2 * DFF]), ("f1_out0", [DFF, D]), ("f2_in0", [D, 2 * DFF]), ("f2_out0", [DFF, D]),
                            ("f1_in1", [D, 2 * DFF]), ("f1_out1", [DFF, D]), ("sgu_w_in", [D, 8192]), ("sgu_w_out", [4096, D]),
                            ("mla_w_in", [D, 1088]), ("w_q_up", [512, 3072]), ("w_kv_up", [512, 4096])):
            need = {"f1_in0": "f10", "f1_out0": "f10", "f2_in0": "f20", "f2_out0": "f20", "f1_in1": "f11", "f1_out1": "f11",
                    "sgu_w_in": "sgu", "sgu_w_out": "sgu", "mla_w_in": "mlaA", "w_q_up": "mlaA", "w_kv_up": "mlaA"}[name]
            if need in phases:
                W[name] = I(name, shape)
        if "f10" in phases:
            ffn_phase(P, M, cur, xs, gc[:, 0:16], gtok, W["f1_in0"], W["f1_out0"])
            cur = xs
        if "sgu" in phases:
            w3 = S["wsT"].reshape([128, 8, 128])
            zch = P.chan("zmask", dma=True)
            t = P.op("gpsimd", lambda e: e.dma_start(out=w3[64:128, :, 0:64], in_=sd["zeros"].rearrange("p (g f) -> p g f", g=8)), waits=stoks, post=zch)
            C = {"gam": S["gam"], "bet": S["bet"], "wsT": w3, "ident": S["ident"], "bspB": sd["bspB"], "ctoks": stoks + [t],
                 "vtok": [nc.alloc_sbuf_tensor(f"vtok{i}", [128, 512], BF16) for i in range(2)]}
            sgu_phase(P, M, cur, xs, gc[:, 32:48], gtok, W["sgu_w_in"], W["sgu_w_out"], C)
            cur = xs
        if "f20" in phases:
            ffn_phase(P, M, cur, xs, gc[:, 64:80], gtok, W["f2_in0"], W["f2_out0"])
            cur = xs
        if "f11" in phases:
            ffn_phase(P, M, cur, xs, gc[:, 16:32], gtok, W["f1_in1"], W["f1_out1"])
            cur = xs
        if "mlaA" in phases:
            stg_setup(P, M)
            out = {"Qn": O("Qn", [NH, 128, NTOK], BF16), "Qr": O("Qr", [NH, 64, NTOK], BF16), "KnT": O("KnT", [NH, 128, NTOK], BF16),
                   "KrT": O("KrT", [64, NTOK], BF16), "V": O("V", [NH, 128, 16, 128], BF16)}
            out["KnT_f"] = lambda h: out["KnT"][h]
            out["V_f"] = lambda h: out["V"][h:h + 2]
            C = {"gq": S["gq"], "gkv": S["gkv"], "invf": S["invf"], "sign": S["sign"], "ctoks": stoks, "pos": pos}
            mla_a_phase(P, M, cur, gc[:, 48:64], gtok, W["mla_w_in"], W["w_q_up"], W["w_kv_up"], C, out)
        if cur is xT:
            cp = P.chan("cp", dma=True)
            P.op("sync", lambda e: e.dma_start(out=xs[:, :], in_=xT[:, :]), post=cp)
            M.store_toks = M.store_toks + [(cp, cp.n)]
        P.op("sync", lambda e: e.nop(), waits=M.store_toks)
    else:
        oT = O("oT", [D, NTOK])
        xs = nc.dram_tensor("xs_scratch", [D, NTOK], F32).ap()
        S, stoks = _load_small(P, nc, small)
        gc = S["gc"]
        gtok = stoks[0]
        if "mlaB" in phases:
            attn_setup(P, M)
            Qn, Qr = I("Qn", [NH, 128, NTOK], BF16), I("Qr", [NH, 64, NTOK], BF16)
            KnT_all, KrT_all = I("KnT_all", [4, NH, 128, NTOK], BF16), I("KrT_all", [4, 64, NTOK], BF16)
            V_all = I("V_all", [4, NH, 128, 16, 128], BF16)
            qmask, kmask = I("qmask", [2, 2048]), I("kmask", [2, 128])
            w_out = I("mla_w_out", [D, D])
            mla_b_phase(P, M, cur, xs, w_out, Qn, Qr, lambda h: KnT_all[:, h], KrT_all, lambda h: V_all[:, h], qmask, kmask, lambda kb: (kb // 16, kb % 16))
            cur = xs
        if "f21" in phases:
            w1, w2 = I("f2_in1", [D, 2 * DFF]), I("f2_out1", [DFF, D])
            ffn_phase(P, M, cur, xs, gc[:, 80:96], gtok, w1, w2)
            cur = xs
        final_norm_phase(P, M, cur, oT, gc[:, 96:112], gtok)
    P.emit()
    return nc


def _keytile_rankmajor(kb):
    c8, j2 = kb % 8, kb // 8
    return (c8, 2 * j2) if c8 < 4 else (7 - c8, 2 * j2 + 1)


def build_fused():
    nc = bass.Bass("TRN2", target_bir_lowering=False)
    I = lambda n, s, dt=F32: nc.dram_tensor(n, s, dt, kind="ExternalInput").ap()
    P = Prog(nc)
    M = Mem(P)
    setup_consts(P, M)
    xT = I("xT", [D, NTOK])
    oT = nc.dram_tensor("oT", [D, NTOK], F32, kind="ExternalOutput").ap()
    xs = nc.dram_tensor("xs_scratch", [D, NTOK], F32).ap()
    gcd = I("gc", [128, 112])
    sd = {k: I(k, s) for k, s in (("gam", [128, 32]), ("bet", [128, 32]), ("wsT", [128, 1024]), ("ident", [128, 128]), ("bspB", [128, 4096]), ("zeros", [64, 512]))}
    md = {k: I(k, s) for k, s in (("gq", [128, 4]), ("gkv", [128, 4]), ("invf", [64, 1]), ("sign", [64, 1]))}
    pos = I("pos", [64, NTOK], I32)
    qmask, kmask = I("qmask", [2, 2048]), I("kmask", [2, 128])
    small = [("gc", gcd[:, :], [128, 112], F32, False),
             ("gam", sd["gam"][:, :], [128, 32], F32, False), ("bet", sd["bet"][:, :], [128, 32], F32, False),
             ("wsT", sd["wsT"][:, :], [128, 1024], BF16, True), ("ident", sd["ident"][:, :], [128, 128], BF16, True)]
    small += [(k, md[k][:, :], list(md[k].shape), F32, False) for k in md]
    S, stoks = _load_small(P, nc, small)
    gc = S["gc"]
    gtok = stoks[0]
    W = {name: I(name, shape) for name, shape in (
        ("f1_in0", [D, 2 * DFF]), ("f1_out0", [DFF, D]), ("f2_in0", [D, 2 * DFF]), ("f2_out0", [DFF, D]),
        ("f1_in1", [D, 2 * DFF]), ("f1_out1", [DFF, D]), ("f2_in1", [D, 2 * DFF]), ("f2_out1", [DFF, D]),
        ("sgu_w_in", [D, 8192]), ("sgu_w_out", [4096, D]),
        ("mla_w_in", [D, 1088]), ("w_q_up", [512, 3072]), ("w_kv_up", [512, 4096]), ("mla_w_out", [D, D]))}
    ffn_phase(P, M, xT, xs, gc[:, 0:16], gtok, W["f1_in0"], W["f1_out0"])
    w3 = S["wsT"].reshape([128, 8, 128])
    zch = P.chan("zmask", dma=True)
    t = P.op("gpsimd", lambda e: e.dma_start(out=w3[64:128, :, 0:64], in_=sd["zeros"].rearrange("p (g f) -> p g f", g=8)), waits=stoks, post=zch)
    C = {"gam": S["gam"], "bet": S["bet"], "wsT": w3, "ident": S["ident"], "bspB": sd["bspB"], "ctoks": stoks + [t],
         "vtok": [nc.alloc_sbuf_tensor(f"vtok{i}", [128, 512], BF16) for i in range(2)]}
    sgu_phase(P, M, xs, xs, gc[:, 32:48], gtok, W["sgu_w_in"], W["sgu_w_out"], C)
    ffn_phase(P, M, xs, xs, gc[:, 64:80], gtok, W["f2_in0"], W["f2_out0"])
    ffn_phase(P, M, xs, xs, gc[:, 16:32], gtok, W["f1_in1"], W["f1_out1"])
    stg_setup(P, M)
    Qn = nc.dram_tensor("Qn_s", [NH, 128, NTOK], BF16).ap()
    Qr = nc.dram_tensor("Qr_s", [NH, 64, NTOK], BF16).ap()
    kn_own = [nc.dram_tensor(f"kn_own{k}", [256, NTOK], BF16).ap() for k in range(8)]
    v_own = [nc.dram_tensor(f"v_own{k}", [256, NTOK], BF16).ap() for k in range(8)]
    kr_own = nc.dram_tensor("kr_own", [64, NTOK], BF16).ap()
    kn_all = [nc.dram_tensor(f"kn_all{k}", [4 * 256, NTOK], BF16).ap() for k in range(8)]
    v_all = [nc.dram_tensor(f"v_all{k}", [4 * 256, NTOK], BF16).ap() for k in range(8)]
    kr_all = nc.dram_tensor("kr_all", [4 * 64, NTOK], BF16).ap()
    out = {"Qn": Qn, "Qr": Qr, "KrT": kr_own,
           "KnT_f": lambda h: kn_own[h // 2][(h % 2) * 128:(h % 2) * 128 + 128, :],
           "V_f": lambda h: v_own[h // 2].rearrange("(h p) (b d) -> h p b d", p=128, d=128)}
    CA = {"gq": S["gq"], "gkv": S["gkv"], "invf": S["invf"], "sign": S["sign"], "ctoks": stoks, "pos": pos}
    mla_a_phase(P, M, xs, gc[:, 48:64], gtok, W["mla_w_in"], W["w_q_up"], W["w_kv_up"], CA, out)
    cc = P.chan("cc")
    RG = [[0, 1, 2, 3], [4, 5, 6, 7]]
    for src, dst in [(kr_own, kr_all)] + list(zip(kn_own, kn_all)) + list(zip(v_own, v_all)):
        cctok = P.op("gpsimd", lambda e, src=src, dst=dst: e.collective_compute("AllGather", op=ALU.bypass, replica_groups=RG,
                                                                               ins=[src.opt()], outs=[dst.opt()]),
                     waits=M.store_toks, post=cc)
    M.store_toks = [cctok]
    Kf = lambda h: kn_all[h // 2].rearrange("(r q) t -> r q t", r=4)[:, (h % 2) * 128:(h % 2) * 128 + 128, :]
    Vf = lambda h: v_all[h // 2].rearrange("(r q) (b d) -> r q b d", r=4, d=128)[:, (h % 2) * 128:(h % 2) * 128 + 128]
    KrT_all = kr_all.rearrange("(r p) t -> r p t", r=4)
    attn_setup(P, M)
    mla_b_phase(P, M, xs, xs, W["mla_w_out"], Qn, Qr, Kf, KrT_all, Vf, qmask, kmask, _keytile_rankmajor)
    ffn_phase(P, M, xs, xs, gc[:, 80:96], gtok, W["f2_in1"], W["f2_out1"])
    final_norm_phase(P, M, xs, oT, gc[:, 96:112], gtok)
    P.emit()
    return nc


from concourse.bass_utils import run_bass_kernel_spmd
import ml_dtypes

_BF = ml_dtypes.bfloat16
SEQ = 8192
_FUSED = False


def _gblock(r, i):
    return 8 * (i // 2) + (r if i % 2 == 0 else 7 - r)


def _tok_index(r):
    return np.concatenate([np.arange(_gblock(r, i) * 128, _gblock(r, i) * 128 + 128) for i in range(16)])


def _col(v, n):
    return np.ascontiguousarray(np.asarray(v, np.float32).reshape(n, 128).T)


def _qmask(r):
    qm = np.zeros((2, 4, 4, 128), np.float32)
    for j in range(4):
        delta = r if j % 2 == 0 else 3 - r
        for m in range(4):
            if m > delta:
                qm[0, m, j, :] = NEG
            elif m == delta:
                qm[1, m, j, :64] = NEG
    return qm.reshape(2, 2048)


def kernel(x, positions, ln_ffn1, ffn1_w_in, ffn1_w_out, ln_mix, ln_ffn2, ffn2_w_in, ffn2_w_out,
           sgu_w_in, sgu_v_gain, sgu_v_bias, sgu_w_spatial, sgu_b_spatial, sgu_w_out,
           mla_w_in, mla_q_norm, mla_w_q_up, mla_kv_norm, mla_w_kv_up, mla_w_out, ln_final):
    f32 = lambda a: np.ascontiguousarray(np.asarray(a, np.float32))
    x = np.asarray(x, np.float32)
    positions = np.asarray(positions, np.int32)
    gc = np.concatenate([_col(ln_ffn1[0], 16), _col(ln_ffn1[1], 16), _col(ln_mix[0], 16), _col(ln_mix[1], 16),
                         _col(ln_ffn2[0], 16), _col(ln_ffn2[1], 16), _col(ln_final, 16)], axis=1)
    invf = (1.0 / (10000.0 ** (np.arange(32, dtype=np.float32) / 32))).astype(np.float32)
    common = {
        "gc": gc, "gam": _col(sgu_v_gain[0], 32), "bet": _col(sgu_v_bias[0], 32),
        "wsT": np.ascontiguousarray(np.asarray(sgu_w_spatial[0], np.float32).transpose(2, 0, 1).reshape(128, 1024)),
        "ident": np.eye(128, dtype=np.float32),
        "bspB": np.ascontiguousarray(np.broadcast_to(np.tile(np.asarray(sgu_b_spatial[0], np.float32), (1, 4)).reshape(1, 4096), (128, 4096))),
        "zeros": np.zeros((64, 512), np.float32),
        "gq": _col(mla_q_norm[0], 4), "gkv": _col(mla_kv_norm[0], 4),
        "invf": np.concatenate([invf, invf])[:, None].copy(),
        "sign": np.concatenate([-np.ones(32), np.ones(32)]).astype(np.float32)[:, None].copy(),
        "f1_in0": f32(ffn1_w_in[0]), "f1_out0": f32(ffn1_w_out[0]), "f2_in0": f32(ffn2_w_in[0]), "f2_out0": f32(ffn2_w_out[0]),
        "f1_in1": f32(ffn1_w_in[1]), "f1_out1": f32(ffn1_w_out[1]),
        "sgu_w_in": f32(sgu_w_in[0]), "sgu_w_out": f32(sgu_w_out[0]),
        "mla_w_in": f32(mla_w_in[0]), "w_q_up": f32(mla_w_q_up[0]), "w_kv_up": f32(mla_w_kv_up[0]),
    }
    idxs = [_tok_index(r) for r in range(4)]
    kmask = np.stack([np.ones(128), np.concatenate([np.zeros(64), np.ones(64)])]).astype(np.float32)
    common.update({"kmask": kmask, "mla_w_out": f32(mla_w_out[0]), "f2_in1": f32(ffn2_w_in[1]), "f2_out1": f32(ffn2_w_out[1])})
    maps = []
    for c in range(8):
        b, r = c // 4, c % 4
        m = dict(common)
        m["xT"] = np.ascontiguousarray(x[b, idxs[r]].T)
        m["pos"] = np.ascontiguousarray(np.broadcast_to(positions[b, idxs[r]][None], (64, 2048))).astype(np.int32)
        m["qmask"] = _qmask(r)
        maps.append(m)
    nc = build_fused()
    R = run_bass_kernel_spmd(nc, maps, core_ids=list(range(8))).results
    out = np.empty((2, SEQ, D), np.float32)
    for c in range(8):
        b, r = c // 4, c % 4
        out[b, idxs[r]] = R[c]["oT"].T
    return out
```

```python
import numpy as np
import concourse.bass as bass
import concourse.mybir as mybir

F32 = mybir.dt.float32
BF16 = mybir.dt.bfloat16
AF = mybir.ActivationFunctionType
ALU = mybir.AluOpType

D = 2048
DFF = 5632
NTOK = 2048
EPS = 1e-6
ENGS = ("sync", "scalar", "vector", "gpsimd", "tensor")


class Chan:
    def __init__(self, sem, inc):
        self.sem = sem
        self.inc = inc
        self.n = 0


class Prog:
    def __init__(self, nc):
        self.nc = nc
        self.q = {e: [] for e in ENGS}
        self.waited = {}
        self.nchan = 0
        self.pe = self.chan("pe")
        self.act = self.chan("act")
        self.dve = self.chan("dve")
        self.pool = self.chan("pool")

    def chan(self, name, dma=False):
        self.nchan += 1
        sem = self.nc.alloc_semaphore(name=f"{name}_{self.nchan}")
        return Chan(sem, 16 if dma else 1)

    def op(self, eng, fn, waits=(), post=None):
        ws = []
        for w in waits:
            if w is None:
                continue
            ch, val = w
            k = (eng, id(ch))
            if self.waited.get(k, 0) >= val:
                continue
            self.waited[k] = val
            ws.append((ch.sem, val))
        tok = None
        if post is not None:
            post.n += post.inc
            tok = (post, post.n)
        self.q[eng].append((fn, ws, post))
        return tok

    def PE(self, fn, waits=(), post=False):
        return self.op("tensor", fn, waits, self.pe if post else None)

    def ACT(self, fn, waits=()):
        return self.op("scalar", fn, waits, self.act)

    def DVE(self, fn, waits=()):
        return self.op("vector", fn, waits, self.dve)

    def emit(self):
        with self.nc.Block() as block:
            for eng in ENGS:
                ops = self.q[eng]

                def body(e, ops=ops):
                    for fn, ws, post in ops:
                        for sem, val in ws:
                            e.wait_ge(sem, val)
                        ins = fn(e)
                        if post is not None:
                            ins.then_inc(post.sem, post.inc)

                getattr(block, eng)(body)


class Mem:
    def __init__(self, P):
        nc = P.nc
        self.P = P
        A = nc.alloc_sbuf_tensor
        self.NW = 3
        self.wraw = [A(f"w{i}", [128, 8192], BF16) for i in range(self.NW)]
        self.wchan = [P.chan(f"wl{i}", dma=True) for i in range(self.NW)]
        self.wfree = [None] * self.NW
        self.wsave = [None] * self.NW
        self.wk = 0
        self.hT = A("hT", [128, 16 * 1024], BF16)
        self.big = A("big", [128, 44 * 1024], BF16)
        self.NX = 4
        self.xt = [A(f"xt{i}", [128, 512], F32) for i in range(self.NX)]
        self.xl = [P.chan(f"xl{i}", dma=True) for i in range(self.NX)]
        self.xs = [P.chan(f"xs{i}", dma=True) for i in range(self.NX)]
        self.xk = 0
        self.NSG = 4
        self.sg = [A(f"sg{i}", [128, 512], F32) for i in range(self.NSG)]
        self.sgfree = [None] * self.NSG
        self.sgk = 0
        self.sq = [A(f"sq{i}", [128, 512], BF16) for i in range(2)]
        self.sqfree = [None, None]
        self.rstd = A("rstd", [128, 1024], F32)
        self.ones = A("ones", [128, 128], BF16)
        self.accl = [P.chan(f"accl{i}", dma=True) for i in range(4)]
        self.NB = 5
        self.bank = [nc.alloc_psum_tensor(f"pb{i}", [128, 512], F32) for i in range(5)]
        self.bankT = nc.alloc_psum_tensor("pbT", [128, 1024], BF16)
        self.bankT_free = None
        self.bank += [None, nc.alloc_psum_tensor("pb6", [128, 512], F32), nc.alloc_psum_tensor("pb7", [128, 512], F32)]
        self.bfree = [None] * 8
        self.bk = 0
        self.big_free = None
        self.hT_free = None
        self.store_toks = []

    def bank_acquire(self):
        b = self.bk % self.NB
        self.bk += 1
        return b, self.bfree[b]

    def wslot_acquire(self):
        s = self.wk % self.NW
        self.wk += 1
        return s, self.wfree[s]


def setup_consts(P, M):
    P.op("gpsimd", lambda e: e.memset(M.ones[:], 1.0), post=P.pool)
    M.ones_tok = (P.pool, P.pool.n)


def prologue(P, M, xin, t0, T, gcol, gtok):
    acc = M.big.bitcast(F32).reshape([128, 22, 1024])
    xv = xin.rearrange("(c p) t -> p c t", p=128)
    hT = M.hT.reshape([128, 16, 1024])
    ltoks = []
    for k in range(4):
        waits = [M.big_free] + M.store_toks + getattr(M, "big_free_extra", [])
        tok = P.op("sync", lambda e, k=k: e.dma_start(out=acc[:, 4 * k:4 * k + 4, 0:T],
                                                      in_=xv[:, 4 * k:4 * k + 4, t0:t0 + T]),
                   waits=waits, post=M.accl[k])
        ltoks.append(tok)
    M.store_toks = []
    ntt = T // 512
    rtoks = []
    for tt in range(ntt):
        b, btok = M.bank_acquire()
        cs = slice(tt * 512, (tt + 1) * 512)
        for c in range(16):
            s = c % 2
            atok = P.ACT(lambda e, c=c, s=s, cs=cs: e.activation(out=M.sq[s][:], in_=acc[:, c, cs], func=AF.Square),
                         waits=[ltoks[c // 4], M.sqfree[s]])
            ptok = P.PE(lambda e, c=c, s=s, b=b: e.matmul(M.bank[b][:], M.ones[:], M.sq[s][:], start=(c == 0), stop=(c == 15)),
                        waits=[atok, btok, M.ones_tok], post=True)
            M.sqfree[s] = ptok
        d1 = P.DVE(lambda e, b=b, cs=cs: e.tensor_scalar(M.rstd[:, cs], M.bank[b][:], 1.0 / D, EPS, ALU.mult, ALU.add),
                   waits=[ptok, M.hT_free])
        M.bfree[b] = d1
        a2 = P.ACT(lambda e, cs=cs: e.activation(out=M.rstd[:, cs], in_=M.rstd[:, cs], func=AF.Sqrt), waits=[d1])
        d2 = P.DVE(lambda e, cs=cs: e.reciprocal(M.rstd[:, cs], M.rstd[:, cs]), waits=[a2])
        rtoks.append(d2)
    last = None
    for tt in range(ntt):
        cs = slice(tt * 512, (tt + 1) * 512)
        for c in range(16):
            last = P.DVE(lambda e, c=c, cs=cs: e.scalar_tensor_tensor(out=hT[:, c, cs], in0=acc[:, c, cs], scalar=gcol[:, c:c + 1],
                                                                      in1=M.rstd[:, cs], op0=ALU.mult, op1=ALU.mult),
                         waits=[rtoks[tt], gtok, M.hT_free])
    return acc, hT, last


def stream_linear(P, M, groups, KC, in_T, T, evac, in_tok, pre=None, kp=128):
    ntt = T // 512
    lastpe = None
    for grp in groups:
        loads_fn, view_fn, chunks = grp[0], grp[1], grp[2]
        ex = grp[3] if len(grp) > 3 else {}
        s, sfree = M.wslot_acquire()
        raw = M.wraw[s]
        for dst, src in loads_fn(raw):
            ld = P.op("gpsimd", lambda e, dst=dst, src=src: e.dma_start(out=dst, in_=src),
                      waits=[sfree, M.wsave[s]] + ex.get("waits", []), post=M.wchan[s])
        if "save" in ex:
            sdst, ssrc_fn, cb = ex["save"]
            ssrc = ssrc_fn(raw)
            M.wsave[s] = P.op("gpsimd", lambda e, sdst=sdst, ssrc=ssrc: e.dma_start(out=sdst, in_=ssrc), waits=[ld], post=M.wchan[s])
            cb(M.wsave[s])
        wv = view_fn(raw)
        for c0, width, tag in chunks:
            for tt in range(ntt):
                if pre is not None:
                    pre(tag, tt)
                b, btok = M.bank_acquire()
                for kc in range(KC):
                    lastk = kc == KC - 1
                    tok = P.PE(lambda e, b=b, kc=kc, c0=c0, width=width, tt=tt, wv=wv: e.matmul(
                        M.bank[b][0:width, :], wv[0:kp, kc, c0:c0 + width], in_T[0:kp, kc, tt * 512:(tt + 1) * 512],
                        start=(kc == 0), stop=(kc == KC - 1)),
                        waits=[ld, btok, in_tok], post=lastk)
                M.bfree[b] = evac(tag, tt, M.bank[b], tok)
                lastpe = tok
        M.wfree[s] = lastpe
    return lastpe


def ffn_phase(P, M, xin, xout, gcol, gtok, w_in, w_out):
    w_in_v = w_in.rearrange("(c p) f -> p c f", p=128)
    w_out_v = w_out.rearrange("(k p) d -> p k d", p=128)
    xin_v = xin.rearrange("(c p) t -> p c t", p=128)
    xout_v = xout.rearrange("(c p) t -> p c t", p=128)
    T = 1024
    actT = M.big.reshape([128, 44, 1024])
    for half in range(2):
        t0 = half * T
        acc, hT, htok = prologue(P, M, xin, t0, T, gcol, gtok)
        groups = []
        for F in range(22):
            f0 = F * 256

            def loads(raw, f0=f0):
                v = raw.reshape([128, 16, 512])
                return [(v[:, :, 0:256], w_in_v[:, :, f0:f0 + 256]),
                        (v[:, :, 256:512], w_in_v[:, :, DFF + f0:DFF + f0 + 256])]
            chunks = [(0, 128, ("g", 2 * F)), (256, 128, ("u", 2 * F)),
                      (128, 128, ("g", 2 * F + 1)), (384, 128, ("u", 2 * F + 1))]
            groups.append((loads, lambda raw: raw.reshape([128, 16, 512]), chunks))
        pend = {}
        st = {"last": None}

        def evac1(tag, tt, bank, tok):
            kind, j = tag
            if kind == "g":
                s = M.sgk % M.NSG
                M.sgk += 1
                a = P.ACT(lambda e, s=s, bank=bank: e.activation(out=M.sg[s][:], in_=bank[:], func=AF.Silu),
                          waits=[tok, M.sgfree[s]])
                pend[(j, tt)] = (s, a)
                return a
            s, a = pend.pop((j, tt))
            d = P.DVE(lambda e, s=s, bank=bank, j=j, tt=tt: e.tensor_tensor(out=actT[:, j, tt * 512:(tt + 1) * 512], in0=M.sg[s][:],
                                                                            in1=bank[:], op=ALU.mult),
                      waits=[tok, a, htok])
            M.sgfree[s] = d
            st["last"] = d
            return d
        pe1 = stream_linear(P, M, groups, 16, hT, T, evac1, htok)
        M.hT_free = pe1
        act_tok = st["last"]
        groups2 = []
        for dc in range(16):
            def loads2(raw, dc=dc):
                v = raw.reshape([128, 64, 128])
                return [(v[:, 0:44, :], w_out_v[:, :, dc * 128:(dc + 1) * 128])]
            groups2.append((loads2, lambda raw: raw.reshape([128, 64, 128]), [(0, 128, dc)]))
        xsl = {}

        def pre2(dc, tt):
            i = M.xk % M.NX
            M.xk += 1
            prev_store = (M.xs[i], M.xs[i].n) if M.xs[i].n else None
            l = P.op("sync", lambda e, i=i, dc=dc, tt=tt, t0=t0: e.dma_start(out=M.xt[i][:], in_=xin_v[:, dc, t0 + tt * 512:t0 + (tt + 1) * 512]),
                     waits=[prev_store], post=M.xl[i])
            xsl[(dc, tt)] = (i, l)

        def evac2(dc, tt, bank, tok):
            i, l = xsl.pop((dc, tt))
            d = P.DVE(lambda e, i=i, bank=bank: e.scalar_tensor_tensor(out=M.xt[i][:], in0=bank[:], scalar=0.5, in1=M.xt[i][:],
                                                                       op0=ALU.mult, op1=ALU.add),
                      waits=[tok, l])
            stok = P.op("sync", lambda e, i=i, dc=dc, tt=tt, t0=t0: e.dma_start(out=xout_v[:, dc, t0 + tt * 512:t0 + (tt + 1) * 512], in_=M.xt[i][:]),
                        waits=[d], post=M.xs[i])
            M.store_toks = [(M.xs[k], M.xs[k].n) for k in range(M.NX) if M.xs[k].n]
            return d
        pe2 = stream_linear(P, M, groups2, 44, actT, T, evac2, act_tok, pre=pre2)
        M.big_free = pe2


def final_norm_phase(P, M, xin, out, gcol, gtok):
    T = 1024
    out_v = out.rearrange("(c p) t -> p c t", p=128)
    ost = P.chan("ost", dma=True)
    for half in range(2):
        t0 = half * T
        acc = M.big.bitcast(F32).reshape([128, 22, 1024])
        xv = xin.rearrange("(c p) t -> p c t", p=128)
        ltoks = []
        for k in range(4):
            waits = [M.big_free] + M.store_toks + ([(ost, ost.n)] if ost.n else [])
            ltoks.append(P.op("sync", lambda e, k=k, t0=t0: e.dma_start(out=acc[:, 4 * k:4 * k + 4, 0:T], in_=xv[:, 4 * k:4 * k + 4, t0:t0 + T]),
                              waits=waits, post=M.accl[k]))
        M.store_toks = []
        rtoks = []
        for tt in range(2):
            b, btok = M.bank_acquire()
            cs = slice(tt * 512, (tt + 1) * 512)
            for c in range(16):
                s = c % 2
                atok = P.ACT(lambda e, c=c, s=s, cs=cs: e.activation(out=M.sq[s][:], in_=acc[:, c, cs], func=AF.Square),
                             waits=[ltoks[c // 4], M.sqfree[s]])
                ptok = P.PE(lambda e, c=c, s=s, b=b: e.matmul(M.bank[b][:], M.ones[:], M.sq[s][:], start=(c == 0), stop=(c == 15)),
                            waits=[atok, btok, M.ones_tok], post=True)
                M.sqfree[s] = ptok
            d1 = P.DVE(lambda e, b=b, cs=cs: e.tensor_scalar(M.rstd[:, cs], M.bank[b][:], 1.0 / D, EPS, ALU.mult, ALU.add), waits=[ptok])
            M.bfree[b] = d1
            a2 = P.ACT(lambda e, cs=cs: e.activation(out=M.rstd[:, cs], in_=M.rstd[:, cs], func=AF.Sqrt), waits=[d1])
            rtoks.append(P.DVE(lambda e, cs=cs: e.reciprocal(M.rstd[:, cs], M.rstd[:, cs]), waits=[a2]))
        for k in range(4):
            for c in range(4 * k, 4 * k + 4):
                for tt in range(2):
                    cs = slice(tt * 512, (tt + 1) * 512)
                    last = P.DVE(lambda e, c=c, cs=cs: e.scalar_tensor_tensor(out=acc[:, c, cs], in0=acc[:, c, cs], scalar=gcol[:, c:c + 1],
                                                                              in1=M.rstd[:, cs], op0=ALU.mult, op1=ALU.mult),
                                 waits=[rtoks[tt], gtok, P.act and (P.act, P.act.n)])
            P.op("sync", lambda e, k=k, t0=t0: e.dma_start(out=out_v[:, 4 * k:4 * k + 4, t0:t0 + T], in_=acc[:, 4 * k:4 * k + 4, 0:T]),
                 waits=[last], post=ost)
        M.big_free = None
    P.op("sync", lambda e: e.wait_ge(ost.sem, ost.n), waits=[])


def resid_evac(P, M, xin_v, xout_v, t0, scale, first_wait=None):
    xsl = {}

    def pre(dc, tt):
        i = M.xk % M.NX
        M.xk += 1
        prev_store = (M.xs[i], M.xs[i].n) if M.xs[i].n else None
        l = P.op("sync", lambda e, i=i, dc=dc, tt=tt: e.dma_start(out=M.xt[i][:], in_=xin_v[:, dc, t0 + tt * 512:t0 + (tt + 1) * 512]),
                 waits=[prev_store, first_wait], post=M.xl[i])
        xsl[(dc, tt)] = (i, l)

    def evac(dc, tt, bank, tok):
        i, l = xsl.pop((dc, tt))
        d = P.DVE(lambda e, i=i, bank=bank: e.scalar_tensor_tensor(out=M.xt[i][:], in0=bank[:], scalar=scale, in1=M.xt[i][:],
                                                                   op0=ALU.mult, op1=ALU.add), waits=[tok, l])
        P.op("sync", lambda e, i=i, dc=dc, tt=tt: e.dma_start(out=xout_v[:, dc, t0 + tt * 512:t0 + (tt + 1) * 512], in_=M.xt[i][:]),
             waits=[d], post=M.xs[i])
        M.store_toks = [(M.xs[k], M.xs[k].n) for k in range(M.NX) if M.xs[k].n]
        return d
    return pre, evac


def sgu_phase(P, M, xin, xout, gcol, gtok, w_in, w_out, C):
    nc = P.nc
    w_in_v = w_in.rearrange("(c p) f -> p c f", p=128)
    w_out_v = w_out.rearrange("(k p) d -> p k d", p=128)
    xin_v = xin.rearrange("(c p) t -> p c t", p=128)
    xout_v = xout.rearrange("(c p) t -> p c t", p=128)
    T = 512
    bigF = M.big.bitcast(F32).reshape([128, 44, 512])
    vT = bigF
    gT = M.big.reshape([128, 44, 1024])
    bsp = bigF[:, 32:40, :]
    mu, rv, msq = bigF[:, 40, :], bigF[:, 41, :], bigF[:, 42, :]
    bch = P.chan("bsp", dma=True)
    btok = P.op("sync", lambda e: e.dma_start(out=bsp, in_=C["bspB"].rearrange("p (g f) -> p g f", g=8)), waits=[M.big_free], post=bch)
    GA, GB = 0.044715, 1.5957691216057308
    SGU_DEPTH = 2

    def gelu_head(bank, tok):
        s_ = M.sgk % M.NSG
        M.sgk += 1
        sg = M.sg[s_]
        a1 = P.ACT(lambda e: e.activation(out=sg[:], in_=bank[:], func=AF.Square), waits=[tok, M.sgfree[s_]])
        d1 = P.DVE(lambda e: e.tensor_scalar(sg[:], sg[:], GA, 1.0, ALU.mult, ALU.add), waits=[a1])
        d2 = P.DVE(lambda e: e.tensor_tensor(out=sg[:], in0=sg[:], in1=bank[:], op=ALU.mult), waits=[d1])
        a2 = P.ACT(lambda e: e.activation(out=sg[:], in_=sg[:], func=AF.Sigmoid, scale=GB), waits=[d2])
        return s_, a2

    def gelu_fin(bank, s_, a2, out_ap, extra):
        sg = M.sg[s_]
        d3 = P.DVE(lambda e: e.tensor_tensor(out=out_ap, in0=sg[:], in1=bank[:], op=ALU.mult), waits=[a2] + extra)
        M.sgfree[s_] = d3
        M.bfree[[i for i, t in enumerate(M.bank) if t is bank][0]] = d3
        return d3

    class Pipe:
        def __init__(self):
            self.active = []

        def push(self, gen):
            for g_ in list(self.active):
                try:
                    next(g_)
                except StopIteration:
                    self.active.remove(g_)
            try:
                next(gen)
                self.active.append(gen)
            except StopIteration:
                pass

        def drain(self):
            while self.active:
                for g_ in list(self.active):
                    try:
                        next(g_)
                    except StopIteration:
                        self.active.remove(g_)

    scr = {}
    scr_tok = {}

    def _sgu_group(q, key, src_view, vshape, kc, chunks):
        vw = lambda raw: raw.reshape(vshape)
        if q == 0:
            scr[key] = nc.dram_tensor("sguscr_%s_%d" % key, [128, kc, vshape[2]], BF16).ap()

            def loads(raw):
                return [(raw.reshape(vshape)[:, 0:kc, :], src_view)]

            def cb(tok, key=key):
                scr_tok[key] = tok
            return (loads, vw, chunks, {"save": (scr[key], lambda raw: raw.reshape(vshape)[:, 0:kc, :], cb)})

        def loads2(raw):
            return [(raw.reshape(vshape)[:, 0:kc, :], scr[key])]
        return (loads2, vw, chunks, {"waits": [scr_tok[key]]})

    for q in range(4):
        t0 = q * T
        acc, hT, htok = prologue(P, M, xin, t0, T, gcol, gtok)
        groups = []
        for g in range(8):
            groups.append(_sgu_group(q, ("v", g), w_in_v[:, :, 4096 + g * 512:4096 + (g + 1) * 512], [128, 16, 512], 16,
                                     [(k * 128, 128, 4 * g + k) for k in range(4)]))
        st = {}

        pipe = Pipe()

        def work_v(c, bank, tok):
            s_, a2 = gelu_head(bank, tok)
            yield
            d3 = gelu_fin(bank, s_, a2, vT[:, c, :], [htok, btok])
            yield
            for which, fn in ((6, AF.Copy), (7, AF.Square)):
                k = (2 * c + which) % 2
                a = P.ACT(lambda e, k=k, fn=fn, c=c: e.activation(out=M.sq[k][:], in_=vT[:, c, :], func=fn), waits=[d3, M.sqfree[k]])
                p = P.PE(lambda e, k=k, which=which, c=c: e.matmul(M.bank[which][:], M.ones[:], M.sq[k][:], start=(c == 0), stop=(c == 31)),
                         waits=[a, M.bfree[which], M.ones_tok], post=True)
                M.sqfree[k] = p
            st["p"] = p

        def evac_v(c, tt, bank, tok):
            pipe.push(work_v(c, bank, tok))
            return M.bfree[[i for i, t in enumerate(M.bank) if t is bank][0]]
        pe1 = stream_linear(P, M, groups, 16, hT, T, evac_v, htok)
        pipe.drain()
        d = P.DVE(lambda e: e.tensor_scalar(mu, M.bank[6][:], 1.0 / 4096, 0.0, ALU.mult, ALU.add), waits=[st["p"]])
        d = P.DVE(lambda e: e.tensor_tensor(out=msq, in0=mu, in1=mu, op=ALU.mult), waits=[d])
        d = P.DVE(lambda e: e.scalar_tensor_tensor(out=rv, in0=M.bank[7][:], scalar=1.0 / 4096, in1=msq, op0=ALU.mult, op1=ALU.subtract), waits=[d])
        M.bfree[6] = d
        M.bfree[7] = d
        d = P.DVE(lambda e: e.tensor_scalar(rv, rv, 1.0, EPS, ALU.mult, ALU.add), waits=[d])
        a = P.ACT(lambda e: e.activation(out=rv, in_=rv, func=AF.Sqrt), waits=[d])
        stat_tok = P.DVE(lambda e: e.reciprocal(rv, rv), waits=[a])
        groups = []
        for g in range(8):
            groups.append(_sgu_group(q, ("u", g), w_in_v[:, :, g * 512:(g + 1) * 512], [128, 16, 512], 16,
                                     [(k * 128, 128, 4 * g + k) for k in range(4)]))
        st3 = {}
        pipe3 = Pipe()

        def work_u(c, bank, tok):
            g = c // 4
            i = M.xk % M.NX
            M.xk += 1
            prev_store = (M.xs[i], M.xs[i].n) if M.xs[i].n else None
            ut = M.xt[i]
            s_, a2 = gelu_head(bank, tok)
            d = P.DVE(lambda e: e.tensor_tensor(out=vT[:, c, :], in0=vT[:, c, :], in1=mu, op=ALU.subtract), waits=[stat_tok])
            d = P.DVE(lambda e: e.tensor_tensor(out=vT[:, c, :], in0=vT[:, c, :], in1=rv, op=ALU.mult), waits=[d])
            k = c % 2
            d = P.DVE(lambda e: e.tensor_scalar(M.sq[k][:], vT[:, c, :], C["gam"][:, c:c + 1], C["bet"][:, c:c + 1], ALU.mult, ALU.add),
                      waits=[d, M.sqfree[k], *C["ctoks"]])
            for blk in range(4):
                p = P.PE(lambda e, blk=blk: e.transpose(M.bankT[:, blk * 128:(blk + 1) * 128], M.sq[k][:, blk * 128:(blk + 1) * 128], C["ident"][:]),
                         waits=[d, M.bankT_free, *C["ctoks"]], post=(blk == 3))
            M.sqfree[k] = p
            yield
            du = gelu_fin(bank, s_, a2, ut[:], [prev_store, st3.get(("utfree", i))])
            vtok = C["vtok"][c % 2]
            dcp = P.DVE(lambda e: e.tensor_copy(vtok[:], M.bankT[:, 0:512]), waits=[p, st3.get(("vtokfree", c % 2))])
            M.bankT_free = dcp
            b, btk = M.bank_acquire()
            for blk in range(4):
                p2 = P.PE(lambda e, blk=blk: e.matmul(M.bank[b][:, blk * 128:(blk + 1) * 128], vtok[:, blk * 128:(blk + 1) * 128],
                                                      C["wsT"][:, g, :], start=True, stop=True),
                          waits=[dcp, btk, *C["ctoks"]], post=(blk == 3))
            st3[("vtokfree", c % 2)] = p2
            yield
            s2 = M.sgk % M.NSG
            M.sgk += 1
            sg = M.sg[s2]
            d = P.DVE(lambda e: e.tensor_tensor(out=sg[:], in0=M.bank[b][:], in1=bsp[:, g, :], op=ALU.add), waits=[p2, M.sgfree[s2], btok])
            M.bfree[b] = d
            d = P.DVE(lambda e: e.tensor_tensor(out=gT[:, c, 0:512], in0=sg[:], in1=ut[:], op=ALU.mult), waits=[d, du])
            M.sgfree[s2] = d
            st3[("utfree", i)] = d
            st3["last"] = d

        def evac_u(c, tt, bank, tok):
            pipe3.push(work_u(c, bank, tok))
            return M.bfree[[i for i, t in enumerate(M.bank) if t is bank][0]]
        pe3 = stream_linear(P, M, groups, 16, hT, T, evac_u, htok)
        pipe3.drain()
        pe3 = (P.pe, P.pe.n)
        M.hT_free = pe3
        groups = []
        for dc in range(16):
            groups.append(_sgu_group(q, ("o", dc), w_out_v[:, :, dc * 128:(dc + 1) * 128], [128, 64, 128], 32, [(0, 128, dc)]))
        pre, evac = resid_evac(P, M, xin_v, xout_v, t0, 1.0, first_wait=st3["last"])
        pe4 = stream_linear(P, M, groups, 32, gT, T, evac, st3["last"], pre=pre)
        M.big_free = pe4


I32 = mybir.dt.int32
NH = 16
QS = 192.0 ** -0.5
PI = float(np.pi)
NEG = -30000.0


def stg_setup(P, M):
    nc = P.nc
    M.stg = [nc.alloc_sbuf_tensor(f"stg{i}", [128, 512], BF16) for i in range(4)]
    M.sts = [P.chan(f"sts{i}", dma=True) for i in range(4)]
    M.stk = 0


def stg_acquire(M):
    i = M.stk % 4
    M.stk += 1
    prev = (M.sts[i], M.sts[i].n) if M.sts[i].n else None
    return i, prev


def mla_a_phase(P, M, xin, gcol, gtok, w_in, w_q_up, w_kv_up, C, out):
    w_in_v = w_in.rearrange("(c p) f -> p c f", p=128)
    wq_v = w_q_up.rearrange("(c p) f -> p c f", p=128)
    wkv_v = w_kv_up.rearrange("(c p) f -> p c f", p=128)
    bigF = M.big.bitcast(F32).reshape([128, 22, 1024])
    bigI = M.big.bitcast(I32).reshape([128, 22, 1024])
    bigB = M.big.reshape([128, 44, 1024])
    latT = bigF
    qlnT = bigB[:, 20:24, :]
    kvlnT = bigB[:, 24:28, :]
    cos2 = bigF[0:64, 16:18, :]
    sinS = bigF[0:64, 18:20, :]
    rq, rkv = bigF[:, 20, :], bigF[:, 21, :]
    pch = P.chan("pos", dma=True)
    posi = bigI[0:64, 0:2, :]
    t = P.op("sync", lambda e: e.dma_start(out=posi, in_=C["pos"].rearrange("p (a t) -> p a t", a=2)), waits=[M.big_free] + M.store_toks, post=pch)
    posf, ang, tmpk, tmpf = bigF[0:64, 2:4, :], bigF[0:64, 4:6, :], bigI[0:64, 6:8, :], bigF[0:64, 8:10, :]
    d = P.DVE(lambda e: e.tensor_copy(posf, posi), waits=[t])
    d = P.DVE(lambda e: e.tensor_scalar(ang, posf, C["invf"][:, 0:1], 0.0, ALU.mult, ALU.add), waits=[d] + C["ctoks"])

    def reduce_sin(dst, shift, d):
        d = P.DVE(lambda e: e.tensor_scalar(dst, ang, 1.0, shift, ALU.mult, ALU.add), waits=[d])
        d = P.DVE(lambda e: e.tensor_scalar(tmpk, dst, 1.0 / (2 * PI), 0.0, ALU.mult, ALU.add), waits=[d])
        d = P.DVE(lambda e: e.tensor_copy(tmpf, tmpk), waits=[d])
        d = P.DVE(lambda e: e.scalar_tensor_tensor(out=dst, in0=tmpf, scalar=-2 * PI, in1=dst, op0=ALU.mult, op1=ALU.add), waits=[d])
        d = P.DVE(lambda e: e.tensor_scalar(tmpf, dst, PI, -2 * PI, ALU.is_gt, ALU.mult), waits=[d])
        d = P.DVE(lambda e: e.tensor_tensor(out=dst, in0=dst, in1=tmpf, op=ALU.add), waits=[d])
        d = P.DVE(lambda e: e.tensor_scalar(tmpf, dst, -PI, 2 * PI, ALU.is_lt, ALU.mult), waits=[d])
        d = P.DVE(lambda e: e.tensor_tensor(out=dst, in0=dst, in1=tmpf, op=ALU.add), waits=[d])
        d = P.DVE(lambda e: e.tensor_scalar(dst, dst, -3.14159, 3.14159, ALU.max, ALU.min), waits=[d])
        a = P.ACT(lambda e: e.activation(out=dst, in_=dst, func=AF.Sin), waits=[d])
        return a
    a = reduce_sin(cos2, PI / 2, d)
    a2 = reduce_sin(sinS, 0.0, a)
    tabtok = P.DVE(lambda e: e.tensor_scalar(sinS, sinS, C["sign"][:, 0:1], 0.0, ALU.mult, ALU.add), waits=[a2])
    M.big_free_extra = [tabtok]
    T = 1024
    for half in range(2):
        t0 = half * T
        acc, hT, htok = prologue(P, M, xin, t0, T, gcol, gtok)
        cosh, sinh = cos2[:, half, :], sinS[:, half, :]
        def l0(raw):
            return [(raw.reshape([128, 16, 512])[:, :, :], w_in_v[:, :, 0:512])]

        def l1(raw):
            return [(raw.reshape([128, 16, 512])[:, :, :], w_in_v[:, :, 512:1024])]

        def l2(raw):
            v = raw.reshape([128, 16, 512])
            return [(v[:, :, 0:64], w_in_v[:, :, 1024:1088]), (v[:, :, 64:96], w_in_v[:, :, 1056:1088]), (v[:, :, 96:128], w_in_v[:, :, 1024:1056])]
        vw = lambda raw: raw.reshape([128, 16, 512])
        groups = [(l0, vw, [(k * 128, 128, k) for k in range(4)]), (l1, vw, [(k * 128, 128, 4 + k) for k in range(4)]),
                  (l2, vw, [(0, 64, 8), (64, 64, 9)])]
        st = {}

        def evac_lat(k, tt, bank, tok):
            w = 128 if k < 8 else 64
            a = P.ACT(lambda e: e.activation(out=latT[0:w, k, tt * 512:(tt + 1) * 512], in_=bank[0:w, :], func=AF.Copy), waits=[tok, htok])
            st[(k, tt)] = a
            return a
        pe = stream_linear(P, M, groups, 16, hT, T, evac_lat, htok)
        M.hT_free = pe
        lntok = {}
        for name, base, gl, dstT, rs in (("q", 0, C["gq"], qlnT, rq), ("kv", 4, C["gkv"], kvlnT, rkv)):
            for tt in range(2):
                cs = slice(tt * 512, (tt + 1) * 512)
                b, btok = M.bank_acquire()
                for c in range(4):
                    s = c % 2
                    at = P.ACT(lambda e, c=c, s=s, cs=cs, base=base: e.activation(out=M.sq[s][:], in_=latT[:, base + c, cs], func=AF.Square),
                               waits=[st[(base + c, tt)], M.sqfree[s]])
                    pt = P.PE(lambda e, c=c, s=s, b=b: e.matmul(M.bank[b][:], M.ones[:], M.sq[s][:], start=(c == 0), stop=(c == 3)),
                              waits=[at, btok, M.ones_tok], post=True)
                    M.sqfree[s] = pt
                d1 = P.DVE(lambda e, b=b, cs=cs, rs=rs: e.tensor_scalar(rs[:, cs], M.bank[b][:], 1.0 / 512, EPS, ALU.mult, ALU.add), waits=[pt])
                M.bfree[b] = d1
                a2 = P.ACT(lambda e, cs=cs, rs=rs: e.activation(out=rs[:, cs], in_=rs[:, cs], func=AF.Sqrt), waits=[d1])
                d2 = P.DVE(lambda e, cs=cs, rs=rs: e.reciprocal(rs[:, cs], rs[:, cs]), waits=[a2])
                for c in range(4):
                    dl = P.DVE(lambda e, c=c, cs=cs, rs=rs, gl=gl, dstT=dstT, base=base: e.scalar_tensor_tensor(
                        out=dstT[:, c, cs], in0=latT[:, base + c, cs], scalar=gl[:, c:c + 1], in1=rs[:, cs], op0=ALU.mult, op1=ALU.mult),
                        waits=[d2] + C["ctoks"])
            lntok[name] = dl
        for tt in range(2):
            cs = slice(tt * 512, (tt + 1) * 512)
            d = P.DVE(lambda e, cs=cs, cosh=cosh: e.tensor_tensor(out=latT[0:64, 8, cs], in0=latT[0:64, 8, cs], in1=cosh[:, cs], op=ALU.mult),
                      waits=[st[(8, tt)], tabtok])
            d = P.DVE(lambda e, cs=cs, sinh=sinh: e.tensor_tensor(out=latT[0:64, 9, cs], in0=latT[0:64, 9, cs], in1=sinh[:, cs], op=ALU.mult),
                      waits=[st[(9, tt)], d])
            i, prev = stg_acquire(M)
            d = P.DVE(lambda e, cs=cs, i=i: e.tensor_tensor(out=M.stg[i][0:64, :], in0=latT[0:64, 8, cs], in1=latT[0:64, 9, cs], op=ALU.add), waits=[d, prev])
            P.op("sync", lambda e, t0=t0, tt=tt, i=i: e.dma_start(out=out["KrT"][:, t0 + tt * 512:t0 + (tt + 1) * 512], in_=M.stg[i][0:64, :]),
                 waits=[d], post=M.sts[i])
        groups = []
        for hg in range(4):
            h0 = hg * 4

            def lq(raw, h0=h0):
                v = raw.reshape([128, 4, 2048])
                L = [(v[:, :, 0:768], wq_v[:, :, h0 * 192:h0 * 192 + 768])]
                for k in range(4):
                    b0 = (h0 + k) * 192
                    L.append((v[:, :, 768 + k * 64:768 + k * 64 + 32], wq_v[:, :, b0 + 160:b0 + 192]))
                    L.append((v[:, :, 768 + k * 64 + 32:768 + k * 64 + 64], wq_v[:, :, b0 + 128:b0 + 160]))
                return L
            ch = []
            for k in range(4):
                ch += [(k * 192, 128, ("qn", h0 + k)), (k * 192 + 128, 64, ("qr", h0 + k)), (768 + k * 64, 64, ("qs", h0 + k))]
            groups.append((lq, lambda raw: raw.reshape([128, 4, 2048]), ch))
        pend = {}

        def evac_q(tag, tt, bank, tok):
            kind, h = tag
            cs = slice(tt * 512, (tt + 1) * 512)
            gs = slice(t0 + tt * 512, t0 + (tt + 1) * 512)
            if kind == "qn":
                i, prev = stg_acquire(M)
                a = P.ACT(lambda e: e.activation(out=M.stg[i][:], in_=bank[:], func=AF.Copy, scale=QS), waits=[tok, prev])
                P.op("sync", lambda e: e.dma_start(out=out["Qn"][h, :, gs], in_=M.stg[i][:]), waits=[a], post=M.sts[i])
                return a
            if kind == "qr":
                s = M.sgk % M.NSG
                M.sgk += 1
                a = P.ACT(lambda e: e.activation(out=M.sg[s][0:64, :], in_=bank[0:64, :], func=AF.Copy), waits=[tok, M.sgfree[s]])
                pend[(h, tt)] = (s, a)
                return a
            s, a = pend.pop((h, tt))
            csl, ssl = cosh[:, cs], sinh[:, cs]
            s2 = M.sgk % M.NSG
            M.sgk += 1
            d2 = P.DVE(lambda e: e.scalar_tensor_tensor(out=M.sg[s2][0:64, :], in0=bank[0:64, :], scalar=QS, in1=ssl, op0=ALU.mult, op1=ALU.mult),
                       waits=[tok, M.sgfree[s2], tabtok])
            d1 = P.DVE(lambda e: e.scalar_tensor_tensor(out=M.sg[s][0:64, :], in0=M.sg[s][0:64, :], scalar=QS, in1=csl, op0=ALU.mult, op1=ALU.mult),
                       waits=[a, d2])
            i, prev = stg_acquire(M)
            d3 = P.DVE(lambda e: e.tensor_tensor(out=M.stg[i][0:64, :], in0=M.sg[s][0:64, :], in1=M.sg[s2][0:64, :], op=ALU.add), waits=[d1, prev])
            M.sgfree[s] = d3
            M.sgfree[s2] = d3
            P.op("sync", lambda e: e.dma_start(out=out["Qr"][h, :, gs], in_=M.stg[i][0:64, :]), waits=[d3], post=M.sts[i])
            return d2
        stream_linear(P, M, groups, 4, qlnT, T, evac_q, lntok["q"])
        groups = []
        for hg in range(2):
            def lk(raw, hg=hg):
                return [(raw.reshape([128, 4, 2048])[:, :, :], wkv_v[:, :, hg * 2048:(hg + 1) * 2048])]
            groups.append((lk, lambda raw: raw.reshape([128, 4, 2048]), [(k * 256, 128, hg * 8 + k) for k in range(8)]))

        def evac_kn(h, tt, bank, tok):
            gs = slice(t0 + tt * 512, t0 + (tt + 1) * 512)
            i, prev = stg_acquire(M)
            a = P.ACT(lambda e: e.activation(out=M.stg[i][:], in_=bank[:], func=AF.Copy), waits=[tok, prev])
            P.op("sync", lambda e: e.dma_start(out=out["KnT_f"](h)[:, gs], in_=M.stg[i][:]), waits=[a], post=M.sts[i])
            return a
        stream_linear(P, M, groups, 4, kvlnT, T, evac_kn, lntok["kv"])
        lastpe = None
        for hg in range(4):
            h0 = hg * 4
            s, sfree = M.wslot_acquire()
            wv = M.wraw[s].reshape([128, 4, 2048])
            for k in range(4):
                ld = P.op("gpsimd", lambda e, k=k, wv=wv, h0=h0: e.dma_start(out=wv[:, :, k * 128:(k + 1) * 128],
                                                                           in_=wkv_v[:, :, (h0 + k) * 256 + 128:(h0 + k) * 256 + 256]),
                          waits=[sfree], post=M.wchan[s])
            for blk in range(T // 128):
                b, btok = M.bank_acquire()
                for kc in range(4):
                    tok = P.PE(lambda e, b=b, kc=kc, blk=blk, wv=wv: e.matmul(M.bank[b][:], kvlnT[:, kc, blk * 128:(blk + 1) * 128], wv[:, kc, 0:512],
                                                                              start=(kc == 0), stop=(kc == 3)),
                               waits=[ld, btok, lntok["kv"]], post=(kc == 3))
                i, prev = stg_acquire(M)
                a = P.ACT(lambda e, b=b, i=i: e.activation(out=M.stg[i][:], in_=M.bank[b][:], func=AF.Copy), waits=[tok, prev])
                M.bfree[b] = a
                bg = half * 8 + blk
                for jp in range(2):
                    P.op("sync", lambda e, i=i, h0=h0, bg=bg, jp=jp: e.dma_start(out=out["V_f"](h0 + 2 * jp)[:, :, bg, :].rearrange("h p d -> p h d"),
                                                                                 in_=M.stg[i].reshape([128, 4, 128])[:, 2 * jp:2 * jp + 2, :]),
                         waits=[a], post=M.sts[i])
                lastpe = tok
            M.wfree[s] = lastpe
        M.big_free = lastpe
        M.big_free_extra = [(P.dve, P.dve.n), (P.act, P.act.n)]
    M.store_toks = M.store_toks + [(M.sts[i], M.sts[i].n) for i in range(4) if M.sts[i].n] + [(pch, pch.n)]


def attn_setup(P, M):
    nc = P.nc
    if hasattr(M, "stg"):
        M.pT = M.stg[0:4]
    else:
        M.pT = [nc.alloc_sbuf_tensor(f"pT{i}", [128, 512], BF16) for i in range(4)]
    M.pTfree = [None] * 4
    M.pTk = 0


def mla_b_phase(P, M, xin, xout, w_out, Qn, Qr, Kf, KrT_all, Vf, qmask_dram, kmask_dram, keytile):
    w_out_v = w_out.rearrange("(c p) f -> p c f", p=128)
    xin_v = xin.rearrange("(c p) t -> p c t", p=128)
    xout_v = xout.rearrange("(c p) t -> p c t", p=128)
    bigB = M.big.reshape([128, 44, 1024])
    hB = M.hT.reshape([128, 32, 512])
    Qn_g, Qr_g = hB[:, 0:16, :], hB[:, 16:32, :]
    OT = bigB[:, 32:40, :].rearrange("p c (a t) -> p (c a) t", a=2)
    kset = [bigB[:, 16 * s:16 * s + 8, :] for s in range(2)]
    vset = [bigB[:, 16 * s + 8:16 * s + 16, :] for s in range(2)]
    kch = [P.chan(f"kv{s}", dma=True) for s in range(2)]
    kvfree = [None, None]
    qch = P.chan("qld", dma=True)
    cch = P.chan("acst", dma=True)
    M.NW = 2
    M.wk = 0
    kr3 = M.wraw[2].reshape([128, 4, 2048])
    qm = kr3[64:66, 0, :].rearrange("p (m t) -> p m t", m=4)
    kmask = kr3[64:66, 1, 0:128]
    w0 = [M.big_free] + M.store_toks + getattr(M, "big_free_extra", [])
    P.op("sync", lambda e: e.dma_start(out=kr3[0:64, :, :], in_=KrT_all.rearrange("r p t -> p r t")), waits=w0 + [M.wfree[2]], post=cch)
    cch2 = P.chan("acst2", dma=True)
    P.op("gpsimd", lambda e: e.dma_start(out=kr3[64:66, 0, :], in_=qmask_dram[:, :]), waits=[M.wfree[2]], post=cch2)
    P.op("gpsimd", lambda e: e.dma_start(out=kmask, in_=kmask_dram[:, :]), post=cch2)
    ctoks = [(cch, cch.n), (cch2, cch2.n)]
    kvk = 0
    qfree = None
    ot_free = None
    for g in range(4):
        P.op("sync", lambda e, g=g: e.dma_start(out=Qn_g, in_=Qn[:, :, g * 512:(g + 1) * 512].rearrange("h p t -> p h t")), waits=[qfree, M.hT_free] + w0, post=qch)
        P.op("sync", lambda e, g=g: e.dma_start(out=Qr_g[0:64], in_=Qr[:, :, g * 512:(g + 1) * 512].rearrange("h p t -> p h t")), waits=[qfree], post=qch)
        qtok = (qch, qch.n)
        nkb = 16 * g + 16
        for h in range(NH):
            s = kvk % 2
            kvk += 1
            P.op("sync", lambda e, s=s, h=h: e.dma_start(out=kset[s].rearrange("p (r a) t -> p r a t", a=2), in_=Kf(h).rearrange("r p (a t) -> p r a t", a=2)), waits=[kvfree[s]] + w0, post=kch[s])
            P.op("sync", lambda e, s=s, h=h: e.dma_start(out=vset[s].rearrange("p (r a) (b d) -> p r a b d", a=2, d=128), in_=Vf(h).rearrange("r p (a b) d -> p r a b d", a=2)), waits=[kvfree[s]], post=kch[s])
            kvtok = (kch[s], kch[s].n)
            K4 = kset[s].rearrange("p (r a) t -> p r a t", a=2)
            V5 = vset[s].rearrange("p (r a) (b d) -> p r a b d", a=2, d=128)
            pq = []
            lastpv = None

            def emit_pv(item, last, V5=V5, kvtok=kvtok):
                kb, c0, pslot, atok = item
                r, i = keytile(kb)
                t1 = P.PE(lambda e: e.matmul(M.bank[6][:, c0:512], V5[:, r, i // 8, i % 8, :], M.pT[pslot][:, c0:512], start=(kb == 0), stop=last),
                          waits=[atok, M.bfree[6], kvtok])
                t2 = P.PE(lambda e: e.matmul(M.bank[7][:, c0:512], M.ones[:], M.pT[pslot][:, c0:512], start=(kb == 0), stop=last),
                          waits=[M.bfree[7], M.ones_tok], post=True)
                M.pTfree[pslot] = t2
                return t2
            for kb in range(nkb):
                r, i = keytile(kb)
                rel = kb - 16 * g
                j0 = 0 if rel <= 3 else (rel - 3 + 3) // 4
                c0 = j0 * 128
                b, btok = M.bank_acquire()
                kcols = slice((i % 8) * 128, (i % 8) * 128 + 128)
                krcols = slice(i * 128, i * 128 + 128)
                maybe = rel >= 0
                P.PE(lambda e, b=b, c0=c0, r=r, i=i, kcols=kcols, h=h, K4=K4: e.matmul(M.bank[b][:, c0:512], K4[:, r, i // 8, kcols], Qn_g[:, h, c0:512], start=True, stop=False),
                     waits=[btok, kvtok, qtok] + ctoks)
                tok = P.PE(lambda e, b=b, c0=c0, r=r, krcols=krcols, h=h, maybe=maybe: e.matmul(M.bank[b][:, c0:512], kr3[0:64, r, krcols], Qr_g[0:64, h, c0:512],
                                                                                              start=False, stop=(not maybe)), post=True)
                if maybe:
                    jm, m = rel // 4, rel % 4
                    tok = P.PE(lambda e, b=b, jm=jm, m=m: e.matmul(M.bank[b][:, jm * 128:(jm + 1) * 128], kmask, qm[:, m, jm * 128:(jm + 1) * 128],
                                                                  start=False, stop=True), post=True)
                pslot = M.pTk % 4
                M.pTk += 1
                atok = P.ACT(lambda e, b=b, c0=c0, pslot=pslot: e.activation(out=M.pT[pslot][:, c0:512], in_=M.bank[b][:, c0:512], func=AF.Exp),
                             waits=[tok, M.pTfree[pslot]])
                M.bfree[b] = atok
                pq.append((kb, c0, pslot, atok))
                if len(pq) > 2:
                    emit_pv(pq.pop(0), False)
            while pq:
                lastpv = emit_pv(pq.pop(0), len(pq) == 0)
            kvfree[s] = lastpv
            sgi = M.sgk % M.NSG
            M.sgk += 1
            d = P.DVE(lambda e, sgi=sgi: e.reciprocal(M.sg[sgi][:], M.bank[7][:]), waits=[lastpv, M.sgfree[sgi]])
            M.bfree[7] = d
            d = P.DVE(lambda e, sgi=sgi, h=h: e.tensor_tensor(out=OT[:, h, :], in0=M.bank[6][:], in1=M.sg[sgi][:], op=ALU.mult), waits=[d, ot_free])
            M.bfree[6] = d
            M.sgfree[sgi] = d
        qfree = lastpv
        groups = []
        for dg in range(4):
            def lo(raw, dg=dg):
                return [(raw.reshape([128, 16, 512])[:, :, :], w_out_v[:, :, dg * 512:(dg + 1) * 512])]
            groups.append((lo, lambda raw: raw.reshape([128, 16, 512]), [(k * 128, 128, dg * 4 + k) for k in range(4)]))
        pre, evac = resid_evac(P, M, xin_v, xout_v, g * 512, 1.0)
        pe = stream_linear(P, M, groups, 16, OT, 512, evac, d, pre=pre)
        M.big_free = pe
        ot_free = pe
    M.hT_free = qfree
    M.big_free_extra = []
    M.NW = 3
    M.wk = 0
    M.wfree[2] = qfree


def _load_small(P, nc, specs):
    ch = P.chan("small", dma=True)
    ch2 = P.chan("small2", dma=True)
    T = {}
    for name, src, shape, dt, cast in specs:
        t = nc.alloc_sbuf_tensor(name + "_sb", shape, dt)
        T[name] = t
        if cast:
            P.op("gpsimd", lambda e, t=t, src=src: e.dma_start(out=t[:], in_=src), post=ch2)
        else:
            P.op("sync", lambda e, t=t, src=src: e.dma_start(out=t[:], in_=src), post=ch)
    toks = [(c, c.n) for c in (ch, ch2) if c.n]
    return T, toks


def build_program(mode, phases):
    nc = bass.Bass("TRN2", target_bir_lowering=False)
    I = lambda n, s, dt=F32: nc.dram_tensor(n, s, dt, kind="ExternalInput").ap()
    O = lambda n, s, dt=F32: nc.dram_tensor(n, s, dt, kind="ExternalOutput").ap()
    P = Prog(nc)
    M = Mem(P)
    setup_consts(P, M)
    xT = I("xT", [D, NTOK])
    gcd = I("gc", [128, 112])
    small = [("gc", gcd[:, :], [128, 112], F32, False)]
    cur = xT
    if mode == "A":
        xs = O("xT_mid", [D, NTOK])
        if "sgu" in phases:
            sd = {k: I(k, s) for k, s in (("gam", [128, 32]), ("bet", [128, 32]), ("wsT", [128, 1024]), ("ident", [128, 128]), ("bspB", [128, 4096]), ("zeros", [64, 512]))}
            small += [("gam", sd["gam"][:, :], [128, 32], F32, False), ("bet", sd["bet"][:, :], [128, 32], F32, False),
                      ("wsT", sd["wsT"][:, :], [128, 1024], BF16, True), ("ident", sd["ident"][:, :], [128, 128], BF16, True)]
        if "mlaA" in phases:
            md = {k: I(k, s) for k, s in (("gq", [128, 4]), ("gkv", [128, 4]), ("invf", [64, 1]), ("sign", [64, 1]))}
            pos = I("pos", [64, NTOK], I32)
            small += [(k, md[k][:, :], list(md[k].shape), F32, False) for k in md]
        S, stoks = _load_small(P, nc, small)
        gc = S["gc"]
        gtok = stoks[0]
        W = {}
        for name, shape in (("f1_in0", [D, 2 * DFF]), ("f1_out0", [DFF, D]), ("f2_in0", [D, 2 * DFF]), ("f2_out0", [DFF, D]),
                            ("f1_in1", [D, 2 * DFF]), ("f1_out1", [DFF, D]), ("sgu_w_in", [D, 8192]), ("sgu_w_out", [4096, D]),
                            ("mla_w_in", [D, 1088]), ("w_q_up", [512, 3072]), ("w_kv_up", [512, 4096])):
            need = {"f1_in0": "f10", "f1_out0": "f10", "f2_in0": "f20", "f2_out0": "f20", "f1_in1": "f11", "f1_out1": "f11",
                    "sgu_w_in": "sgu", "sgu_w_out": "sgu", "mla_w_in": "mlaA", "w_q_up": "mlaA", "w_kv_up": "mlaA"}[name]
            if need in phases:
                W[name] = I(name, shape)
        if "f10" in phases:
            ffn_phase(P, M, cur, xs, gc[:, 0:16], gtok, W["f1_in0"], W["f1_out0"])
            cur = xs
        if "sgu" in phases:
            w3 = S["wsT"].reshape([128, 8, 128])
            zch = P.chan("zmask", dma=True)
            t = P.op("gpsimd", lambda e: e.dma_start(out=w3[64:128, :, 0:64], in_=sd["zeros"].rearrange("p (g f) -> p g f", g=8)), waits=stoks, post=zch)
            C = {"gam": S["gam"], "bet": S["bet"], "wsT": w3, "ident": S["ident"], "bspB": sd["bspB"], "ctoks": stoks + [t],
                 "vtok": [nc.alloc_sbuf_tensor(f"vtok{i}", [128, 512], BF16) for i in range(2)]}
            sgu_phase(P, M, cur, xs, gc[:, 32:48], gtok, W["sgu_w_in"], W["sgu_w_out"], C)
            cur = xs
        if "f20" in phases:
            ffn_phase(P, M, cur, xs, gc[:, 64:80], gtok, W["f2_in0"], W["f2_out0"])
            cur = xs
        if "f11" in phases:
            ffn_phase(P, M, cur, xs, gc[:, 16:32], gtok, W["f1_in1"], W["f1_out1"])
            cur = xs
        if "mlaA" in phases:
            stg_setup(P, M)
            out = {"Qn": O("Qn", [NH, 128, NTOK], BF16), "Qr": O("Qr", [NH, 64, NTOK], BF16), "KnT": O("KnT", [NH, 128, NTOK], BF16),
                   "KrT": O("KrT", [64, NTOK], BF16), "V": O("V", [NH, 128, 16, 128], BF16)}
            out["KnT_f"] = lambda h: out["KnT"][h]
            out["V_f"] = lambda h: out["V"][h:h + 2]
            C = {"gq": S["gq"], "gkv": S["gkv"], "invf": S["invf"], "sign": S["sign"], "ctoks": stoks, "pos": pos}
            mla_a_phase(P, M, cur, gc[:, 48:64], gtok, W["mla_w_in"], W["w_q_up"], W["w_kv_up"], C, out)
        if cur is xT:
            cp = P.chan("cp", dma=True)
            P.op("sync", lambda e: e.dma_start(out=xs[:, :], in_=xT[:, :]), post=cp)
            M.store_toks = M.store_toks + [(cp, cp.n)]
        P.op("sync", lambda e: e.nop(), waits=M.store_toks)
    else:
        oT = O("oT", [D, NTOK])
        xs = nc.dram_tensor("xs_scratch", [D, NTOK], F32).ap()
        S, stoks = _load_small(P, nc, small)
        gc = S["gc"]
        gtok = stoks[0]
        if "mlaB" in phases:
            attn_setup(P, M)
            Qn, Qr = I("Qn", [NH, 128, NTOK], BF16), I("Qr", [NH, 64, NTOK], BF16)
            KnT_all, KrT_all = I("KnT_all", [4, NH, 128, NTOK], BF16), I("KrT_all", [4, 64, NTOK], BF16)
            V_all = I("V_all", [4, NH, 128, 16, 128], BF16)
            qmask, kmask = I("qmask", [2, 2048]), I("kmask", [2, 128])
            w_out = I("mla_w_out", [D, D])
            mla_b_phase(P, M, cur, xs, w_out, Qn, Qr, lambda h: KnT_all[:, h], KrT_all, lambda h: V_all[:, h], qmask, kmask, lambda kb: (kb // 16, kb % 16))
            cur = xs
        if "f21" in phases:
            w1, w2 = I("f2_in1", [D, 2 * DFF]), I("f2_out1", [DFF, D])
            ffn_phase(P, M, cur, xs, gc[:, 80:96], gtok, w1, w2)
            cur = xs
        final_norm_phase(P, M, cur, oT, gc[:, 96:112], gtok)
    P.emit()
    return nc


def _keytile_rankmajor(kb):
    c8, j2 = kb % 8, kb // 8
    return (c8, 2 * j2) if c8 < 4 else (7 - c8, 2 * j2 + 1)


def build_fused():
    nc = bass.Bass("TRN2", target_bir_lowering=False)
    I = lambda n, s, dt=F32: nc.dram_tensor(n, s, dt, kind="ExternalInput").ap()
    P = Prog(nc)
    M = Mem(P)
    setup_consts(P, M)
    xT = I("xT", [D, NTOK])
    oT = nc.dram_tensor("oT", [D, NTOK], F32, kind="ExternalOutput").ap()
    xs = nc.dram_tensor("xs_scratch", [D, NTOK], F32).ap()
    gcd = I("gc", [128, 112])
    sd = {k: I(k, s) for k, s in (("gam", [128, 32]), ("bet", [128, 32]), ("wsT", [128, 1024]), ("ident", [128, 128]), ("bspB", [128, 4096]), ("zeros", [64, 512]))}
    md = {k: I(k, s) for k, s in (("gq", [128, 4]), ("gkv", [128, 4]), ("invf", [64, 1]), ("sign", [64, 1]))}
    pos = I("pos", [64, NTOK], I32)
    qmask, kmask = I("qmask", [2, 2048]), I("kmask", [2, 128])
    small = [("gc", gcd[:, :], [128, 112], F32, False),
             ("gam", sd["gam"][:, :], [128, 32], F32, False), ("bet", sd["bet"][:, :], [128, 32], F32, False),
             ("wsT", sd["wsT"][:, :], [128, 1024], BF16, True), ("ident", sd["ident"][:, :], [128, 128], BF16, True)]
    small += [(k, md[k][:, :], list(md[k].shape), F32, False) for k in md]
    S, stoks = _load_small(P, nc, small)
    gc = S["gc"]
    gtok = stoks[0]
    W = {name: I(name, shape) for name, shape in (
        ("f1_in0", [D, 2 * DFF]), ("f1_out0", [DFF, D]), ("f2_in0", [D, 2 * DFF]), ("f2_out0", [DFF, D]),
        ("f1_in1", [D, 2 * DFF]), ("f1_out1", [DFF, D]), ("f2_in1", [D, 2 * DFF]), ("f2_out1", [DFF, D]),
        ("sgu_w_in", [D, 8192]), ("sgu_w_out", [4096, D]),
        ("mla_w_in", [D, 1088]), ("w_q_up", [512, 3072]), ("w_kv_up", [512, 4096]), ("mla_w_out", [D, D]))}
    ffn_phase(P, M, xT, xs, gc[:, 0:16], gtok, W["f1_in0"], W["f1_out0"])
    w3 = S["wsT"].reshape([128, 8, 128])
    zch = P.chan("zmask", dma=True)
    t = P.op("gpsimd", lambda e: e.dma_start(out=w3[64:128, :, 0:64], in_=sd["zeros"].rearrange("p (g f) -> p g f", g=8)), waits=stoks, post=zch)
    C = {"gam": S["gam"], "bet": S["bet"], "wsT": w3, "ident": S["ident"], "bspB": sd["bspB"], "ctoks": stoks + [t],
         "vtok": [nc.alloc_sbuf_tensor(f"vtok{i}", [128, 512], BF16) for i in range(2)]}
    sgu_phase(P, M, xs, xs, gc[:, 32:48], gtok, W["sgu_w_in"], W["sgu_w_out"], C)
    ffn_phase(P, M, xs, xs, gc[:, 64:80], gtok, W["f2_in0"], W["f2_out0"])
    ffn_phase(P, M, xs, xs, gc[:, 16:32], gtok, W["f1_in1"], W["f1_out1"])
    stg_setup(P, M)
    Qn = nc.dram_tensor("Qn_s", [NH, 128, NTOK], BF16).ap()
    Qr = nc.dram_tensor("Qr_s", [NH, 64, NTOK], BF16).ap()
    kn_own = [nc.dram_tensor(f"kn_own{k}", [256, NTOK], BF16).ap() for k in range(8)]
    v_own = [nc.dram_tensor(f"v_own{k}", [256, NTOK], BF16).ap() for k in range(8)]
    kr_own = nc.dram_tensor("kr_own", [64, NTOK], BF16).ap()
    kn_all = [nc.dram_tensor(f"kn_all{k}", [4 * 256, NTOK], BF16).ap() for k in range(8)]
    v_all = [nc.dram_tensor(f"v_all{k}", [4 * 256, NTOK], BF16).ap() for k in range(8)]
    kr_all = nc.dram_tensor("kr_all", [4 * 64, NTOK], BF16).ap()
    out = {"Qn": Qn, "Qr": Qr, "KrT": kr_own,
           "KnT_f": lambda h: kn_own[h // 2][(h % 2) * 128:(h % 2) * 128 + 128, :],
           "V_f": lambda h: v_own[h // 2].rearrange("(h p) (b d) -> h p b d", p=128, d=128)}
    CA = {"gq": S["gq"], "gkv": S["gkv"], "invf": S["invf"], "sign": S["sign"], "ctoks": stoks, "pos": pos}
    mla_a_phase(P, M, xs, gc[:, 48:64], gtok, W["mla_w_in"], W["w_q_up"], W["w_kv_up"], CA, out)
    cc = P.chan("cc")
    RG = [[0, 1, 2, 3], [4, 5, 6, 7]]
    for src, dst in [(kr_own, kr_all)] + list(zip(kn_own, kn_all)) + list(zip(v_own, v_all)):
        cctok = P.op("gpsimd", lambda e, src=src, dst=dst: e.collective_compute("AllGather", op=ALU.bypass, replica_groups=RG,
                                                                               ins=[src.opt()], outs=[dst.opt()]),
                     waits=M.store_toks, post=cc)
    M.store_toks = [cctok]
    Kf = lambda h: kn_all[h // 2].rearrange("(r q) t -> r q t", r=4)[:, (h % 2) * 128:(h % 2) * 128 + 128, :]
    Vf = lambda h: v_all[h // 2].rearrange("(r q) (b d) -> r q b d", r=4, d=128)[:, (h % 2) * 128:(h % 2) * 128 + 128]
    KrT_all = kr_all.rearrange("(r p) t -> r p t", r=4)
    attn_setup(P, M)
    mla_b_phase(P, M, xs, xs, W["mla_w_out"], Qn, Qr, Kf, KrT_all, Vf, qmask, kmask, _keytile_rankmajor)
    ffn_phase(P, M, xs, xs, gc[:, 80:96], gtok, W["f2_in1"], W["f2_out1"])
    final_norm_phase(P, M, xs, oT, gc[:, 96:112], gtok)
    P.emit()
    return nc


from concourse.bass_utils import run_bass_kernel_spmd
import ml_dtypes

_BF = ml_dtypes.bfloat16
SEQ = 8192
_FUSED = False


def _gblock(r, i):
    return 8 * (i // 2) + (r if i % 2 == 0 else 7 - r)


def _tok_index(r):
    return np.concatenate([np.arange(_gblock(r, i) * 128, _gblock(r, i) * 128 + 128) for i in range(16)])


def _col(v, n):
    return np.ascontiguousarray(np.asarray(v, np.float32).reshape(n, 128).T)


def _qmask(r):
    qm = np.zeros((2, 4, 4, 128), np.float32)
    for j in range(4):
        delta = r if j % 2 == 0 else 3 - r
        for m in range(4):
            if m > delta:
                qm[0, m, j, :] = NEG
            elif m == delta:
                qm[1, m, j, :64] = NEG
    return qm.reshape(2, 2048)


def kernel(x, positions, ln_ffn1, ffn1_w_in, ffn1_w_out, ln_mix, ln_ffn2, ffn2_w_in, ffn2_w_out,
           sgu_w_in, sgu_v_gain, sgu_v_bias, sgu_w_spatial, sgu_b_spatial, sgu_w_out,
           mla_w_in, mla_q_norm, mla_w_q_up, mla_kv_norm, mla_w_kv_up, mla_w_out, ln_final):
    f32 = lambda a: np.ascontiguousarray(np.asarray(a, np.float32))
    x = np.asarray(x, np.float32)
    positions = np.asarray(positions, np.int32)
    gc = np.concatenate([_col(ln_ffn1[0], 16), _col(ln_ffn1[1], 16), _col(ln_mix[0], 16), _col(ln_mix[1], 16),
                         _col(ln_ffn2[0], 16), _col(ln_ffn2[1], 16), _col(ln_final, 16)], axis=1)
    invf = (1.0 / (10000.0 ** (np.arange(32, dtype=np.float32) / 32))).astype(np.float32)
    common = {
        "gc": gc, "gam": _col(sgu_v_gain[0], 32), "bet": _col(sgu_v_bias[0], 32),
        "wsT": np.ascontiguousarray(np.asarray(sgu_w_spatial[0], np.float32).transpose(2, 0, 1).reshape(128, 1024)),
        "ident": np.eye(128, dtype=np.float32),
        "bspB": np.ascontiguousarray(np.broadcast_to(np.tile(np.asarray(sgu_b_spatial[0], np.float32), (1, 4)).reshape(1, 4096), (128, 4096))),
        "zeros": np.zeros((64, 512), np.float32),
        "gq": _col(mla_q_norm[0], 4), "gkv": _col(mla_kv_norm[0], 4),
        "invf": np.concatenate([invf, invf])[:, None].copy(),
        "sign": np.concatenate([-np.ones(32), np.ones(32)]).astype(np.float32)[:, None].copy(),
        "f1_in0": f32(ffn1_w_in[0]), "f1_out0": f32(ffn1_w_out[0]), "f2_in0": f32(ffn2_w_in[0]), "f2_out0": f32(ffn2_w_out[0]),
        "f1_in1": f32(ffn1_w_in[1]), "f1_out1": f32(ffn1_w_out[1]),
        "sgu_w_in": f32(sgu_w_in[0]), "sgu_w_out": f32(sgu_w_out[0]),
        "mla_w_in": f32(mla_w_in[0]), "w_q_up": f32(mla_w_q_up[0]), "w_kv_up": f32(mla_w_kv_up[0]),
    }
    idxs = [_tok_index(r) for r in range(4)]
    kmask = np.stack([np.ones(128), np.concatenate([np.zeros(64), np.ones(64)])]).astype(np.float32)
    common.update({"kmask": kmask, "mla_w_out": f32(mla_w_out[0]), "f2_in1": f32(ffn2_w_in[1]), "f2_out1": f32(ffn2_w_out[1])})
    maps = []
    for c in range(8):
        b, r = c // 4, c % 4
        m = dict(common)
        m["xT"] = np.ascontiguousarray(x[b, idxs[r]].T)
        m["pos"] = np.ascontiguousarray(np.broadcast_to(positions[b, idxs[r]][None], (64, 2048))).astype(np.int32)
        m["qmask"] = _qmask(r)
        maps.append(m)
    nc = build_fused()
    R = run_bass_kernel_spmd(nc, maps, core_ids=list(range(8))).results
    out = np.empty((2, SEQ, D), np.float32)
    for c in range(8):
        b, r = c // 4, c % 4
        out[b, idxs[r]] = R[c]["oT"].T
    return out
```
